# Optimizing a Trainium2 kernel written in Bass

```python
import math
import jax, jax.numpy as jnp
from jax import lax
import numpy as np

D_MODEL = 1024
BATCH = 4
SEQ = 8192
DEPTH = 1

CHUNK = 64
N_META = 16
SSM_EXPAND = 2
D_INNER = SSM_EXPAND * D_MODEL
SSM_HEAD_DIM = 64
SSM_HEADS = D_INNER // SSM_HEAD_DIM
SSM_GROUPS = 4
SSM_HEADS_PER_GROUP = SSM_HEADS // SSM_GROUPS
D_STATE = 128
SSM_CONV = 4
D_XBC = D_INNER + 2 * SSM_GROUPS * D_STATE
DT_MIN = 1e-3
DT_MAX = 1e-1
D_POOL = D_MODEL
POOL_WINDOWS = (2, 4, 8, 16)
N_POOL_GROUPS = len(POOL_WINDOWS)
POOL_GROUP_DIM = D_POOL // N_POOL_GROUPS
D_FF = 2816
FFN_CONV = 3
DEEPNORM_ALPHA = (2.0 * DEPTH) ** 0.25
DEEPNORM_BETA = (8.0 * DEPTH) ** -0.25
LN_EPS = 1e-5
RMS_EPS = 1e-5
D_IN_PROJ = D_INNER + D_XBC + SSM_HEADS + D_POOL + 2 * D_MODEL
IN_SPLITS = (D_INNER,
             D_INNER + D_XBC,
             D_INNER + D_XBC + SSM_HEADS,
             D_INNER + D_XBC + SSM_HEADS + D_POOL,
             D_INNER + D_XBC + SSM_HEADS + D_POOL + D_MODEL)

kernel_name = "hybrid_ssd_pool_gated_deepnorm"


def layer_norm(x, g, b):
    xf = x.astype(jnp.float32)
    mu = jnp.mean(xf, axis=-1, keepdims=True)
    var = jnp.mean(jnp.square(xf - mu), axis=-1, keepdims=True)
    return ((xf - mu) * lax.rsqrt(var + LN_EPS)).astype(x.dtype) * g + b


def causal_dwconv(x, w, b):
    k, c = w.shape
    y = lax.conv_general_dilated(x, w[:, None, :].astype(x.dtype), window_strides=(1,),
                                 padding=[(k - 1, 0)],
                                 dimension_numbers=("NWC", "WIO", "NWC"),
                                 feature_group_count=c)
    return y + b


def ssd_scan(xh, dt, a, bm, cm):
    f32 = jnp.float32
    bsz, L = xh.shape[0], xh.shape[1]
    pad = (-N_META) % CHUNK
    nc = (L + pad) // CHUNK
    G, J, P, N = SSM_GROUPS, SSM_HEADS_PER_GROUP, SSM_HEAD_DIM, D_STATE

    def lpad(t):
        return jnp.pad(t.astype(f32), [(0, 0), (pad, 0)] + [(0, 0)] * (t.ndim - 2))

    xdt = lpad(xh.astype(f32) * dt[..., None]).reshape(bsz, nc, CHUNK, G, J, P)
    dta = lpad(dt * a).reshape(bsz, nc, CHUNK, G, J)
    bm = lpad(bm).reshape(bsz, nc, CHUNK, G, N)
    cm = lpad(cm).reshape(bsz, nc, CHUNK, G, N)

    a_cum = jnp.cumsum(dta, axis=2)
    causal = jnp.tril(jnp.ones((CHUNK, CHUNK), dtype=bool))[:, :, None, None]
    seg = a_cum[:, :, :, None] - a_cum[:, :, None, :]
    decay = jnp.exp(jnp.where(causal, seg, -jnp.inf))
    cb = jnp.einsum("bclgn,bcsgn->bclsg", cm, bm)
    y_diag = jnp.einsum("bclsg,bclsgj,bcsgjp->bclgjp", cb, decay, xdt)

    a_last = a_cum[:, :, -1]
    state_decay = jnp.exp(a_last[:, :, None] - a_cum)

    def step(h, inp):
        c_k, b_k, x_k, acum_k, sdec_k, alast_k = inp
        y_off = jnp.einsum("blgn,bgjpn->blgjp", c_k, h) * jnp.exp(acum_k)[..., None]
        h = h * jnp.exp(alast_k)[..., None, None] + jnp.einsum("blgn,blgj,blgjp->bgjpn", b_k, sdec_k, x_k)
        return h, y_off

    h0 = jnp.zeros((bsz, G, J, P, N), f32)
    xs = tuple(jnp.moveaxis(t, 1, 0) for t in (cm, bm, xdt, a_cum, state_decay, a_last))
    _, y_off = lax.scan(step, h0, xs)
    y = y_diag + jnp.moveaxis(y_off, 0, 1)
    return y.reshape(bsz, nc * CHUNK, SSM_HEADS, P)[:, pad:]


def mamba2_branch(z, xbc, dt_raw, conv_w, conv_b, dt_bias, a_log, d_skip, norm_w):
    f32 = jnp.float32
    bsz, L, _ = z.shape
    xbc = jax.nn.silu(causal_dwconv(xbc, conv_w, conv_b))
    xs, bm, cm = jnp.split(xbc, [D_INNER, D_INNER + SSM_GROUPS * D_STATE], axis=-1)
    xs = xs.reshape(bsz, L, SSM_HEADS, SSM_HEAD_DIM)
    bm = bm.reshape(bsz, L, SSM_GROUPS, D_STATE)
    cm = cm.reshape(bsz, L, SSM_GROUPS, D_STATE)
    dt = jax.nn.softplus(dt_raw.astype(f32) + dt_bias.astype(f32))
    a = -jnp.exp(a_log.astype(f32))
    y = ssd_scan(xs, dt, a, bm, cm) + xs.astype(f32) * d_skip.astype(f32)[:, None]
    y = y.reshape(bsz, L, D_INNER) * jax.nn.silu(z.astype(f32))
    yg = y.reshape(bsz, L, SSM_GROUPS, D_INNER // SSM_GROUPS)
    yg = yg * lax.rsqrt(jnp.mean(yg * yg, axis=-1, keepdims=True) + RMS_EPS)
    return (yg.reshape(bsz, L, D_INNER) * norm_w.astype(f32)).astype(z.dtype)


def pool_branch(u, w_grp, scale):
    f32 = jnp.float32
    bsz, L, _ = u.shape
    uf = u.astype(f32)
    cs = jnp.pad(jnp.cumsum(uf, axis=1), ((0, 0), (1, 0), (0, 0)))
    outs = []
    for gi, w in enumerate(POOL_WINDOWS):
        sl = slice(gi * POOL_GROUP_DIM, (gi + 1) * POOL_GROUP_DIM)
        cs_g = cs[..., sl]
        lag = jnp.pad(cs_g[:, :L + 1 - w], ((0, 0), (w - 1, 0), (0, 0)))
        cnt = jnp.minimum(jnp.arange(1, L + 1), w).astype(f32)
        mean = (cs_g[:, 1:] - lag) / cnt[None, :, None]
        outs.append(mean - uf[..., sl])
    p = jnp.stack(outs, axis=2)
    y = jnp.einsum("blgc,gcd->blgd", p, w_grp.astype(f32)).reshape(bsz, L, D_POOL)
    return (y * scale.astype(f32)).astype(u.dtype)


def conv_glu_ffn(x, w_up, conv_w, conv_b, w_down):
    a, v = jnp.split(x @ w_up, 2, axis=-1)
    a = causal_dwconv(a, conv_w, conv_b)
    return (jax.nn.silu(a) * v) @ w_down


def setup_inputs(seed: int = 0) -> dict:
    key = jax.random.key(seed)
    ks = jax.random.split(key, 26)
    f32 = jnp.float32
    nrm = lambda k, shape, s: jax.random.normal(k, shape, f32) * s
    beta = DEEPNORM_BETA
    dt0 = jnp.exp(jax.random.uniform(ks[7], (DEPTH, SSM_HEADS), f32, math.log(DT_MIN), math.log(DT_MAX)))
    return {
        "x": nrm(ks[0], (BATCH, SEQ, D_MODEL), 1.0),
        "meta_tokens": nrm(ks[1], (N_META, D_MODEL), 1.0),
        "ln_in_g": 1.0 + nrm(ks[2], (D_MODEL,), 0.02),
        "ln_in_b": nrm(ks[3], (D_MODEL,), 0.02),
        "w_in": nrm(ks[4], (DEPTH, D_MODEL, D_IN_PROJ), D_MODEL ** -0.5),
        "ssm_conv_w": nrm(ks[5], (DEPTH, SSM_CONV, D_XBC), SSM_CONV ** -0.5),
        "ssm_conv_b": nrm(ks[6], (DEPTH, D_XBC), 0.02),
        "ssm_dt_bias": dt0 + jnp.log(-jnp.expm1(-dt0)),
        "ssm_a_log": jnp.log(jax.random.uniform(ks[8], (DEPTH, SSM_HEADS), f32, 1.0, 16.0)),
        "ssm_d": 1.0 + nrm(ks[9], (DEPTH, SSM_HEADS), 0.1),
        "ssm_norm_w": 1.0 + nrm(ks[10], (DEPTH, D_INNER), 0.02),
        "pool_w": nrm(ks[11], (DEPTH, N_POOL_GROUPS, POOL_GROUP_DIM, POOL_GROUP_DIM), POOL_GROUP_DIM ** -0.5),
        "pool_scale": 1.0 + nrm(ks[12], (DEPTH, D_POOL), 0.02),
        "w_proj_ssm": nrm(ks[13], (DEPTH, D_INNER, D_MODEL), beta * D_INNER ** -0.5),
        "w_proj_pool": nrm(ks[14], (DEPTH, D_POOL, D_MODEL), beta * D_POOL ** -0.5),
        "w_out": nrm(ks[15], (DEPTH, D_MODEL, D_MODEL), beta * D_MODEL ** -0.5),
        "ln1_g": 1.0 + nrm(ks[16], (DEPTH, D_MODEL), 0.02),
        "ln1_b": nrm(ks[17], (DEPTH, D_MODEL), 0.02),
        "ffn_w_up": nrm(ks[18], (DEPTH, D_MODEL, 2 * D_FF), beta * D_MODEL ** -0.5),
        "ffn_conv_w": nrm(ks[19], (DEPTH, FFN_CONV, D_FF), FFN_CONV ** -0.5),
        "ffn_conv_b": nrm(ks[20], (DEPTH, D_FF), 0.02),
        "ffn_w_down": nrm(ks[21], (DEPTH, D_FF, D_MODEL), beta * D_FF ** -0.5),
        "ln2_g": 1.0 + nrm(ks[22], (DEPTH, D_MODEL), 0.02),
        "ln2_b": nrm(ks[23], (DEPTH, D_MODEL), 0.02),
    }


def reference(x, meta_tokens, ln_in_g, ln_in_b, w_in, ssm_conv_w, ssm_conv_b, ssm_dt_bias,
              ssm_a_log, ssm_d, ssm_norm_w, pool_w, pool_scale, w_proj_ssm, w_proj_pool, w_out,
              ln1_g, ln1_b, ffn_w_up, ffn_conv_w, ffn_conv_b, ffn_w_down, ln2_g, ln2_b):
    bsz = x.shape[0]
    meta = jnp.broadcast_to(meta_tokens[None].astype(x.dtype), (bsz, N_META, D_MODEL))
    h = layer_norm(jnp.concatenate([meta, x], axis=1), ln_in_g, ln_in_b)
    for i in range(DEPTH):
        proj = h @ w_in[i]
        z, xbc, dt_raw, u, g_ssm, g_pool = jnp.split(proj, IN_SPLITS, axis=-1)
        y_ssm = mamba2_branch(z, xbc, dt_raw, ssm_conv_w[i], ssm_conv_b[i], ssm_dt_bias[i],
                              ssm_a_log[i], ssm_d[i], ssm_norm_w[i])
        y_pool = pool_branch(u, pool_w[i], pool_scale[i])
        merged = (jax.nn.sigmoid(g_ssm) * (y_ssm @ w_proj_ssm[i])
                  + jax.nn.sigmoid(g_pool) * (y_pool @ w_proj_pool[i]))
        h = layer_norm(DEEPNORM_ALPHA * h + merged @ w_out[i], ln1_g[i], ln1_b[i])
        f = conv_glu_ffn(h, ffn_w_up[i], ffn_conv_w[i], ffn_conv_b[i], ffn_w_down[i])
        h = layer_norm(DEEPNORM_ALPHA * h + f, ln2_g[i], ln2_b[i])
    return h[:, N_META:]
```

```python
import numpy as np
from contextlib import ExitStack
import concourse.bass as bass
import concourse.mybir as mybir
from concourse.bass_utils import run_bass_kernel_spmd

F32 = mybir.dt.float32
BF16 = mybir.dt.bfloat16
ALU = mybir.AluOpType
AF = mybir.ActivationFunctionType

D = 1024
DI = 2048
NH = 32
W = 384
NCH = 3
ALPHA = 2.0 ** 0.25
LN_EPS = 1e-5
RMS_EPS = 1e-5
POOL_WINDOWS = (2, 4, 8, 16)
DFF = 2816
SLOTW = 5632
NSLOT = 3


class Tk:
    __slots__ = ("w", "rs")

    def __init__(self):
        self.w = None
        self.rs = []


def alias_tokens(new, old):
    deps = []
    for t in old:
        if t.w is not None:
            deps.append(t.w)
        deps.extend(t.rs)
    for n in new:
        n.rs = list(n.rs) + deps


class Sync:
    ENG = ("pe", "act", "dve", "pool", "sp")

    def __init__(self):
        self.prog = {e: [] for e in self.ENG}
        self.cnt = {}
        self.seen = {e: {} for e in self.ENG}
        self.sems = {}

    def add_sem(self, key, handle):
        self.sems[key] = handle
        self.cnt[key] = 0

    def _deps(self, eng, reads, writes, pe_chain=False):
        deps = {}

        def add(d):
            if d is None:
                return
            k, v = d
            if deps.get(k, 0) < v:
                deps[k] = v
        for t in reads:
            add(t.w)
        for t in writes:
            add(t.w)
            for r in t.rs:
                add(r)
        out = []
        for k, v in deps.items():
            if k == "pe" and eng == "pe":
                continue
            if self.seen[eng].get(k, 0) < v:
                self.seen[eng][k] = v
                out.append((k, v))
        return out

    def op(self, eng, fn, reads=(), writes=()):
        waits = self._deps(eng, reads, writes)
        self.cnt[eng] += 1
        me = (eng, self.cnt[eng])
        self.prog[eng].append((waits, fn, (eng, 1)))
        for t in reads:
            t.rs.append(me)
            if len(t.rs) > 64:
                t.rs = _compact(t.rs)
        for t in writes:
            t.w = me
            t.rs = []

    def dma(self, eng, fn, semkey, reads=(), writes=()):
        waits = self._deps(eng, reads, writes)
        self.cnt[semkey] += 16
        me = (semkey, self.cnt[semkey])
        self.prog[eng].append((waits, fn, (semkey, 16)))
        for t in reads:
            t.rs.append(me)
        for t in writes:
            t.w = me
            t.rs = []

    def final_wait(self, eng, toks):
        waits = self._deps(eng, (), toks)
        self.prog[eng].append((waits, None, None))

    def emit(self, block):
        sems = self.sems

        def run(engname):
            def body(e):
                for waits, fn, inc in self.prog[engname]:
                    for k, v in waits:
                        e.wait_ge(sems[k], v)
                    if fn is not None:
                        fn(e).then_inc(sems[inc[0]], inc[1])
            return body
        block.tensor(run("pe"))
        block.scalar(run("act"))
        block.vector(run("dve"))
        block.gpsimd(run("pool"))
        block.sync(run("sp"))


def _compact(rs):
    best = {}
    for k, v in rs:
        if best.get(k, 0) < v:
            best[k] = v
    return list(best.items())


C_Z, C_X, C_B, C_C, C_DT, C_U, C_GS, C_GP = 0, 2048, 4096, 4608, 5120, 5152, 6176, 7200

BLK_NAMES = (["U0", "U1", "PW", "PP0", "PP1", "GP0", "GP1", "DT", "BC0", "BC1"]
             + [f"X{i}" for i in range(4)] + [f"Z{i}" for i in range(4)]
             + [f"PS{i}" for i in range(4)] + ["GS0", "GS1", "WO0", "WO1"]
             + [f"F{i}" for i in range(11)] + [f"FD{i}" for i in range(4)])
BLK_ID = {n: i for i, n in enumerate(BLK_NAMES)}
NBLK = len(BLK_NAMES)

MAIN_SEQ = (["U0", "U1", "PW", "PP0", "GP0", "PP1", "GP1", "DT", "BC0", "BC1", "X0", "X1", "X2", "X3",
             "Z0", "Z1", "Z2", "Z3", "PS0", "GS0", "PS1", "PS2", "GS1", "PS3", "WO0", "WO1"]
            + [f"F{i}" for i in range(11)] + [f"FD{i}" for i in range(4)])
PRE_SEQ = ["DT", "BC0", "X0", "X1", "X2", "X3"]


def build_program(NT_PRE, NT_MAIN, debug=False):
    TP = NT_PRE * W
    TM = NT_MAIN * W
    NOUT = (NT_MAIN * NCH - 1) * 128
    nc = bass.Bass("TRN2", target_bir_lowering=False)

    def din(name, shape, dt=F32):
        return nc.dram_tensor(name, list(shape), dt, kind="ExternalInput").ap()
    xpre_d = din("xpre", [TP, D])
    xmain_d = din("xmain", [TM, D])
    mpre_d = din("mpre", [96, TP])
    mmain_d = din("mmain", [96, TM])
    tm0_d = din("tokmask0", [128, 16 + W])
    ic0_d = din("invcnt0", [128, 4 * W])
    tmp_d = din("tokmaskp", [128, 16 + W])
    pp_d = din("pp", [128, 256])
    lnrow_d = din("lnrows", [3, 2 * D])
    w_in_d = din("w_in", [D, 8224])
    w_ps_d = din("w_proj_ssm", [DI, D])
    w_pp_d = din("w_proj_pool", [D, D])
    w_pw_d = din("pool_w", [4, 256, 256])
    w_o_d = din("w_out", [D, D])
    w_up_d = din("ffn_w_up", [D, 2 * DFF])
    w_dn_d = din("ffn_w_down", [DFF, D])
    y_d = nc.dram_tensor("y", [NOUT, D], F32, kind="ExternalOutput").ap()
    wsc = nc.dram_tensor("wsc", [NBLK, 128, SLOTW], BF16, kind="Internal").ap()
    if debug:
        dbg_yn = nc.dram_tensor("dbg_yn", [128, 16 * W], BF16, kind="ExternalOutput").ap()
        dbg_yp = nc.dram_tensor("dbg_yp", [128, 8 * W], BF16, kind="ExternalOutput").ap()
        dbg_pp = nc.dram_tensor("dbg_pp", [128, 8 * W], F32, kind="ExternalOutput").ap()
        dbg_mg = nc.dram_tensor("dbg_mg", [128, 8 * W], BF16, kind="ExternalOutput").ap()
        dbg_h1 = nc.dram_tensor("dbg_h1", [128, 3 * D], F32, kind="ExternalOutput").ap()
        dbg_h0 = nc.dram_tensor("dbg_h0", [128, 3 * D], F32, kind="ExternalOutput").ap()
        dbg_act = nc.dram_tensor("dbg_act", [128, 22 * W], BF16, kind="ExternalOutput").ap()
        dbg_xs = nc.dram_tensor("dbg_xs", [128, 16 * W], F32, kind="ExternalOutput").ap()
        dbg_bc = nc.dram_tensor("dbg_bc", [128, 8 * W], BF16, kind="ExternalOutput").ap()

    es = ExitStack()
    with es:
        ARENA_WORDS = 53100
        arena = es.enter_context(nc.sbuf_tensor("arena", [128, ARENA_WORDS], F32))
        psb = [es.enter_context(nc.psum_tensor(f"psb{i}", [128, 512], F32)) for i in range(8)]
        tk_ps = [Tk() for _ in range(8)]
        S = Sync()
        for e in Sync.ENG:
            S.add_sem(e, es.enter_context(nc.semaphore("s_" + e)))
        DMA_KEYS = (["wl0", "wl1", "wl2", "cvi0", "cvi1", "cvo0", "cvo1", "cvo2", "xin0", "xin1", "xin2",
                     "gb", "msk", "out0", "out1", "out2", "setup", "dbg"])
        for k in DMA_KEYS:
            S.add_sem(k, es.enter_context(nc.semaphore("s_" + k)))
        block = es.enter_context(nc.Block())

        off = [0]

        def alloc(nwords):
            o = off[0]
            off[0] += nwords
            assert off[0] <= ARENA_WORDS, off[0]
            return o

        def f32v(o, n):
            return arena[:, o:o + n]

        def bfv(o, nbf):
            return arena[:, o:o + nbf // 2].bitcast(BF16)

        o_hres = alloc(3 * D)
        hres = f32v(o_hres, 3 * D).rearrange("p (j d) -> p j d", j=3)
        tk_hres = [Tk() for _ in range(3)]
        o_gb = alloc(2 * D)
        gbt = f32v(o_gb, 2 * D)
        tk_gb = Tk()
        hTs = [bfv(alloc(8 * W // 2), 8 * W).rearrange("p (k t) -> p k t", k=8) for _ in range(2)]
        tk_hTs = [[Tk() for _ in range(3)] for _ in range(2)]

        class Cur:
            pass
        cur = Cur()
        cur.v, cur.tk = hTs[0], tk_hTs[0]
        o_RA = alloc(16 * W)
        xs = f32v(o_RA, 16 * W).rearrange("p (i t) -> p i t", i=16)
        actT = bfv(o_RA, 22 * W).rearrange("p (i t) -> p i t", i=22)
        tk_xs = [Tk() for _ in range(16)]
        tk_act = [Tk() for _ in range(22)]
        o_RB = alloc(8 * W)
        pT = bfv(o_RB, 8 * W).rearrange("p (i t) -> p i t", i=8)
        ypT = bfv(o_RB + 4 * W, 8 * W).rearrange("p (i t) -> p i t", i=8)
        ynT = bfv(o_RB, 16 * W).rearrange("p (i t) -> p i t", i=16)
        tk_pT = [Tk() for _ in range(8)]
        hstage = f32v(o_RB, 3 * D).rearrange("p (j d) -> p j d", j=3)
        tk_hst = [Tk() for _ in range(3)]
        tk_ypT = [Tk() for _ in range(8)]
        tk_ynT = [Tk() for _ in range(16)]
        o_PP = alloc(8 * W)
        poolpart = f32v(o_PP, 8 * W).rearrange("p (i t) -> p i t", i=8)
        tk_pp = [Tk() for _ in range(8)]
        stg = [f32v(o_RA, SLOTW), f32v(o_RB, SLOTW)]
        tk_stg = [Tk(), Tk()]
        o_RC = alloc(3 * DI // 2)
        xtok = bfv(o_RC, 3 * DI).rearrange("p (j c) -> p j c", j=3)
        mergedT = bfv(o_RC, 8 * W).rearrange("p (i t) -> p i t", i=8)
        tk_xtok = [Tk() for _ in range(3)]
        tk_mrg = [Tk() for _ in range(8)]
        BCT = bfv(alloc(8 * W // 2), 8 * W).rearrange("p (i t) -> p i t", i=8)
        tk_BCT = [Tk() for _ in range(8)]
        RAWW = 16 + W
        o_raw = alloc(2 * RAWW)
        raw = [f32v(o_raw, RAWW), f32v(o_raw + RAWW, RAWW)]
        tk_raw = [Tk(), Tk()]
        o_acc = alloc(2 * W)
        cacc = [f32v(o_acc, W), f32v(o_acc + W, W)]
        tk_acc = [Tk(), Tk()]
        halo_x = f32v(alloc(24 * 3), 72).rearrange("p (i k) -> p i k", i=24)
        halo_u = f32v(alloc(8 * 15), 120).rearrange("p (i k) -> p i k", i=8)
        halo_f = f32v(alloc(22 * 2), 44).rearrange("p (i k) -> p i k", i=22)
        tk_hx = [Tk() for _ in range(24)]
        tk_hu = [Tk() for _ in range(8)]
        tk_hf = [Tk() for _ in range(22)]
        xw = bfv(alloc(DI // 2), DI)
        tk_xw = Tk()
        Btok = bfv(alloc(256), 512)
        tk_Btok = Tk()
        Mt = bfv(alloc(NH * 128 // 2), NH * 128).rearrange("p (h t) -> p h t", h=NH)
        tk_Mt = [Tk() for _ in range(8)]
        esg = f32v(alloc(512), 512)
        tk_esg = Tk()
        Et = f32v(alloc(512), 512)
        tk_E = Tk()
        t1b = esg
        tk_t1 = tk_esg
        hS = f32v(alloc(DI), DI)
        tk_hS = [Tk() for _ in range(4)]
        hSb = bfv(alloc(DI // 2), DI)
        tk_hSb = [Tk() for _ in range(4)]
        o_tmp = alloc(3 * W)
        tmpA = [f32v(o_tmp + i * W, W) for i in range(3)]
        tk_tmpA = [Tk() for _ in range(3)]
        o_sq = alloc(W)
        sqb = [bfv(o_sq, W), bfv(o_sq + W // 2, W)]
        tk_sq = [Tk(), Tk()]
        rstd = f32v(alloc(4 * W), 4 * W).rearrange("p (g t) -> p g t", g=4)
        tk_rstd = [Tk() for _ in range(4)]
        UW = 16 + W
        o_u = alloc(4 * UW)
        ub = [f32v(o_u + i * UW, UW) for i in range(4)]
        tk_ub = [Tk() for _ in range(4)]
        NDT = 8
        o_dt = alloc(NDT * W)
        dtc = [f32v(o_dt + i * W, W) for i in range(NDT)]
        tk_dtc = [Tk() for _ in range(NDT)]
        S3 = bfv(alloc(W // 2), W)
        nS3 = bfv(alloc(W // 2), W)
        tk_S3, tk_nS3 = Tk(), Tk()
        wtok = f32v(alloc(3 * 32), 96).rearrange("p (j h) -> p j h", j=3)
        tk_wtok = [Tk() for _ in range(3)]
        dAt = f32v(alloc(32), 32)
        tk_dA = Tk()
        rdA = bfv(alloc(16), 32)
        tk_rdA = Tk()
        st12s = [f32v(alloc(16), 16) for _ in range(3)]
        tk_sts = [Tk() for _ in range(3)]
        mvs = [f32v(alloc(8), 8) for _ in range(3)]
        tk_mvs = [Tk() for _ in range(3)]
        dmask = f32v(alloc(W), W)
        tk_dmask = Tk()
        tokm0 = f32v(alloc(16 + W), 16 + W)
        invc0 = rstd
        ppt = f32v(alloc(256), 256)
        identf = f32v(alloc(128), 128)
        identb = bfv(alloc(64), 128)
        onesb = bfv(alloc(64), 128)
        SEL = bfv(alloc(16), 32)
        aneg = f32v(alloc(2), 2)
        tk_const = Tk()
        o_slot = alloc(NSLOT * SLOTW // 2)
        slots = [bfv(o_slot + i * SLOTW // 2, SLOTW) for i in range(NSLOT)]
        tk_slot = [Tk() for _ in range(NSLOT)]

        def PPc(c):
            return ppt[:, c:c + 1]

        tokmp = f32v(alloc(16 + W), 16 + W)
        epst = f32v(alloc(2), 2)
        epsc = epst[:, 0:1]
        rmsepsc = epst[:, 1:2]
        tk_cdma = Tk()
        tk_k = Tk()
        S.dma("sp", lambda e: e.dma_start(out=ppt, in_=pp_d[:, :]), "setup", writes=[tk_cdma])
        S.dma("sp", lambda e: e.dma_start(out=tokm0, in_=tm0_d[:, :]), "setup", writes=[tk_cdma])
        S.dma("sp", lambda e: e.dma_start(out=tokmp, in_=tmp_d[:, :]), "setup", writes=[tk_cdma])
        S.dma("sp", lambda e: e.dma_start(out=invc0.rearrange("p g t -> p (g t)"), in_=ic0_d[:, :]), "setup", writes=[tk_cdma])
        tk_cdma.w = ("setup", S.cnt["setup"])
        for t_ in tk_rstd:
            t_.w = tk_cdma.w
        S.op("pool", lambda e: e.memset(identf, 1.0), writes=[tk_k])
        S.op("pool", lambda e: e.affine_select(out=identf, in_=identf, pattern=[[1, 128]], compare_op=ALU.is_equal, fill=0.0, base=0, channel_multiplier=-1), reads=[tk_k], writes=[tk_k])
        S.op("pool", lambda e: e.tensor_copy(out=identb, in_=identf), reads=[tk_k], writes=[tk_k])
        S.op("pool", lambda e: e.memset(onesb, 1.0), writes=[tk_k])
        S.op("pool", lambda e: e.memset(epst[:, 0:1], LN_EPS), writes=[tk_k])
        S.op("pool", lambda e: e.memset(epst[:, 1:2], RMS_EPS), writes=[tk_k])
        S.op("pool", lambda e: e.memset(SEL, 0.0), writes=[tk_k])
        for r in range(3):
            S.op("pool", lambda e, r=r: e.tensor_copy(out=SEL[32 * r:32 * r + 32, :], in_=identf[32 * r:32 * r + 32, 32 * r:32 * r + 32]), reads=[tk_k], writes=[tk_k])
        S.op("pool", lambda e: e.memset(hS, 0.0), writes=tk_hS)
        S.op("pool", lambda e: e.memset(hSb, 0.0), writes=tk_hSb)
        S.op("pool", lambda e: e.memset(halo_x.rearrange("p i k -> p (i k)"), 0.0), writes=tk_hx)
        S.op("pool", lambda e: e.memset(halo_u.rearrange("p i k -> p (i k)"), 0.0), writes=tk_hu)
        S.op("pool", lambda e: e.memset(halo_f.rearrange("p i k -> p (i k)"), 0.0), writes=tk_hf)
        S.op("act", lambda e: e.activation(out=aneg[:, 0:1], in_=PPc(249), func=AF.Exp), reads=[tk_cdma], writes=[tk_k])
        S.op("dve", lambda e: e.tensor_scalar(out=aneg[:, 1:2], in0=aneg[:, 0:1], scalar1=-1.0, scalar2=None, op0=ALU.mult), reads=[tk_k], writes=[tk_k])
        S.op("pool", lambda e: e.memset(aneg[:, 0:1], 0.0), reads=[tk_cdma, tk_k], writes=[tk_const])

        def kc_view(ap2d, kc):
            return ap2d.rearrange("(k p) c -> p k c", p=128)

        def conv_sources(name):
            if name in ("U0", "U1"):
                c0 = C_U + 512 * int(name[1])
                return 8, 512, [((0, 512), kc_view(w_in_d[:, c0:c0 + 512], 8))]
            if name in ("GP0", "GP1"):
                c0 = C_GP + 512 * int(name[2])
                return 8, 512, [((0, 512), kc_view(w_in_d[:, c0:c0 + 512], 8))]
            if name in ("GS0", "GS1"):
                c0 = C_GS + 512 * int(name[2])
                return 8, 512, [((0, 512), kc_view(w_in_d[:, c0:c0 + 512], 8))]
            if name in ("BC0", "BC1"):
                c0 = C_B + 512 * int(name[2])
                return 8, 512, [((0, 512), kc_view(w_in_d[:, c0:c0 + 512], 8))]
            if name[0] == "X":
                c0 = C_X + 512 * int(name[1])
                return 8, 512, [((0, 512), kc_view(w_in_d[:, c0:c0 + 512], 8))]
            if name[0] == "Z":
                c0 = C_Z + 512 * int(name[1])
                return 8, 512, [((0, 512), kc_view(w_in_d[:, c0:c0 + 512], 8))]
            if name == "DT":
                return 8, 96, [((32 * r, 32 * r + 32), kc_view(w_in_d[:, C_DT:C_DT + 32], 8)) for r in range(3)]
            if name[:2] == "PS":
                q = int(name[2])
                return 16, 256, [((0, 256), kc_view(w_ps_d[:, 256 * q:256 * q + 256], 16))]
            if name[:2] == "PP":
                q = int(name[2])
                return 8, 512, [((0, 512), kc_view(w_pp_d[:, 512 * q:512 * q + 512], 8))]
            if name == "PW":
                return 8, 256, [((0, 256), w_pw_d.rearrange("g (k p) c -> p (g k) c", p=128))]
            if name[:2] == "WO":
                q = int(name[2])
                return 8, 512, [((0, 512), kc_view(w_o_d[:, 512 * q:512 * q + 512], 8))]
            if name[:2] == "FD":
                q = int(name[2])
                return 22, 256, [((0, 256), kc_view(w_dn_d[:, 256 * q:256 * q + 256], 22))]
            if name[0] == "F":
                fb = int(name[1:])
                return 8, 512, [((0, 256), kc_view(w_up_d[:, 256 * fb:256 * fb + 256], 8)),
                                ((256, 512), kc_view(w_up_d[:, DFF + 256 * fb:DFF + 256 * fb + 256], 8))]
            raise KeyError(name)

        BLK_SHAPE = {}
        cast_engs = ["dve", "pool", "act"]
        tk_blk = [[Tk()] for _ in range(NBLK)]
        for name in BLK_NAMES:
            KC, CB, srcs = conv_sources(name)
            BLK_SHAPE[name] = (KC, CB)
        PH0 = list(PRE_SEQ) if NT_PRE > 0 else list(BLK_NAMES)
        for pi, name in enumerate(PH0):
            bi = BLK_ID[name]
            KC, CB, srcs = conv_sources(name)
            n = KC * CB
            si = pi % 2
            sl = pi % NSLOT
            sview = stg[si][:, 0:n].rearrange("p (k c) -> p k c", k=KC)
            for (c0, c1), src in srcs:
                S.dma("sp", lambda e, sview=sview, c0=c0, c1=c1, src=src: e.dma_start(out=sview[:, :, c0:c1], in_=src),
                      f"cvi{si}", writes=[tk_stg[si]])
            ce = cast_engs[pi % 3]
            if ce == "act":
                S.op("act", lambda e, sl=sl, si=si, n=n: e.activation(out=slots[sl][:, 0:n], in_=stg[si][:, 0:n], func=AF.Copy),
                     reads=[tk_stg[si]], writes=[tk_slot[sl]])
            else:
                S.op(ce, lambda e, sl=sl, si=si, n=n: e.tensor_copy(out=slots[sl][:, 0:n], in_=stg[si][:, 0:n]),
                     reads=[tk_stg[si]], writes=[tk_slot[sl]])
            S.dma("act", lambda e, sl=sl, bi=bi, n=n: e.dma_start(out=wsc[bi, :, 0:n], in_=slots[sl][:, 0:n]),
                  f"cvo{sl}", reads=[tk_slot[sl]], writes=tk_blk[bi])
        alias_tokens(tk_xs + tk_act, [tk_stg[0]])
        alias_tokens(tk_pT + tk_ypT + tk_ynT + tk_pp + tk_hst, [tk_stg[1]])

        stq = [f32v(o_PP, 1536), f32v(o_PP + 1536, 1536)]
        bq = [bfv(o_u, 1536), bfv(o_u + 800, 1536)]
        tk_stq = [Tk(), Tk()]
        tk_bq = [Tk(), Tk()]
        alias_tokens(tk_stq, tk_pp)
        bgq = []
        for name in BLK_NAMES:
            if name in PH0:
                continue
            bi = BLK_ID[name]
            KC, CB, srcs = conv_sources(name)
            cuts = [round(KC * i / 4) for i in range(5)]
            tk_blk[bi] = [Tk() for _ in range(4)]
            for qi in range(4):
                bgq.append((bi, KC, CB, srcs, cuts[qi], cuts[qi + 1], qi))
        bgst = {"i": 0}

        def bg_load(i):
            bi, KC, CB, srcs, k0, k1, qi = bgq[i]
            b = i % 2
            nk = k1 - k0
            n = nk * CB
            sview = stq[b][:, 0:n].rearrange("p (k c) -> p k c", k=nk)
            for (c0, c1), src in srcs:
                S.dma("pool", lambda e, sview=sview, c0=c0, c1=c1, src=src, k0=k0, k1=k1: e.dma_start(out=sview[:, :, c0:c1], in_=src[:, k0:k1, :]),
                      f"cvi{b}", writes=[tk_stq[b]])

        def bg_step():
            i = bgst["i"]
            if i >= len(bgq):
                return
            bgst["i"] += 1
            if i == 0:
                bg_load(0)
            if i + 1 < len(bgq):
                bg_load(i + 1)
            bi, KC, CB, srcs, k0, k1, qi = bgq[i]
            b = i % 2
            n = (k1 - k0) * CB
            S.op("pool", lambda e, b=b, n=n: e.tensor_copy(out=bq[b][:, 0:n], in_=stq[b][:, 0:n]), reads=[tk_stq[b]], writes=[tk_bq[b]])
            S.dma("pool", lambda e, b=b, bi=bi, n=n, k0=k0, CB=CB: e.dma_start(out=wsc[bi, :, k0 * CB:k0 * CB + n], in_=bq[b][:, 0:n]),
                  f"cvo{b}", reads=[tk_bq[b]], writes=[tk_blk[bi][qi]])

        def bg_flush():
            while bgst["i"] < len(bgq):
                bg_step()
            alias_tokens(tk_pp, tk_stq)
            alias_tokens(tk_ub[0:4], tk_bq)

        seq = []
        for _ in range(NT_PRE):
            seq += PRE_SEQ
        for _ in range(NT_MAIN):
            seq += MAIN_SEQ
        wst = {"issued": 0, "next": 0}

        def w_issue():
            k = wst["issued"]
            if k >= len(seq):
                return
            name = seq[k]
            bi = BLK_ID[name]
            KC, CB = BLK_SHAPE[name]
            n = KC * CB
            sl = k % NSLOT
            while not all(t.w is not None for t in tk_blk[bi]):
                bg_step()
            S.dma("sp", lambda e, sl=sl, bi=bi, n=n: e.dma_start(out=slots[sl][:, 0:n], in_=wsc[bi, :, 0:n]),
                  f"wl{sl}", reads=tk_blk[bi], writes=[tk_slot[sl]])
            assert all(t.w is not None for t in tk_blk[bi]), ("block not converted before its load was issued", name)
            wst["issued"] += 1

        class WB:
            pass

        def w_get(name):
            k = wst["next"]
            assert seq[k] == name, (k, seq[k], name)
            assert k < wst["issued"], (k, wst["issued"])
            wst["next"] += 1
            KC, CB = BLK_SHAPE[name]
            b = WB()
            b.k = k
            b.tk = tk_slot[k % NSLOT]
            b.v = slots[k % NSLOT][:, 0:KC * CB].rearrange("p (k c) -> p k c", k=KC)
            return b

        def w_rel(b):
            w_issue()

        for _ in range(NSLOT):
            w_issue()

        bank_rr = [0]

        def next_bank():
            b = bank_rr[0]
            bank_rr[0] ^= 1
            return b
        PT_, PS_, PG_, PY_, PO_, PX_ = 2, 3, 4, 5, 6, 7

        def mm(out_ap, lhsT, rhs, start, stop, reads, bank, **kw):
            S.op("pe", lambda e: e.matmul(out_ap, lhsT=lhsT, rhs=rhs, start=start, stop=stop, **kw),
                 reads=reads, writes=[tk_ps[bank]])

        def proj_fm(wb, col0, rhs_fn, KC, rhs_tks, bank, ncols=128, wcols=W):
            for kc in range(KC):
                mm(psb[bank][0:ncols, 0:wcols], wb.v[:, kc, col0:col0 + ncols], rhs_fn(kc), kc == 0, kc == KC - 1,
                   [wb.tk] + rhs_tks, bank)

        def layernorm_chunk(buf, tks, j):
            hj = buf[:, j, :]
            st12, tk_st, mv, tk_mv = st12s[j], tk_sts[j], mvs[j], tk_mvs[j]
            for h2 in range(2):
                S.op("dve", lambda e, h2=h2: e.bn_stats(out=st12[:, 6 * h2:6 * h2 + 6], in_=buf[:, j, 512 * h2:512 * h2 + 512]),
                     reads=[tks[j]], writes=[tk_st])
            S.op("dve", lambda e: e.bn_aggr(out=mv[:, 0:2], in_=st12[:, 0:12]),
                 reads=[tk_st], writes=[tk_mv])
            S.op("act", lambda e: e.activation(out=mv[:, 2:3], in_=mv[:, 1:2], func=AF.Sqrt, bias=epsc, scale=1.0),
                 reads=[tk_mv, tk_const], writes=[tk_mv])
            S.op("dve", lambda e: e.reciprocal(out=mv[:, 3:4], in_=mv[:, 2:3]), reads=[tk_mv], writes=[tk_mv])
            S.op("dve", lambda e: e.tensor_scalar(out=hj, in0=hj, scalar1=mv[:, 0:1], scalar2=mv[:, 3:4], op0=ALU.subtract, op1=ALU.mult),
                 reads=[tk_mv, tks[j]], writes=[tks[j]])
            S.op("dve", lambda e: e.tensor_tensor(out=hj, in0=hj, in1=gbt[:, 0:D], op=ALU.mult),
                 reads=[tk_gb, tks[j]], writes=[tks[j]])
            S.op("pool", lambda e: e.tensor_tensor(out=hj, in0=hj, in1=gbt[:, D:2 * D], op=ALU.add),
                 reads=[tk_gb, tks[j]], writes=[tks[j]])

        gb_cur = [None]

        def load_gb(row):
            if gb_cur[0] == row:
                return
            gb_cur[0] = row
            S.dma("act", lambda e: e.dma_start(out=gbt, in_=lnrow_d[row:row + 1, :].partition_broadcast(128)),
                  "gb", writes=[tk_gb])

        def transpose_to(buf, tks, j, dst, dst_tks):
            for half in range(2):
                for k4 in range(4):
                    kc = half * 4 + k4
                    S.op("pe", lambda e, kc=kc, k4=k4: e.transpose(out=psb[PT_][:, 128 * k4:128 * k4 + 128], in_=buf[:, j, 128 * kc:128 * kc + 128], identity=identf),
                         reads=[tks[j], tk_const], writes=[tk_ps[PT_]])
                S.op("act", lambda e, half=half: e.activation(out=dst[:, 4 * half:4 * half + 4, 128 * j:128 * j + 128],
                                                              in_=psb[PT_][:, 0:512].rearrange("p (k t) -> p k t", k=4), func=AF.Copy),
                     reads=[tk_ps[PT_]], writes=[dst_tks[j]])

        NT_ALL = NT_PRE + NT_MAIN

        def pf_load(gt):
            if gt < NT_PRE:
                xd, row0 = xpre_d, gt * W
            else:
                xd, row0 = xmain_d, (gt - NT_PRE) * W
            for j in range(NCH):
                r0 = row0 + 128 * j
                S.dma("act", lambda e, j=j, r0=r0: e.dma_start(out=hstage[:, j, :], in_=xd[r0:r0 + 128, :]), f"xin{j}", writes=[tk_hst[j]])

        def pf_ln(j):
            load_gb(0)
            layernorm_chunk(hstage, tk_hst, j)

        def pf_compute(gt):
            pf_load(gt)
            for j in range(NCH):
                pf_ln(j)

        def pf_transposes(gt):
            for j in range(NCH):
                transpose_to(hstage, tk_hst, j, hTs[gt % 2], tk_hTs[gt % 2])

        def conv_chunk(bank, halo, tk_h, hk, wcols, bcol, first, silu_out, silu_tks, conv_eng):
            ri = conv_chunk.rr
            conv_chunk.rr ^= 1
            rw = raw[ri]
            ac = cacc[ri]
            if halo_eng[0] == "act":
                S.op("act", lambda e: e.activation(out=rw[:, 16 - hk:16], in_=halo, func=AF.Copy), reads=[tk_h], writes=[tk_raw[ri]])
            else:
                S.op("pool", lambda e: e.tensor_copy(out=rw[:, 16 - hk:16], in_=halo), reads=[tk_h], writes=[tk_raw[ri]])
            S.op("act", lambda e: e.activation(out=rw[:, 16:16 + W], in_=psb[bank][:, 0:W], func=AF.Copy), reads=[tk_ps[bank]], writes=[tk_raw[ri]])
            if first is not None and first is not False:
                mk = first
                S.op("pool", lambda e: e.tensor_tensor(out=rw[:, 16 - hk:16 + W], in0=rw[:, 16 - hk:16 + W], in1=mk[:, 16 - hk:16 + W], op=ALU.mult),
                     reads=[tk_raw[ri], tk_const], writes=[tk_raw[ri]])
            if halo_eng[0] == "act":
                S.op("act", lambda e: e.activation(out=halo, in_=rw[:, 16 + W - hk:16 + W], func=AF.Copy), reads=[tk_raw[ri]], writes=[tk_h])
            else:
                S.op("pool", lambda e: e.tensor_copy(out=halo, in_=rw[:, 16 + W - hk:16 + W]), reads=[tk_raw[ri]], writes=[tk_h])
            if first is not None and first is not False:
                S.op(conv_eng, lambda e: e.tensor_scalar(out=ac, in0=rw[:, 16:16 + W], scalar1=PPc(wcols[hk]), scalar2=PPc(bcol), op0=ALU.mult, op1=ALU.add),
                     reads=[tk_raw[ri], tk_const], writes=[tk_acc[ri]])
            else:
                S.op("act", lambda e: e.activation(out=ac, in_=psb[bank][:, 0:W], func=AF.Identity, scale=PPc(wcols[hk]), bias=PPc(bcol)),
                     reads=[tk_ps[bank], tk_const], writes=[tk_acc[ri]])
            for k in range(hk):
                sh = hk - k
                S.op(conv_eng, lambda e, k=k, sh=sh: e.scalar_tensor_tensor(out=ac, in0=rw[:, 16 - sh:16 - sh + W], scalar=PPc(wcols[k]), in1=ac, op0=ALU.mult, op1=ALU.add),
                     reads=[tk_raw[ri], tk_acc[ri], tk_const], writes=[tk_acc[ri]])
            def partB():
                S.op("act", lambda e: e.activation(out=silu_out, in_=ac, func=AF.Silu), reads=[tk_acc[ri]], writes=silu_tks)
            return partB
        conv_chunk.rr = 0
        halo_eng = ["pool"]

        def dt_phase(md, col0, masked):
            wb = w_get("DT")
            proj_fm(wb, 0, lambda kc: cur.v[:, kc, :], 8, cur.tk, PX_, ncols=96)
            w_rel(wb)
            P = slice(0, 96)
            e_, dt_, dta_, acum_, lnd_, q_, t0_, t1_ = [d[P, :] for d in dtc]
            S.op("act", lambda e: e.activation(out=e_, in_=psb[PX_][0:96, 0:W], func=AF.Exp, bias=ppt[0:96, 248:249], scale=1.0),
                 reads=[tk_ps[PX_], tk_const], writes=[tk_dtc[0]])
            S.op("act", lambda e: e.activation(out=dt_, in_=e_, func=AF.Ln, bias=1.0), reads=[tk_dtc[0]], writes=[tk_dtc[1]])
            if masked:
                S.dma("act", lambda e: e.dma_start(out=dmask[0:96, :], in_=md[:, col0:col0 + W]), "msk", writes=[tk_dmask])
                S.op("dve", lambda e: e.tensor_scalar(out=t0_, in0=dmask[0:96, :], scalar1=-1.0, scalar2=1.0, op0=ALU.mult, op1=ALU.add),
                     reads=[tk_dmask], writes=[tk_dtc[6]])
                S.op("dve", lambda e: e.tensor_tensor(out=dt_, in0=dt_, in1=dmask[0:96, :], op=ALU.mult), reads=[tk_dmask, tk_dtc[1]], writes=[tk_dtc[1]])
                S.op("dve", lambda e: e.tensor_tensor(out=t1_, in0=dt_, in1=t0_, op=ALU.add), reads=[tk_dtc[1], tk_dtc[6]], writes=[tk_dtc[7]])
                S.op("act", lambda e: e.activation(out=lnd_, in_=t1_, func=AF.Ln), reads=[tk_dtc[7]], writes=[tk_dtc[4]])
                S.op("dve", lambda e: e.scalar_tensor_tensor(out=lnd_, in0=t0_, scalar=-200.0, in1=lnd_, op0=ALU.mult, op1=ALU.add),
                     reads=[tk_dtc[6], tk_dtc[4]], writes=[tk_dtc[4]])
            else:
                S.op("act", lambda e: e.activation(out=lnd_, in_=dt_, func=AF.Ln), reads=[tk_dtc[1]], writes=[tk_dtc[4]])
            S.op("dve", lambda e: e.tensor_scalar(out=dta_, in0=dt_, scalar1=aneg[0:96, 1:2], scalar2=None, op0=ALU.mult),
                 reads=[tk_dtc[1], tk_const], writes=[tk_dtc[2]])
            S.op("dve", lambda e: e.memset(t1_, 1.0), reads=[], writes=[tk_dtc[7]])
            for j in range(NCH):
                cs = slice(128 * j, 128 * j + 128)
                S.op("dve", lambda e, cs=cs: e.tensor_tensor_scan(out=acum_[:, cs], data0=t1_[:, cs], data1=dta_[:, cs], initial=0.0, op0=ALU.mult, op1=ALU.add),
                     reads=[tk_dtc[2], tk_dtc[7]], writes=[tk_dtc[3]])
            S.op("dve", lambda e: e.tensor_tensor(out=q_, in0=lnd_, in1=acum_, op=ALU.subtract), reads=[tk_dtc[4], tk_dtc[3]], writes=[tk_dtc[5]])

            def split3(src, tk_src, dst, tk_dst):
                hi = dtc[6]
                r1 = dtc[7]
                mid = dtc[0]
                S.op("dve", lambda e: e.tensor_copy(out=dst[0:32, :], in_=src[0:32, :]), reads=[tk_src], writes=[tk_dst])
                S.op("dve", lambda e: e.tensor_copy(out=mid[0:96, 0:W // 2].bitcast(BF16), in_=src[0:96, :]), reads=[tk_src], writes=[tk_dtc[0]])
                S.op("dve", lambda e: e.tensor_tensor(out=r1[0:96, :], in0=src[0:96, :], in1=mid[0:96, 0:W // 2].bitcast(BF16), op=ALU.subtract),
                     reads=[tk_src, tk_dtc[0]], writes=[tk_dtc[7]])
                S.op("dve", lambda e: e.tensor_copy(out=dst[32:64, :], in_=r1[32:64, :]), reads=[tk_dtc[7]], writes=[tk_dst])
                S.op("dve", lambda e: e.tensor_copy(out=hi[64:96, 0:W // 2].bitcast(BF16), in_=r1[64:96, :]), reads=[tk_dtc[7]], writes=[tk_dtc[6]])
                S.op("dve", lambda e: e.tensor_tensor(out=r1[64:96, :], in0=r1[64:96, :], in1=hi[64:96, 0:W // 2].bitcast(BF16), op=ALU.subtract),
                     reads=[tk_dtc[6], tk_dtc[7]], writes=[tk_dtc[7]])
                S.op("dve", lambda e: e.tensor_copy(out=dst[64:96, :], in_=r1[64:96, :]), reads=[tk_dtc[7]], writes=[tk_dst])
            split3(dtc[3], tk_dtc[3], S3, tk_S3)
            split3(dtc[5], tk_dtc[5], nS3, tk_nS3)
            for j in range(NCH):
                cs = slice(128 * j, 128 * j + 128)
                S.op("act", lambda e, cs=cs, j=j: e.activation(out=dtc[1][0:32, cs], in_=dtc[5][0:32, cs], func=AF.Exp,
                                                               bias=dtc[3][0:32, 128 * j + 127:128 * j + 128], scale=1.0),
                     reads=[tk_dtc[5], tk_dtc[3]], writes=[tk_dtc[1]])
                S.op("pe", lambda e, cs=cs, j=j: e.transpose(out=psb[PX_][:, 128 + 32 * j:160 + 32 * j], in_=dtc[1][0:32, cs], identity=identf[0:32, 0:32]),
                     reads=[tk_dtc[1], tk_const], writes=[tk_ps[PX_]])
            S.op("act", lambda e: e.activation(out=wtok.rearrange("p j h -> p (j h)"), in_=psb[PX_][:, 128:224], func=AF.Copy),
                 reads=[tk_ps[PX_]], writes=tk_wtok)

        def xbc_phase(first, with_c, hook=None):
            names = ["BC0"] + (["BC1"] if with_c else []) + ["X0", "X1", "X2", "X3"]
            pend = [None]
            for name in names:
                wb = w_get(name)
                for c in range(4):
                    if name[0] == "B":
                        idx = 16 + 4 * int(name[2]) + c
                        out_ap, otk = BCT[:, idx - 16, :], [tk_BCT[idx - 16]]
                    else:
                        idx = 4 * int(name[1]) + c
                        out_ap, otk = xs[:, idx, :], [tk_xs[idx]]
                    bank = next_bank()
                    proj_fm(wb, 128 * c, lambda kc: cur.v[:, kc, :], 8, cur.tk, bank)
                    pb_ = conv_chunk(bank, halo_x[:, idx, :], tk_hx[idx], 3, [0 + idx, 24 + idx, 48 + idx, 72 + idx], 96 + idx,
                                     first, out_ap, otk, "dve")
                    if pend[0] is not None:
                        pend[0]()
                    pend[0] = pb_
                    if hook is not None:
                        hook()
                w_rel(wb)
            if pend[0] is not None:
                pend[0]()

        def state_prep_chunk(j):
            cs = slice(128 * j, 128 * j + 128)
            pTb = psb[PG_][:, 0:256].bitcast(BF16)
            for g in range(4):
                S.op("pe", lambda e, g=g: e.transpose(out=pTb[:, 128 * g:128 * g + 128], in_=BCT[:, g, cs], identity=identb),
                     reads=[tk_BCT[g], tk_const], writes=[tk_ps[PG_]])
            S.op("act", lambda e: e.activation(out=Btok, in_=pTb, func=AF.Copy), reads=[tk_ps[PG_]], writes=[tk_Btok])
            for q4 in range(4):
                tb = (PT_, PY_)[q4 % 2]
                for ii in range(4):
                    i = 4 * q4 + ii
                    S.op("pe", lambda e, i=i, ii=ii, tb=tb: e.transpose(out=psb[tb][:, 128 * ii:128 * ii + 128], in_=xs[:, i, cs], identity=identf),
                         reads=[tk_xs[i], tk_const], writes=[tk_ps[tb]])
                S.op("act", lambda e, q4=q4, tb=tb: e.activation(out=xtok[:, j, 512 * q4:512 * q4 + 512], in_=psb[tb][:, 0:512], func=AF.Copy),
                     reads=[tk_ps[tb]], writes=[tk_xtok[j]])
            S.op("dve", lambda e: e.tensor_tensor(out=xw.rearrange("p (h c) -> p h c", h=NH), in0=xtok[:, j, :].rearrange("p (h c) -> p h c", h=NH),
                                                  in1=wtok[:, j, :].unsqueeze(2).to_broadcast([128, NH, 64]), op=ALU.mult),
                 reads=[tk_xtok[j], tk_wtok[j]], writes=[tk_xw])

        def state_update_chunk(j):
            S.op("dve", lambda e: e.tensor_scalar(out=rdA[0:96, :], in0=SEL[0:96, :], scalar1=S3[0:96, 128 * j + 127:128 * j + 128], scalar2=None, op0=ALU.mult),
                 reads=[tk_S3, tk_const], writes=[tk_rdA])
            S.op("pe", lambda e: e.matmul(psb[PX_][:, 256:288], lhsT=onesb[0:96, :], rhs=rdA[0:96, :], start=True, stop=True),
                 reads=[tk_rdA, tk_const], writes=[tk_ps[PX_]])
            S.op("act", lambda e: e.activation(out=dAt, in_=psb[PX_][:, 256:288], func=AF.Exp), reads=[tk_ps[PX_]], writes=[tk_dA])
            for g in range(4):
                gs = slice(512 * g, 512 * g + 512)
                S.op("dve", lambda e, g=g, gs=gs: e.tensor_tensor(out=hS[:, gs].rearrange("p (h c) -> p h c", h=8), in0=hS[:, gs].rearrange("p (h c) -> p h c", h=8),
                                                                 in1=dAt[:, 8 * g:8 * g + 8].unsqueeze(2).to_broadcast([128, 8, 64]), op=ALU.mult),
                     reads=[tk_dA, tk_hS[g]], writes=[tk_hS[g]])
            for g in range(4):
                gs = slice(512 * g, 512 * g + 512)
                bk = (PO_, 0)[g % 2]
                S.op("pe", lambda e, g=g, gs=gs, bk=bk: e.matmul(psb[bk][:, 0:512], lhsT=Btok[:, 128 * g:128 * g + 128], rhs=xw[:, gs], start=True, stop=True),
                     reads=[tk_Btok, tk_xw], writes=[tk_ps[bk]])
                S.op("dve", lambda e, gs=gs, bk=bk: e.tensor_tensor(out=hS[:, gs], in0=hS[:, gs], in1=psb[bk][:, 0:512], op=ALU.add),
                     reads=[tk_ps[bk], tk_hS[g]], writes=[tk_hS[g]])
                S.op("act", lambda e, gs=gs: e.activation(out=hSb[:, gs], in_=hS[:, gs], func=AF.Copy), reads=[tk_hS[g]], writes=[tk_hSb[g]])

        esg2 = f32v(o_u, 512)
        tk_esg2 = Tk()
        Et2 = f32v(o_tmp, 512)
        tk_E2 = Tk()
        esgs, tk_esgs = [esg, esg2], [tk_esg, tk_esg2]
        Ets, tk_Es = [Et, Et2], [tk_E, tk_E2]

        def ssd_y_chunk(j):
            cs = slice(128 * j, 128 * j + 128)
            for g in range(4):
                S.op("pe", lambda e, g=g: e.matmul(psb[PG_][:, 128 * g:128 * g + 128], lhsT=BCT[:, g, cs], rhs=BCT[:, 4 + g, cs], start=True, stop=True),
                     reads=[tk_BCT[g], tk_BCT[4 + g]], writes=[tk_ps[PG_]])
            for hq in range(8):
                psk = (PS_, 0)[hq % 2]
                eb, tke = esgs[hq % 2], tk_esgs[hq % 2]
                for hh in range(4):
                    h = 4 * hq + hh
                    o = psb[psk][:, 128 * hh:128 * hh + 128]
                    S.op("pe", lambda e, o=o, h=h: e.matmul(o, lhsT=SEL[0:96, h:h + 1].to_broadcast([96, 128]), rhs=S3[0:96, cs], start=True, stop=False),
                         reads=[tk_S3, tk_const], writes=[tk_ps[psk]])
                    S.op("pe", lambda e, o=o, h=h: e.matmul(o, lhsT=nS3[0:96, cs], rhs=SEL[0:96, h:h + 1].to_broadcast([96, 128]), start=False, stop=True),
                         reads=[tk_nS3, tk_const], writes=[tk_ps[psk]])
                S.op("act", lambda e, psk=psk, eb=eb: e.activation(out=eb, in_=psb[psk][:, 0:512], func=AF.Exp), reads=[tk_ps[psk]], writes=[tke])
                g = hq // 2
                S.op("dve", lambda e, hq=hq, g=g, eb=eb: e.tensor_tensor(out=Mt[:, 4 * hq:4 * hq + 4, :], in0=eb.rearrange("p (h t) -> p h t", h=4),
                                                                        in1=psb[PG_][:, 128 * g:128 * g + 128].unsqueeze(1).to_broadcast([128, 4, 128]), op=ALU.mult),
                     reads=[tke, tk_ps[PG_]], writes=[tk_Mt[hq]])
                S.op("pool", lambda e, hq=hq: e.affine_select(out=Mt[:, 4 * hq:4 * hq + 4, :], in_=Mt[:, 4 * hq:4 * hq + 4, :], pattern=[[0, 4], [1, 128]],
                                                              compare_op=ALU.is_ge, fill=0.0, base=0, channel_multiplier=-1),
                     reads=[tk_Mt[hq]], writes=[tk_Mt[hq]])
            for pq in range(4):
                g = pq
                pxk = (PX_, 1)[pq % 2]
                Eb, tkE = Ets[pq % 2], tk_Es[pq % 2]
                for ii in range(4):
                    i = 4 * pq + ii
                    for hh in range(2):
                        h = 2 * i + hh
                        S.op("pe", lambda e, ii=ii, hh=hh, h=h, pxk=pxk: e.matmul(psb[pxk][64 * hh:64 * hh + 64, 128 * ii:128 * ii + 128], lhsT=SEL[0:96, h:h + 1].to_broadcast([96, 64]),
                                                                                  rhs=S3[0:96, cs], start=True, stop=True, tile_position=(0, 64 * hh)),
                             reads=[tk_S3, tk_const], writes=[tk_ps[pxk]])
                S.op("act", lambda e, pxk=pxk, Eb=Eb: e.activation(out=Eb, in_=psb[pxk][:, 0:512], func=AF.Exp), reads=[tk_ps[pxk]], writes=[tkE])
                for ii in range(4):
                    i = 4 * pq + ii
                    S.op("pe", lambda e, ii=ii, i=i, g=g: e.matmul(psb[PO_][:, 128 * ii:128 * ii + 128], lhsT=hSb[:, 128 * i:128 * i + 128], rhs=BCT[:, 4 + g, cs], start=True, stop=True),
                         reads=[tk_hSb[g], tk_BCT[4 + g]], writes=[tk_ps[PO_]])
                for ii in range(4):
                    i = 4 * pq + ii
                    for hh in range(2):
                        h = 2 * i + hh
                        S.op("pe", lambda e, ii=ii, hh=hh, h=h: e.matmul(psb[PY_][64 * hh:64 * hh + 64, 128 * ii:128 * ii + 128], lhsT=xtok[:, j, 64 * h:64 * h + 64],
                                                                         rhs=Mt[:, h, :], start=True, stop=True, tile_position=(0, 64 * hh)),
                             reads=[tk_xtok[j], tk_Mt[h // 4]], writes=[tk_ps[PY_]])
                S.op("dve", lambda e, Eb=Eb: e.tensor_tensor(out=t1b, in0=Eb, in1=psb[PO_][:, 0:512], op=ALU.mult), reads=[tkE, tk_ps[PO_]], writes=[tk_t1])
                for ii in range(4):
                    i = 4 * pq + ii
                    S.op("dve", lambda e, ii=ii, i=i: e.scalar_tensor_tensor(out=xs[:, i, cs], in0=xs[:, i, cs], scalar=PPc(120 + i), in1=t1b[:, 128 * ii:128 * ii + 128],
                                                                             op0=ALU.mult, op1=ALU.add),
                         reads=[tk_t1, tk_xs[i], tk_const], writes=[tk_xs[i]])
                S.op("dve", lambda e, pq=pq: e.tensor_tensor(out=xs[:, 4 * pq:4 * pq + 4, cs], in0=xs[:, 4 * pq:4 * pq + 4, cs],
                                                             in1=psb[PY_][:, 0:512].rearrange("p (i t) -> p i t", i=4), op=ALU.add),
                     reads=[tk_ps[PY_]] + tk_xs[4 * pq:4 * pq + 4], writes=tk_xs[4 * pq:4 * pq + 4])

        tk_dbg = []

        def dump(dst, src_ap, toks):
            t = Tk()
            tk_dbg.append(t)
            S.dma("pool", lambda e: e.dma_start(out=dst[:, :], in_=src_ap), "dbg", reads=toks, writes=[t])

        def tile_pre(ti):
            gt = ti
            halo_eng[0] = "act"
            cur.v, cur.tk = hTs[gt % 2], tk_hTs[gt % 2]
            dt_phase(mpre_d, ti * W, True)
            quota = [-(-len(bgq) // max(NT_PRE, 1))]

            def hook():
                if quota[0] > 0:
                    quota[0] -= 1
                    bg_step()
            xbc_phase(tokmp if ti == 0 else None, False, hook)
            while quota[0] > 0:
                hook()
            if gt + 1 < NT_ALL:
                pf_load(gt + 1)
            for j in range(NCH):
                state_prep_chunk(j)
                state_update_chunk(j)
                if gt + 1 < NT_ALL:
                    pf_ln(j)
            if gt + 1 < NT_ALL:
                pf_transposes(gt + 1)

        def tile_main(ti):
            first = (ti == 0)
            gt = NT_PRE + ti
            halo_eng[0] = "pool"
            cur.v, cur.tk = hTs[gt % 2], tk_hTs[gt % 2]
            for j in range(NCH):
                S.op("act", lambda e, j=j: e.activation(out=hres[:, j, :], in_=hstage[:, j, :], func=AF.Copy), reads=[tk_hst[j]], writes=[tk_hres[j]])
            dbgt = debug and ti == NT_MAIN - 1
            if dbgt:
                dump(dbg_h0, hres.rearrange('p j d -> p (j d)'), tk_hres)
            alias_tokens(tk_pT + tk_ypT, tk_ynT + tk_hst)
            for ub_i in range(2):
                wb = w_get(f"U{ub_i}")
                for c in range(4):
                    uc = 4 * ub_i + c
                    wwin = POOL_WINDOWS[uc // 2]
                    bank = next_bank()
                    proj_fm(wb, 128 * c, lambda kc: cur.v[:, kc, :], 8, cur.tk, bank)
                    u0i = 0 if uc % 2 == 0 else 3
                    u0 = ub[u0i]
                    S.op("pool", lambda e, u0=u0, uc=uc: e.tensor_copy(out=u0[:, 1:16], in_=halo_u[:, uc, :]), reads=[tk_hu[uc]], writes=[tk_ub[u0i]])
                    S.op("act", lambda e, u0=u0, bank=bank: e.activation(out=u0[:, 16:16 + W], in_=psb[bank][:, 0:W], func=AF.Copy), reads=[tk_ps[bank]], writes=[tk_ub[u0i]])
                    if first:
                        S.op("pool", lambda e, u0=u0: e.tensor_tensor(out=u0[:, 1:16 + W], in0=u0[:, 1:16 + W], in1=tokm0[:, 1:16 + W], op=ALU.mult),
                             reads=[tk_ub[u0i], tk_const], writes=[tk_ub[u0i]])
                    S.op("pool", lambda e, u0=u0, uc=uc: e.tensor_copy(out=halo_u[:, uc, :], in_=u0[:, 16 + W - 15:16 + W]), reads=[tk_ub[u0i]], writes=[tk_hu[uc]])
                    src, si = u0, u0i
                    k = 1
                    lo = 1
                    while k < wwin:
                        lo += k
                        di = 1 if si != 1 else 2
                        dst = ub[di]
                        S.op("pool", lambda e, src=src, dst=dst, lo=lo, k=k: e.tensor_tensor(out=dst[:, lo:16 + W], in0=src[:, lo:16 + W], in1=src[:, lo - k:16 + W - k], op=ALU.add),
                             reads=[tk_ub[si]], writes=[tk_ub[di]])
                        src, si = dst, di
                        k *= 2
                    if first:
                        S.op("dve", lambda e, src=src, uc=uc: e.tensor_tensor(out=src[:, 16:16 + W], in0=src[:, 16:16 + W], in1=invc0[:, uc // 2, :], op=ALU.mult),
                             reads=[tk_ub[si], tk_const, tk_rstd[uc // 2]], writes=[tk_ub[si]])
                        S.op("dve", lambda e, u0=u0, src=src, uc=uc: e.tensor_tensor(out=pT[:, uc, :], in0=src[:, 16:16 + W], in1=u0[:, 16:16 + W], op=ALU.subtract),
                             reads=[tk_ub[si], tk_ub[u0i]], writes=[tk_pT[uc]])
                    else:
                        S.op("dve", lambda e, u0=u0, src=src, uc=uc, wwin=wwin: e.scalar_tensor_tensor(out=pT[:, uc, :], in0=src[:, 16:16 + W], scalar=1.0 / wwin, in1=u0[:, 16:16 + W],
                                                                                                 op0=ALU.mult, op1=ALU.subtract),
                             reads=[tk_ub[si], tk_ub[u0i]], writes=[tk_pT[uc]])
                w_rel(wb)
            wb = w_get("PW")
            for oc in range(8):
                g, oh = oc // 2, oc % 2
                bank = next_bank()
                for kc in range(2):
                    mm(psb[bank][:, 0:W], wb.v[:, 2 * g + kc, 128 * oh:128 * oh + 128], pT[:, 2 * g + kc, :], kc == 0, kc == 1,
                       [wb.tk, tk_pT[2 * g + kc]], bank)
                S.op("act", lambda e, bank=bank, oc=oc: e.activation(out=ypT[:, oc, :], in_=psb[bank][:, 0:W], func=AF.Identity, scale=PPc(152 + oc)),
                     reads=[tk_ps[bank], tk_const], writes=[tk_ypT[oc]])
            w_rel(wb)
            for half in range(2):
                wpp = w_get(f"PP{half}")
                wgp = w_get(f"GP{half}")
                for c in range(4):
                    dc = 4 * half + c
                    b1 = next_bank()
                    proj_fm(wgp, 128 * c, lambda kc: cur.v[:, kc, :], 8, cur.tk, b1)
                    b0 = next_bank()
                    proj_fm(wpp, 128 * c, lambda kc: ypT[:, kc, :], 8, tk_ypT, b0)
                    ta = dc % 3
                    S.op("act", lambda e, b1=b1, ta=ta: e.activation(out=tmpA[ta], in_=psb[b1][:, 0:W], func=AF.Sigmoid), reads=[tk_ps[b1]], writes=[tk_tmpA[ta]])
                    S.op("dve", lambda e, b0=b0, ta=ta, dc=dc: e.tensor_tensor(out=poolpart[:, dc, :], in0=tmpA[ta], in1=psb[b0][:, 0:W], op=ALU.mult),
                         reads=[tk_ps[b0], tk_tmpA[ta]], writes=[tk_pp[dc]])
                w_rel(wpp)
                w_rel(wgp)
            if dbgt:
                dump(dbg_yp, ypT.rearrange('p i t -> p (i t)'), tk_ypT)
                dump(dbg_pp, poolpart.rearrange('p i t -> p (i t)'), tk_pp)
            alias_tokens(tk_xs, tk_act)
            alias_tokens(tk_xtok, tk_mrg)
            alias_tokens([tk_esg2], tk_ub[0:2])
            alias_tokens([tk_E2], tk_tmpA[0:2])
            dt_phase(mmain_d, ti * W, first)
            xbc_phase(tokm0 if first else None, True)
            if dbgt:
                dump(dbg_xs, xs.rearrange('p i t -> p (i t)'), tk_xs)
                dump(dbg_bc, BCT.rearrange('p i t -> p (i t)'), tk_BCT)
            for j in range(NCH):
                state_prep_chunk(j)
                ssd_y_chunk(j)
                state_update_chunk(j)
            alias_tokens(tk_ub[0:2], [tk_esg2])
            alias_tokens(tk_tmpA[0:2], [tk_E2])
            alias_tokens(tk_ynT, tk_pT + tk_ypT + tk_hst)
            def ones_mm(i):
                sb_, c, zb = i % 2, i % 4, i // 4
                S.op("pe", lambda e: e.matmul(psb[PX_][:, 0:W], lhsT=onesb, rhs=sqb[sb_], start=(c == 0), stop=(c == 3)),
                     reads=[tk_sq[sb_], tk_const], writes=[tk_ps[PX_]])
                if c == 3:
                    S.op("act", lambda e: e.activation(out=rstd[:, zb, :], in_=psb[PX_][:, 0:W], func=AF.Sqrt, bias=rmsepsc, scale=1.0 / 512.0),
                         reads=[tk_ps[PX_], tk_const], writes=[tk_rstd[zb]])
                    S.op("dve", lambda e: e.reciprocal(out=rstd[:, zb, :], in_=rstd[:, zb, :]), reads=[tk_rstd[zb]], writes=[tk_rstd[zb]])
            prev = None
            for zb in range(4):
                wb = w_get(f"Z{zb}")
                for c in range(4):
                    i = 4 * zb + c
                    bank = next_bank()
                    proj_fm(wb, 128 * c, lambda kc: cur.v[:, kc, :], 8, cur.tk, bank)
                    ta = i % 3
                    S.op("act", lambda e, bank=bank, ta=ta: e.activation(out=tmpA[ta], in_=psb[bank][:, 0:W], func=AF.Silu), reads=[tk_ps[bank]], writes=[tk_tmpA[ta]])
                    S.op("dve", lambda e, i=i, ta=ta: e.tensor_tensor(out=xs[:, i, :], in0=xs[:, i, :], in1=tmpA[ta], op=ALU.mult),
                         reads=[tk_tmpA[ta], tk_xs[i]], writes=[tk_xs[i]])
                    sb_ = i % 2
                    S.op("pool", lambda e, i=i, sb_=sb_: e.tensor_tensor(out=sqb[sb_], in0=xs[:, i, :], in1=xs[:, i, :], op=ALU.mult), reads=[tk_xs[i]], writes=[tk_sq[sb_]])
                    if prev is not None:
                        ones_mm(prev)
                    prev = i
                w_rel(wb)
            ones_mm(prev)
            for i in range(16):
                eng = "dve"
                S.op(eng, lambda e, i=i: e.scalar_tensor_tensor(out=ynT[:, i, :], in0=xs[:, i, :], scalar=PPc(136 + i), in1=rstd[:, i // 4, :], op0=ALU.mult, op1=ALU.mult),
                     reads=[tk_xs[i], tk_rstd[i // 4], tk_const], writes=[tk_ynT[i]])
            if dbgt:
                dump(dbg_yn, ynT.rearrange('p i t -> p (i t)'), tk_ynT)
            alias_tokens(tk_mrg, tk_xtok)
            wgs = None
            for q in range(4):
                wps = w_get(f"PS{q}")
                if q % 2 == 0:
                    wgs = w_get(f"GS{q // 2}")
                for c in range(2):
                    dc = 2 * q + c
                    b1 = next_bank()
                    proj_fm(wgs, 128 * (dc % 4), lambda kc: cur.v[:, kc, :], 8, cur.tk, b1)
                    b0 = next_bank()
                    proj_fm(wps, 128 * c, lambda kc: ynT[:, kc, :], 16, tk_ynT, b0)
                    ta = dc % 3
                    S.op("act", lambda e, b1=b1, ta=ta: e.activation(out=tmpA[ta], in_=psb[b1][:, 0:W], func=AF.Sigmoid), reads=[tk_ps[b1]], writes=[tk_tmpA[ta]])
                    S.op("dve", lambda e, b0=b0, ta=ta: e.tensor_tensor(out=tmpA[ta], in0=tmpA[ta], in1=psb[b0][:, 0:W], op=ALU.mult),
                         reads=[tk_ps[b0], tk_tmpA[ta]], writes=[tk_tmpA[ta]])
                    S.op("pool", lambda e, ta=ta, dc=dc: e.tensor_tensor(out=mergedT[:, dc, :], in0=tmpA[ta], in1=poolpart[:, dc, :], op=ALU.add),
                         reads=[tk_tmpA[ta], tk_pp[dc]], writes=[tk_mrg[dc]])
                w_rel(wps)
                if q % 2 == 1:
                    w_rel(wgs)
            if dbgt:
                dump(dbg_mg, mergedT.rearrange('p i t -> p (i t)'), tk_mrg)
            wo = [w_get("WO0"), w_get("WO1")]
            load_gb(1)
            for j in range(NCH):
                cs = slice(128 * j, 128 * j + 128)
                for half in range(2):
                    bank = next_bank()
                    for kc in range(8):
                        mm(psb[bank][:, 0:512], mergedT[:, kc, cs], wo[half].v[:, kc, :], kc == 0, kc == 7, [wo[half].tk, tk_mrg[kc]], bank)
                    S.op("dve", lambda e, bank=bank, half=half, j=j: e.scalar_tensor_tensor(out=hres[:, j, 512 * half:512 * half + 512], in0=hres[:, j, 512 * half:512 * half + 512],
                                                                                         scalar=ALPHA, in1=psb[bank][:, 0:512], op0=ALU.mult, op1=ALU.add),
                         reads=[tk_ps[bank], tk_hres[j]], writes=[tk_hres[j]])
                layernorm_chunk(hres, tk_hres, j)
                transpose_to(hres, tk_hres, j, cur.v, cur.tk)
            w_rel(wo[0])
            w_rel(wo[1])
            if dbgt:
                dump(dbg_h1, hres.rearrange('p j d -> p (j d)'), tk_hres)
            if gt + 1 < NT_ALL:
                alias_tokens(tk_hst, tk_ynT + tk_pT + tk_ypT)
                pf_load(gt + 1)
            alias_tokens(tk_act, tk_xs)
            pend = [None]
            for fb in range(11):
                wb = w_get(f"F{fb}")
                for c in range(2):
                    fi = 2 * fb + c
                    b0 = (0, 2, 4)[fi % 3]
                    b1 = (1, 3, 5)[fi % 3]
                    proj_fm(wb, 128 * c, lambda kc: cur.v[:, kc, :], 8, cur.tk, b0)
                    proj_fm(wb, 256 + 128 * c, lambda kc: cur.v[:, kc, :], 8, cur.tk, b1)
                    ta = fi % 3
                    pb_ = conv_chunk(b0, halo_f[:, fi, :], tk_hf[fi], 2, [160 + fi, 182 + fi, 204 + fi], 226 + fi, tokm0 if first else None, tmpA[ta], [tk_tmpA[ta]],
                                     "dve")

                    def partB2(pb_=pb_, b1=b1, ta=ta, fi=fi):
                        pb_()
                        S.op("dve", lambda e: e.tensor_tensor(out=actT[:, fi, :], in0=tmpA[ta], in1=psb[b1][:, 0:W], op=ALU.mult),
                             reads=[tk_ps[b1], tk_tmpA[ta]], writes=[tk_act[fi]])
                    if pend[0] is not None:
                        pend[0]()
                    pend[0] = partB2
                w_rel(wb)
            pend[0]()
            if dbgt:
                dump(dbg_act, actT.rearrange('p i t -> p (i t)'), tk_act)
            for q in range(4):
                wb = w_get(f"FD{q}")
                for j in range(NCH):
                    cs = slice(128 * j, 128 * j + 128)
                    bank = next_bank()
                    for kc in range(22):
                        mm(psb[bank][:, 0:256], actT[:, kc, cs], wb.v[:, kc, :], kc == 0, kc == 21, [wb.tk, tk_act[kc]], bank)
                    S.op("dve", lambda e, bank=bank, q=q, j=j: e.scalar_tensor_tensor(out=hres[:, j, 256 * q:256 * q + 256], in0=hres[:, j, 256 * q:256 * q + 256],
                                                                                   scalar=ALPHA, in1=psb[bank][:, 0:256], op0=ALU.mult, op1=ALU.add),
                         reads=[tk_ps[bank], tk_hres[j]], writes=[tk_hres[j]])
                w_rel(wb)
                if gt + 1 < NT_ALL and q < NCH:
                    pf_ln(q)
            load_gb(2)
            if gt + 1 < NT_ALL:
                pf_transposes(gt + 1)
            for j in range(NCH):
                layernorm_chunk(hres, tk_hres, j)
                ch = ti * NCH + j
                if ch >= 1:
                    r0 = (ch - 1) * 128
                    ob = j
                    S.dma("pool", lambda e, j=j, r0=r0: e.dma_start(out=y_d[r0:r0 + 128, :], in_=hres[:, j, :]), f"out{ob}", reads=[tk_hres[j]])

        print('arena words used', off[0], flush=True)
        pf_compute(0)
        pf_transposes(0)
        for ti in range(NT_PRE):
            tile_pre(ti)
        bg_flush()
        for ti in range(NT_MAIN):
            tile_main(ti)
        S.final_wait("pool", tk_hres + tk_dbg)
        S.emit(block)
    return nc


def make_pp(inp):
    pp = np.zeros((128, 256), np.float32)
    cw = np.asarray(inp["ssm_conv_w"])[0]
    cb = np.asarray(inp["ssm_conv_b"])[0]
    for k in range(4):
        pp[:, 24 * k:24 * k + 24] = cw[k].reshape(24, 128).T
    pp[:, 96:120] = cb.reshape(24, 128).T
    dsk = np.repeat(np.asarray(inp["ssm_d"])[0], 64)
    pp[:, 120:136] = dsk.reshape(16, 128).T
    pp[:, 136:152] = np.asarray(inp["ssm_norm_w"])[0].reshape(16, 128).T
    pp[:, 152:160] = np.asarray(inp["pool_scale"])[0].reshape(8, 128).T
    fw_ = np.asarray(inp["ffn_conv_w"])[0]
    for k in range(3):
        pp[:, 160 + 22 * k:160 + 22 * k + 22] = fw_[k].reshape(22, 128).T
    pp[:, 226:248] = np.asarray(inp["ffn_conv_b"])[0].reshape(22, 128).T
    pp[0:96, 248] = np.tile(np.asarray(inp["ssm_dt_bias"])[0], 3)
    pp[0:96, 249] = np.tile(np.asarray(inp["ssm_a_log"])[0], 3)
    return pp


def core_inputs(inp, b, hf, NT_PRE, NT_MAIN, shared):
    TP, TM = NT_PRE * W, NT_MAIN * W
    Q1 = TP - 128
    x = np.asarray(inp["x"])[b]
    meta = np.asarray(inp["meta_tokens"])
    seq_len = x.shape[0]

    def rows(q0, n):
        out = np.zeros((n, D), np.float32)
        msk = np.zeros((n,), np.float32)
        for lo, hi, src, s0 in ((112, 128, meta, 0), (128, 128 + seq_len, x, 0)):
            a = max(q0, lo)
            bnd = min(q0 + n, hi)
            if a < bnd:
                out[a - q0:bnd - q0] = src[a - lo:bnd - lo]
                msk[a - q0:bnd - q0] = 1.0
        return out, msk
    if hf == 0:
        xpre = np.zeros((TP, D), np.float32)
        mpre = np.zeros((TP,), np.float32)
        xmain, mmain = rows(0, TM)
    else:
        xpre, mpre = rows(Q1 - TP, TP)
        xmain, mmain = rows(Q1, TM)
    tokmask0 = np.ones((128, 16 + W), np.float32)
    tokmask0[:, 0:16] = float(hf)
    tokmask0[:, 16:] = mmain[None, 0:W]
    tokmaskp = np.zeros((128, 16 + W), np.float32)
    tokmaskp[:, 16:] = mpre[None, 0:W]
    invcnt0 = np.zeros((128, 4, W), np.float32)
    q0 = 0 if hf == 0 else Q1
    lpos = np.arange(q0, q0 + W) - 112
    for g, wdw in enumerate(POOL_WINDOWS):
        cnt = np.minimum(np.maximum(lpos + 1, 1), wdw).astype(np.float32)
        invcnt0[:, g, :] = (1.0 / cnt)[None, :]
    d = dict(shared)
    d.update({
        "xpre": xpre, "xmain": xmain,
        "mpre": np.ascontiguousarray(np.broadcast_to(mpre[None, :], (96, TP))),
        "mmain": np.ascontiguousarray(np.broadcast_to(mmain[None, :], (96, TM))),
        "tokmask0": tokmask0, "invcnt0": invcnt0.reshape(128, 4 * W), "tokmaskp": tokmaskp,
    })
    return d


def shared_inputs(inp):
    lnrows = np.stack([
        np.concatenate([np.asarray(inp["ln_in_g"]), np.asarray(inp["ln_in_b"])]),
        np.concatenate([np.asarray(inp["ln1_g"])[0], np.asarray(inp["ln1_b"])[0]]),
        np.concatenate([np.asarray(inp["ln2_g"])[0], np.asarray(inp["ln2_b"])[0]]),
    ]).astype(np.float32)
    return {
        "pp": make_pp(inp), "lnrows": lnrows,
        "w_in": np.ascontiguousarray(np.asarray(inp["w_in"])[0]),
        "w_proj_ssm": np.ascontiguousarray(np.asarray(inp["w_proj_ssm"])[0]),
        "w_proj_pool": np.ascontiguousarray(np.asarray(inp["w_proj_pool"])[0]),
        "pool_w": np.ascontiguousarray(np.asarray(inp["pool_w"])[0]),
        "w_out": np.ascontiguousarray(np.asarray(inp["w_out"])[0]),
        "ffn_w_up": np.ascontiguousarray(np.asarray(inp["ffn_w_up"])[0]),
        "ffn_w_down": np.ascontiguousarray(np.asarray(inp["ffn_w_down"])[0]),
    }


_NC_CACHE = {}


def kernel(**inputs):
    NT_PRE, NT_MAIN = 11, 11
    key = (NT_PRE, NT_MAIN)
    if key not in _NC_CACHE:
        _NC_CACHE[key] = build_program(NT_PRE, NT_MAIN)
    nc = _NC_CACHE[key]
    shared = shared_inputs(inputs)
    B = np.asarray(inputs["x"]).shape[0]
    in_maps = []
    for core in range(8):
        b, hf = core // 2, core % 2
        in_maps.append(core_inputs(inputs, b, hf, NT_PRE, NT_MAIN, shared))
    res = run_bass_kernel_spmd(nc, in_maps, core_ids=list(range(8)))
    out = np.zeros((B, 8192, D), np.float32)
    for core in range(8):
        b, hf = core // 2, core % 2
        out[b, 4096 * hf:4096 * hf + 4096] = res.results[core]["y"]
    return out
```

```python
import numpy as np
from contextlib import ExitStack
import concourse.bass as bass
import concourse.mybir as mybir
from concourse.bass_utils import run_bass_kernel_spmd

F32 = mybir.dt.float32
BF16 = mybir.dt.bfloat16
ALU = mybir.AluOpType
AF = mybir.ActivationFunctionType

D = 1024
DI = 2048
NH = 32
W = 384
NCH = 3
ALPHA = 2.0 ** 0.25
LN_EPS = 1e-5
RMS_EPS = 1e-5
POOL_WINDOWS = (2, 4, 8, 16)
DFF = 2816
SLOTW = 5632
NSLOT = 3


class Tk:
    __slots__ = ("w", "rs")

    def __init__(self):
        self.w = None
        self.rs = []


def alias_tokens(new, old):
    deps = []
    for t in old:
        if t.w is not None:
            deps.append(t.w)
        deps.extend(t.rs)
    for n in new:
        n.rs = list(n.rs) + deps


class Sync:
    ENG = ("pe", "act", "dve", "pool", "sp")

    def __init__(self):
        self.prog = {e: [] for e in self.ENG}
        self.cnt = {}
        self.seen = {e: {} for e in self.ENG}
        self.sems = {}

    def add_sem(self, key, handle):
        self.sems[key] = handle
        self.cnt[key] = 0

    def _deps(self, eng, reads, writes, pe_chain=False):
        deps = {}

        def add(d):
            if d is None:
                return
            k, v = d
            if deps.get(k, 0) < v:
                deps[k] = v
        for t in reads:
            add(t.w)
        for t in writes:
            add(t.w)
            for r in t.rs:
                add(r)
        out = []
        for k, v in deps.items():
            if k == "pe" and eng == "pe":
                continue
            if self.seen[eng].get(k, 0) < v:
                self.seen[eng][k] = v
                out.append((k, v))
        return out

    def op(self, eng, fn, reads=(), writes=()):
        waits = self._deps(eng, reads, writes)
        self.cnt[eng] += 1
        me = (eng, self.cnt[eng])
        self.prog[eng].append((waits, fn, (eng, 1)))
        for t in reads:
            t.rs.append(me)
            if len(t.rs) > 64:
                t.rs = _compact(t.rs)
        for t in writes:
            t.w = me
            t.rs = []

    def dma(self, eng, fn, semkey, reads=(), writes=()):
        waits = self._deps(eng, reads, writes)
        self.cnt[semkey] += 16
        me = (semkey, self.cnt[semkey])
        self.prog[eng].append((waits, fn, (semkey, 16)))
        for t in reads:
            t.rs.append(me)
        for t in writes:
            t.w = me
            t.rs = []

    def final_wait(self, eng, toks):
        waits = self._deps(eng, (), toks)
        self.prog[eng].append((waits, None, None))

    def emit(self, block):
        sems = self.sems

        def run(engname):
            def body(e):
                for waits, fn, inc in self.prog[engname]:
                    for k, v in waits:
                        e.wait_ge(sems[k], v)
                    if fn is not None:
                        fn(e).then_inc(sems[inc[0]], inc[1])
            return body
        block.tensor(run("pe"))
        block.scalar(run("act"))
        block.vector(run("dve"))
        block.gpsimd(run("pool"))
        block.sync(run("sp"))


def _compact(rs):
    best = {}
    for k, v in rs:
        if best.get(k, 0) < v:
            best[k] = v
    return list(best.items())


C_Z, C_X, C_B, C_C, C_DT, C_U, C_GS, C_GP = 0, 2048, 4096, 4608, 5120, 5152, 6176, 7200

BLK_NAMES = (["U0", "U1", "PW", "PP0", "PP1", "GP0", "GP1", "DT", "BC0", "BC1"]
             + [f"X{i}" for i in range(4)] + [f"Z{i}" for i in range(4)]
             + [f"PS{i}" for i in range(4)] + ["GS0", "GS1", "WO0", "WO1"]
             + [f"F{i}" for i in range(11)] + [f"FD{i}" for i in range(4)])
BLK_ID = {n: i for i, n in enumerate(BLK_NAMES)}
NBLK = len(BLK_NAMES)

MAIN_SEQ = (["U0", "U1", "PW", "PP0", "GP0", "PP1", "GP1", "DT", "BC0", "BC1", "X0", "X1", "X2", "X3",
             "Z0", "Z1", "Z2", "Z3", "PS0", "GS0", "PS1", "PS2", "GS1", "PS3", "WO0", "WO1"]
            + [f"F{i}" for i in range(11)] + [f"FD{i}" for i in range(4)])
PRE_SEQ = ["DT", "BC0", "X0", "X1", "X2", "X3"]


def build_program(NT_PRE, NT_MAIN, debug=False):
    TP = NT_PRE * W
    TM = NT_MAIN * W
    NOUT = (NT_MAIN * NCH - 1) * 128
    nc = bass.Bass("TRN2", target_bir_lowering=False)

    def din(name, shape, dt=F32):
        return nc.dram_tensor(name, list(shape), dt, kind="ExternalInput").ap()
    xpre_d = din("xpre", [TP, D])
    xmain_d = din("xmain", [TM, D])
    mpre_d = din("mpre", [96, TP])
    mmain_d = din("mmain", [96, TM])
    tm0_d = din("tokmask0", [128, 16 + W])
    ic0_d = din("invcnt0", [128, 4 * W])
    tmp_d = din("tokmaskp", [128, 16 + W])
    pp_d = din("pp", [128, 256])
    lnrow_d = din("lnrows", [3, 2 * D])
    w_in_d = din("w_in", [D, 8224])
    w_ps_d = din("w_proj_ssm", [DI, D])
    w_pp_d = din("w_proj_pool", [D, D])
    w_pw_d = din("pool_w", [4, 256, 256])
    w_o_d = din("w_out", [D, D])
    w_up_d = din("ffn_w_up", [D, 2 * DFF])
    w_dn_d = din("ffn_w_down", [DFF, D])
    y_d = nc.dram_tensor("y", [NOUT, D], F32, kind="ExternalOutput").ap()
    wsc = nc.dram_tensor("wsc", [NBLK, 128, SLOTW], BF16, kind="Internal").ap()
    if debug:
        dbg_yn = nc.dram_tensor("dbg_yn", [128, 16 * W], BF16, kind="ExternalOutput").ap()
        dbg_yp = nc.dram_tensor("dbg_yp", [128, 8 * W], BF16, kind="ExternalOutput").ap()
        dbg_pp = nc.dram_tensor("dbg_pp", [128, 8 * W], F32, kind="ExternalOutput").ap()
        dbg_mg = nc.dram_tensor("dbg_mg", [128, 8 * W], BF16, kind="ExternalOutput").ap()
        dbg_h1 = nc.dram_tensor("dbg_h1", [128, 3 * D], F32, kind="ExternalOutput").ap()
        dbg_h0 = nc.dram_tensor("dbg_h0", [128, 3 * D], F32, kind="ExternalOutput").ap()
        dbg_act = nc.dram_tensor("dbg_act", [128, 22 * W], BF16, kind="ExternalOutput").ap()
        dbg_xs = nc.dram_tensor("dbg_xs", [128, 16 * W], F32, kind="ExternalOutput").ap()
        dbg_bc = nc.dram_tensor("dbg_bc", [128, 8 * W], BF16, kind="ExternalOutput").ap()

    es = ExitStack()
    with es:
        ARENA_WORDS = 53100
        arena = es.enter_context(nc.sbuf_tensor("arena", [128, ARENA_WORDS], F32))
        psb = [es.enter_context(nc.psum_tensor(f"psb{i}", [128, 512], F32)) for i in range(8)]
        tk_ps = [Tk() for _ in range(8)]
        S = Sync()
        for e in Sync.ENG:
            S.add_sem(e, es.enter_context(nc.semaphore("s_" + e)))
        DMA_KEYS = (["wl0", "wl1", "wl2", "cvi0", "cvi1", "cvo0", "cvo1", "cvo2", "xin0", "xin1", "xin2",
                     "gb", "msk", "out0", "out1", "out2", "setup", "dbg"])
        for k in DMA_KEYS:
            S.add_sem(k, es.enter_context(nc.semaphore("s_" + k)))
        block = es.enter_context(nc.Block())

        off = [0]

        def alloc(nwords):
            o = off[0]
            off[0] += nwords
            assert off[0] <= ARENA_WORDS, off[0]
            return o

        def f32v(o, n):
            return arena[:, o:o + n]

        def bfv(o, nbf):
            return arena[:, o:o + nbf // 2].bitcast(BF16)

        o_hres = alloc(3 * D)
        hres = f32v(o_hres, 3 * D).rearrange("p (j d) -> p j d", j=3)
        tk_hres = [Tk() for _ in range(3)]
        o_gb = alloc(2 * D)
        gbt = f32v(o_gb, 2 * D)
        tk_gb = Tk()
        hTs = [bfv(alloc(8 * W // 2), 8 * W).rearrange("p (k t) -> p k t", k=8) for _ in range(2)]
        tk_hTs = [[Tk() for _ in range(3)] for _ in range(2)]

        class Cur:
            pass
        cur = Cur()
        cur.v, cur.tk = hTs[0], tk_hTs[0]
        o_RA = alloc(16 * W)
        xs = f32v(o_RA, 16 * W).rearrange("p (i t) -> p i t", i=16)
        actT = bfv(o_RA, 22 * W).rearrange("p (i t) -> p i t", i=22)
        tk_xs = [Tk() for _ in range(16)]
        tk_act = [Tk() for _ in range(22)]
        o_RB = alloc(8 * W)
        pT = bfv(o_RB, 8 * W).rearrange("p (i t) -> p i t", i=8)
        ypT = bfv(o_RB + 4 * W, 8 * W).rearrange("p (i t) -> p i t", i=8)
        ynT = bfv(o_RB, 16 * W).rearrange("p (i t) -> p i t", i=16)
        tk_pT = [Tk() for _ in range(8)]
        hstage = f32v(o_RB, 3 * D).rearrange("p (j d) -> p j d", j=3)
        tk_hst = [Tk() for _ in range(3)]
        tk_ypT = [Tk() for _ in range(8)]
        tk_ynT = [Tk() for _ in range(16)]
        o_PP = alloc(8 * W)
        poolpart = f32v(o_PP, 8 * W).rearrange("p (i t) -> p i t", i=8)
        tk_pp = [Tk() for _ in range(8)]
        stg = [f32v(o_RA, SLOTW), f32v(o_RB, SLOTW)]
        tk_stg = [Tk(), Tk()]
        o_RC = alloc(3 * DI // 2)
        xtok = bfv(o_RC, 3 * DI).rearrange("p (j c) -> p j c", j=3)
        mergedT = bfv(o_RC, 8 * W).rearrange("p (i t) -> p i t", i=8)
        tk_xtok = [Tk() for _ in range(3)]
        tk_mrg = [Tk() for _ in range(8)]
        BCT = bfv(alloc(8 * W // 2), 8 * W).rearrange("p (i t) -> p i t", i=8)
        tk_BCT = [Tk() for _ in range(8)]
        RAWW = 16 + W
        o_raw = alloc(2 * RAWW)
        raw = [f32v(o_raw, RAWW), f32v(o_raw + RAWW, RAWW)]
        tk_raw = [Tk(), Tk()]
        o_acc = alloc(2 * W)
        cacc = [f32v(o_acc, W), f32v(o_acc + W, W)]
        tk_acc = [Tk(), Tk()]
        halo_x = f32v(alloc(24 * 3), 72).rearrange("p (i k) -> p i k", i=24)
        halo_u = f32v(alloc(8 * 15), 120).rearrange("p (i k) -> p i k", i=8)
        halo_f = f32v(alloc(22 * 2), 44).rearrange("p (i k) -> p i k", i=22)
        tk_hx = [Tk() for _ in range(24)]
        tk_hu = [Tk() for _ in range(8)]
        tk_hf = [Tk() for _ in range(22)]
        xw = bfv(alloc(DI // 2), DI)
        tk_xw = Tk()
        Btok = bfv(alloc(256), 512)
        tk_Btok = Tk()
        Mt = bfv(alloc(NH * 128 // 2), NH * 128).rearrange("p (h t) -> p h t", h=NH)
        tk_Mt = [Tk() for _ in range(8)]
        esg = f32v(alloc(512), 512)
        tk_esg = Tk()
        Et = f32v(alloc(512), 512)
        tk_E = Tk()
        t1b = esg
        tk_t1 = tk_esg
        hS = f32v(alloc(DI), DI)
        tk_hS = [Tk() for _ in range(4)]
        hSb = bfv(alloc(DI // 2), DI)
        tk_hSb = [Tk() for _ in range(4)]
        o_tmp = alloc(3 * W)
        tmpA = [f32v(o_tmp + i * W, W) for i in range(3)]
        tk_tmpA = [Tk() for _ in range(3)]
        o_sq = alloc(W)
        sqb = [bfv(o_sq, W), bfv(o_sq + W // 2, W)]
        tk_sq = [Tk(), Tk()]
        rstd = f32v(alloc(4 * W), 4 * W).rearrange("p (g t) -> p g t", g=4)
        tk_rstd = [Tk() for _ in range(4)]
        UW = 16 + W
        o_u = alloc(4 * UW)
        ub = [f32v(o_u + i * UW, UW) for i in range(4)]
        tk_ub = [Tk() for _ in range(4)]
        NDT = 8
        o_dt = alloc(NDT * W)
        dtc = [f32v(o_dt + i * W, W) for i in range(NDT)]
        tk_dtc = [Tk() for _ in range(NDT)]
        S3 = bfv(alloc(W // 2), W)
        nS3 = bfv(alloc(W // 2), W)
        tk_S3, tk_nS3 = Tk(), Tk()
        wtok = f32v(alloc(3 * 32), 96).rearrange("p (j h) -> p j h", j=3)
        tk_wtok = [Tk() for _ in range(3)]
        dAt = f32v(alloc(32), 32)
        tk_dA = Tk()
        rdA = bfv(alloc(16), 32)
        tk_rdA = Tk()
        st12s = [f32v(alloc(16), 16) for _ in range(3)]
        tk_sts = [Tk() for _ in range(3)]
        mvs = [f32v(alloc(8), 8) for _ in range(3)]
        tk_mvs = [Tk() for _ in range(3)]
        dmask = f32v(alloc(W), W)
        tk_dmask = Tk()
        tokm0 = f32v(alloc(16 + W), 16 + W)
        invc0 = rstd
        ppt = f32v(alloc(256), 256)
        identf = f32v(alloc(128), 128)
        identb = bfv(alloc(64), 128)
        onesb = bfv(alloc(64), 128)
        SEL = bfv(alloc(16), 32)
        maskneg = bfv(alloc(64), 128)
        aneg = f32v(alloc(2), 2)
        tk_const = Tk()
        o_slot = alloc(NSLOT * SLOTW // 2)
        slots = [bfv(o_slot + i * SLOTW // 2, SLOTW) for i in range(NSLOT)]
        tk_slot = [Tk() for _ in range(NSLOT)]
        tk_blk = [Tk() for _ in range(NBLK)]

        def PPc(c):
            return ppt[:, c:c + 1]

        tokmp = f32v(alloc(16 + W), 16 + W)
        epst = f32v(alloc(2), 2)
        epsc = epst[:, 0:1]
        rmsepsc = epst[:, 1:2]
        tk_cdma = Tk()
        tk_k = Tk()
        S.dma("sp", lambda e: e.dma_start(out=ppt, in_=pp_d[:, :]), "setup", writes=[tk_cdma])
        S.dma("sp", lambda e: e.dma_start(out=tokm0, in_=tm0_d[:, :]), "setup", writes=[tk_cdma])
        S.dma("sp", lambda e: e.dma_start(out=tokmp, in_=tmp_d[:, :]), "setup", writes=[tk_cdma])
        S.dma("sp", lambda e: e.dma_start(out=invc0.rearrange("p g t -> p (g t)"), in_=ic0_d[:, :]), "setup", writes=[tk_cdma])
        tk_cdma.w = ("setup", S.cnt["setup"])
        for t_ in tk_rstd:
            t_.w = tk_cdma.w
        S.op("pool", lambda e: e.memset(identf, 1.0), writes=[tk_k])
        S.op("pool", lambda e: e.affine_select(out=identf, in_=identf, pattern=[[1, 128]], compare_op=ALU.is_equal, fill=0.0, base=0, channel_multiplier=-1), reads=[tk_k], writes=[tk_k])
        S.op("pool", lambda e: e.tensor_copy(out=identb, in_=identf), reads=[tk_k], writes=[tk_k])
        S.op("pool", lambda e: e.memset(onesb, 1.0), writes=[tk_k])
        S.op("pool", lambda e: e.memset(epst[:, 0:1], LN_EPS), writes=[tk_k])
        S.op("pool", lambda e: e.memset(epst[:, 1:2], RMS_EPS), writes=[tk_k])
        S.op("pool", lambda e: e.memset(maskneg, 0.0), writes=[tk_k])
        S.op("pool", lambda e: e.affine_select(out=maskneg, in_=maskneg, pattern=[[1, 128]], compare_op=ALU.is_ge, fill=-30000.0, base=0, channel_multiplier=-1), reads=[tk_k], writes=[tk_k])
        S.op("pool", lambda e: e.memset(SEL, 0.0), writes=[tk_k])
        for r in range(3):
            S.op("pool", lambda e, r=r: e.tensor_copy(out=SEL[32 * r:32 * r + 32, :], in_=identf[32 * r:32 * r + 32, 32 * r:32 * r + 32]), reads=[tk_k], writes=[tk_k])
        S.op("pool", lambda e: e.memset(hS, 0.0), writes=tk_hS)
        S.op("pool", lambda e: e.memset(hSb, 0.0), writes=tk_hSb)
        S.op("pool", lambda e: e.memset(halo_x.rearrange("p i k -> p (i k)"), 0.0), writes=tk_hx)
        S.op("pool", lambda e: e.memset(halo_u.rearrange("p i k -> p (i k)"), 0.0), writes=tk_hu)
        S.op("pool", lambda e: e.memset(halo_f.rearrange("p i k -> p (i k)"), 0.0), writes=tk_hf)
        S.op("act", lambda e: e.activation(out=aneg[:, 0:1], in_=PPc(249), func=AF.Exp), reads=[tk_cdma], writes=[tk_k])
        S.op("dve", lambda e: e.tensor_scalar(out=aneg[:, 1:2], in0=aneg[:, 0:1], scalar1=-1.0, scalar2=None, op0=ALU.mult), reads=[tk_k], writes=[tk_k])
        S.op("pool", lambda e: e.memset(aneg[:, 0:1], 0.0), reads=[tk_cdma, tk_k], writes=[tk_const])

        def kc_view(ap2d, kc):
            return ap2d.rearrange("(k p) c -> p k c", p=128)

        def conv_sources(name):
            if name in ("U0", "U1"):
                c0 = C_U + 512 * int(name[1])
                return 8, 512, [((0, 512), kc_view(w_in_d[:, c0:c0 + 512], 8))]
            if name in ("GP0", "GP1"):
                c0 = C_GP + 512 * int(name[2])
                return 8, 512, [((0, 512), kc_view(w_in_d[:, c0:c0 + 512], 8))]
            if name in ("GS0", "GS1"):
                c0 = C_GS + 512 * int(name[2])
                return 8, 512, [((0, 512), kc_view(w_in_d[:, c0:c0 + 512], 8))]
            if name in ("BC0", "BC1"):
                c0 = C_B + 512 * int(name[2])
                return 8, 512, [((0, 512), kc_view(w_in_d[:, c0:c0 + 512], 8))]
            if name[0] == "X":
                c0 = C_X + 512 * int(name[1])
                return 8, 512, [((0, 512), kc_view(w_in_d[:, c0:c0 + 512], 8))]
            if name[0] == "Z":
                c0 = C_Z + 512 * int(name[1])
                return 8, 512, [((0, 512), kc_view(w_in_d[:, c0:c0 + 512], 8))]
            if name == "DT":
                return 8, 96, [((32 * r, 32 * r + 32), kc_view(w_in_d[:, C_DT:C_DT + 32], 8)) for r in range(3)]
            if name[:2] == "PS":
                q = int(name[2])
                return 16, 256, [((0, 256), kc_view(w_ps_d[:, 256 * q:256 * q + 256], 16))]
            if name[:2] == "PP":
                q = int(name[2])
                return 8, 512, [((0, 512), kc_view(w_pp_d[:, 512 * q:512 * q + 512], 8))]
            if name == "PW":
                return 8, 256, [((0, 256), w_pw_d.rearrange("g (k p) c -> p (g k) c", p=128))]
            if name[:2] == "WO":
                q = int(name[2])
                return 8, 512, [((0, 512), kc_view(w_o_d[:, 512 * q:512 * q + 512], 8))]
            if name[:2] == "FD":
                q = int(name[2])
                return 22, 256, [((0, 256), kc_view(w_dn_d[:, 256 * q:256 * q + 256], 22))]
            if name[0] == "F":
                fb = int(name[1:])
                return 8, 512, [((0, 256), kc_view(w_up_d[:, 256 * fb:256 * fb + 256], 8)),
                                ((256, 512), kc_view(w_up_d[:, DFF + 256 * fb:DFF + 256 * fb + 256], 8))]
            raise KeyError(name)

        BLK_SHAPE = {}
        cast_engs = ["dve", "pool", "act"]
        for bi, name in enumerate(BLK_NAMES):
            KC, CB, srcs = conv_sources(name)
            BLK_SHAPE[name] = (KC, CB)
            n = KC * CB
            si = bi % 2
            sl = bi % NSLOT
            sview = stg[si][:, 0:n].rearrange("p (k c) -> p k c", k=KC)
            for (c0, c1), src in srcs:
                S.dma("sp", lambda e, sview=sview, c0=c0, c1=c1, src=src: e.dma_start(out=sview[:, :, c0:c1], in_=src),
                      f"cvi{si}", writes=[tk_stg[si]])
            ce = cast_engs[bi % 3]
            if ce == "act":
                S.op("act", lambda e, sl=sl, si=si, n=n: e.activation(out=slots[sl][:, 0:n], in_=stg[si][:, 0:n], func=AF.Copy),
                     reads=[tk_stg[si]], writes=[tk_slot[sl]])
            else:
                S.op(ce, lambda e, sl=sl, si=si, n=n: e.tensor_copy(out=slots[sl][:, 0:n], in_=stg[si][:, 0:n]),
                     reads=[tk_stg[si]], writes=[tk_slot[sl]])
            S.dma("act", lambda e, sl=sl, bi=bi, n=n: e.dma_start(out=wsc[bi, :, 0:n], in_=slots[sl][:, 0:n]),
                  f"cvo{sl}", reads=[tk_slot[sl]], writes=[tk_blk[bi]])
        alias_tokens(tk_xs + tk_act, [tk_stg[0]])
        alias_tokens(tk_pT + tk_ypT + tk_ynT + tk_pp + tk_hst, [tk_stg[1]])

        seq = []
        for _ in range(NT_PRE):
            seq += PRE_SEQ
        for _ in range(NT_MAIN):
            seq += MAIN_SEQ
        wst = {"issued": 0, "next": 0}

        def w_issue():
            k = wst["issued"]
            if k >= len(seq):
                return
            name = seq[k]
            bi = BLK_ID[name]
            KC, CB = BLK_SHAPE[name]
            n = KC * CB
            sl = k % NSLOT
            S.dma("sp", lambda e, sl=sl, bi=bi, n=n: e.dma_start(out=slots[sl][:, 0:n], in_=wsc[bi, :, 0:n]),
                  f"wl{sl}", reads=[tk_blk[bi]], writes=[tk_slot[sl]])
            wst["issued"] += 1

        class WB:
            pass

        def w_get(name):
            k = wst["next"]
            assert seq[k] == name, (k, seq[k], name)
            assert k < wst["issued"], (k, wst["issued"])
            wst["next"] += 1
            KC, CB = BLK_SHAPE[name]
            b = WB()
            b.k = k
            b.tk = tk_slot[k % NSLOT]
            b.v = slots[k % NSLOT][:, 0:KC * CB].rearrange("p (k c) -> p k c", k=KC)
            return b

        def w_rel(b):
            w_issue()

        for _ in range(NSLOT):
            w_issue()

        bank_rr = [0]

        def next_bank():
            b = bank_rr[0]
            bank_rr[0] ^= 1
            return b
        PT_, PS_, PG_, PY_, PO_, PX_ = 2, 3, 4, 5, 6, 7

        def mm(out_ap, lhsT, rhs, start, stop, reads, bank, **kw):
            S.op("pe", lambda e: e.matmul(out_ap, lhsT=lhsT, rhs=rhs, start=start, stop=stop, **kw),
                 reads=reads, writes=[tk_ps[bank]])

        def proj_fm(wb, col0, rhs_fn, KC, rhs_tks, bank, ncols=128, wcols=W):
            for kc in range(KC):
                mm(psb[bank][0:ncols, 0:wcols], wb.v[:, kc, col0:col0 + ncols], rhs_fn(kc), kc == 0, kc == KC - 1,
                   [wb.tk] + rhs_tks, bank)

        def layernorm_chunk(buf, tks, j):
            hj = buf[:, j, :]
            st12, tk_st, mv, tk_mv = st12s[j], tk_sts[j], mvs[j], tk_mvs[j]
            for h2 in range(2):
                S.op("dve", lambda e, h2=h2: e.bn_stats(out=st12[:, 6 * h2:6 * h2 + 6], in_=buf[:, j, 512 * h2:512 * h2 + 512]),
                     reads=[tks[j]], writes=[tk_st])
            S.op("dve", lambda e: e.bn_aggr(out=mv[:, 0:2], in_=st12[:, 0:12]),
                 reads=[tk_st], writes=[tk_mv])
            S.op("act", lambda e: e.activation(out=mv[:, 2:3], in_=mv[:, 1:2], func=AF.Sqrt, bias=epsc, scale=1.0),
                 reads=[tk_mv, tk_const], writes=[tk_mv])
            S.op("dve", lambda e: e.reciprocal(out=mv[:, 3:4], in_=mv[:, 2:3]), reads=[tk_mv], writes=[tk_mv])
            S.op("dve", lambda e: e.tensor_scalar(out=hj, in0=hj, scalar1=mv[:, 0:1], scalar2=mv[:, 3:4], op0=ALU.subtract, op1=ALU.mult),
                 reads=[tk_mv, tks[j]], writes=[tks[j]])
            S.op("dve", lambda e: e.tensor_tensor(out=hj, in0=hj, in1=gbt[:, 0:D], op=ALU.mult),
                 reads=[tk_gb, tks[j]], writes=[tks[j]])
            S.op("pool", lambda e: e.tensor_tensor(out=hj, in0=hj, in1=gbt[:, D:2 * D], op=ALU.add),
                 reads=[tk_gb, tks[j]], writes=[tks[j]])

        gb_cur = [None]

        def load_gb(row):
            if gb_cur[0] == row:
                return
            gb_cur[0] = row
            S.dma("pool", lambda e: e.dma_start(out=gbt, in_=lnrow_d[row:row + 1, :].partition_broadcast(128)),
                  "gb", writes=[tk_gb])

        def transpose_to(buf, tks, j, dst, dst_tks):
            for half in range(2):
                for k4 in range(4):
                    kc = half * 4 + k4
                    S.op("pe", lambda e, kc=kc, k4=k4: e.transpose(out=psb[PT_][:, 128 * k4:128 * k4 + 128], in_=buf[:, j, 128 * kc:128 * kc + 128], identity=identf),
                         reads=[tks[j], tk_const], writes=[tk_ps[PT_]])
                S.op("act", lambda e, half=half: e.activation(out=dst[:, 4 * half:4 * half + 4, 128 * j:128 * j + 128],
                                                              in_=psb[PT_][:, 0:512].rearrange("p (k t) -> p k t", k=4), func=AF.Copy),
                     reads=[tk_ps[PT_]], writes=[dst_tks[j]])

        NT_ALL = NT_PRE + NT_MAIN

        def pf_load(gt):
            if gt < NT_PRE:
                xd, row0 = xpre_d, gt * W
            else:
                xd, row0 = xmain_d, (gt - NT_PRE) * W
            for j in range(NCH):
                r0 = row0 + 128 * j
                S.dma("pool", lambda e, j=j, r0=r0: e.dma_start(out=hstage[:, j, :], in_=xd[r0:r0 + 128, :]), f"xin{j}", writes=[tk_hst[j]])

        def pf_ln(j):
            load_gb(0)
            layernorm_chunk(hstage, tk_hst, j)

        def pf_compute(gt):
            pf_load(gt)
            for j in range(NCH):
                pf_ln(j)

        def pf_transposes(gt):
            for j in range(NCH):
                transpose_to(hstage, tk_hst, j, hTs[gt % 2], tk_hTs[gt % 2])

        def conv_chunk(bank, halo, tk_h, hk, wcols, bcol, first, silu_out, silu_tks, conv_eng):
            ri = conv_chunk.rr
            conv_chunk.rr ^= 1
            rw = raw[ri]
            ac = cacc[ri]
            S.op("pool", lambda e: e.tensor_copy(out=rw[:, 16 - hk:16], in_=halo), reads=[tk_h], writes=[tk_raw[ri]])
            S.op("act", lambda e: e.activation(out=rw[:, 16:16 + W], in_=psb[bank][:, 0:W], func=AF.Copy), reads=[tk_ps[bank]], writes=[tk_raw[ri]])
            if first is not None and first is not False:
                mk = first
                S.op("pool", lambda e: e.tensor_tensor(out=rw[:, 16 - hk:16 + W], in0=rw[:, 16 - hk:16 + W], in1=mk[:, 16 - hk:16 + W], op=ALU.mult),
                     reads=[tk_raw[ri], tk_const], writes=[tk_raw[ri]])
            S.op("pool", lambda e: e.tensor_copy(out=halo, in_=rw[:, 16 + W - hk:16 + W]), reads=[tk_raw[ri]], writes=[tk_h])
            if first is not None and first is not False:
                S.op(conv_eng, lambda e: e.tensor_scalar(out=ac, in0=rw[:, 16:16 + W], scalar1=PPc(wcols[hk]), scalar2=PPc(bcol), op0=ALU.mult, op1=ALU.add),
                     reads=[tk_raw[ri], tk_const], writes=[tk_acc[ri]])
            else:
                S.op("act", lambda e: e.activation(out=ac, in_=psb[bank][:, 0:W], func=AF.Identity, scale=PPc(wcols[hk]), bias=PPc(bcol)),
                     reads=[tk_ps[bank], tk_const], writes=[tk_acc[ri]])
            for k in range(hk):
                sh = hk - k
                S.op(conv_eng, lambda e, k=k, sh=sh: e.scalar_tensor_tensor(out=ac, in0=rw[:, 16 - sh:16 - sh + W], scalar=PPc(wcols[k]), in1=ac, op0=ALU.mult, op1=ALU.add),
                     reads=[tk_raw[ri], tk_acc[ri], tk_const], writes=[tk_acc[ri]])
            def partB():
                S.op("act", lambda e: e.activation(out=silu_out, in_=ac, func=AF.Silu), reads=[tk_acc[ri]], writes=silu_tks)
            return partB
        conv_chunk.rr = 0

        def dt_phase(md, col0, masked):
            wb = w_get("DT")
            proj_fm(wb, 0, lambda kc: cur.v[:, kc, :], 8, cur.tk, PX_, ncols=96)
            w_rel(wb)
            P = slice(0, 96)
            e_, dt_, dta_, acum_, lnd_, q_, t0_, t1_ = [d[P, :] for d in dtc]
            S.op("act", lambda e: e.activation(out=e_, in_=psb[PX_][0:96, 0:W], func=AF.Exp, bias=ppt[0:96, 248:249], scale=1.0),
                 reads=[tk_ps[PX_], tk_const], writes=[tk_dtc[0]])
            S.op("act", lambda e: e.activation(out=dt_, in_=e_, func=AF.Ln, bias=1.0), reads=[tk_dtc[0]], writes=[tk_dtc[1]])
            if masked:
                S.dma("pool", lambda e: e.dma_start(out=dmask[0:96, :], in_=md[:, col0:col0 + W]), "msk", writes=[tk_dmask])
                S.op("dve", lambda e: e.tensor_scalar(out=t0_, in0=dmask[0:96, :], scalar1=-1.0, scalar2=1.0, op0=ALU.mult, op1=ALU.add),
                     reads=[tk_dmask], writes=[tk_dtc[6]])
                S.op("dve", lambda e: e.tensor_tensor(out=dt_, in0=dt_, in1=dmask[0:96, :], op=ALU.mult), reads=[tk_dmask, tk_dtc[1]], writes=[tk_dtc[1]])
                S.op("dve", lambda e: e.tensor_tensor(out=t1_, in0=dt_, in1=t0_, op=ALU.add), reads=[tk_dtc[1], tk_dtc[6]], writes=[tk_dtc[7]])
                S.op("act", lambda e: e.activation(out=lnd_, in_=t1_, func=AF.Ln), reads=[tk_dtc[7]], writes=[tk_dtc[4]])
                S.op("dve", lambda e: e.scalar_tensor_tensor(out=lnd_, in0=t0_, scalar=-200.0, in1=lnd_, op0=ALU.mult, op1=ALU.add),
                     reads=[tk_dtc[6], tk_dtc[4]], writes=[tk_dtc[4]])
            else:
                S.op("act", lambda e: e.activation(out=lnd_, in_=dt_, func=AF.Ln), reads=[tk_dtc[1]], writes=[tk_dtc[4]])
            S.op("dve", lambda e: e.tensor_scalar(out=dta_, in0=dt_, scalar1=aneg[0:96, 1:2], scalar2=None, op0=ALU.mult),
                 reads=[tk_dtc[1], tk_const], writes=[tk_dtc[2]])
            S.op("pool", lambda e: e.memset(t1_, 1.0), reads=[], writes=[tk_dtc[7]])
            for j in range(NCH):
                cs = slice(128 * j, 128 * j + 128)
                S.op("dve", lambda e, cs=cs: e.tensor_tensor_scan(out=acum_[:, cs], data0=t1_[:, cs], data1=dta_[:, cs], initial=0.0, op0=ALU.mult, op1=ALU.add),
                     reads=[tk_dtc[2], tk_dtc[7]], writes=[tk_dtc[3]])
            S.op("dve", lambda e: e.tensor_tensor(out=q_, in0=lnd_, in1=acum_, op=ALU.subtract), reads=[tk_dtc[4], tk_dtc[3]], writes=[tk_dtc[5]])

            def split3(src, tk_src, dst, tk_dst):
                hi = dtc[6]
                r1 = dtc[7]
                mid = dtc[0]
                S.op("dve", lambda e: e.tensor_copy(out=dst[0:32, :], in_=src[0:32, :]), reads=[tk_src], writes=[tk_dst])
                S.op("dve", lambda e: e.tensor_copy(out=mid[0:96, 0:W // 2].bitcast(BF16), in_=src[0:96, :]), reads=[tk_src], writes=[tk_dtc[0]])
                S.op("dve", lambda e: e.tensor_tensor(out=r1[0:96, :], in0=src[0:96, :], in1=mid[0:96, 0:W // 2].bitcast(BF16), op=ALU.subtract),
                     reads=[tk_src, tk_dtc[0]], writes=[tk_dtc[7]])
                S.op("dve", lambda e: e.tensor_copy(out=dst[32:64, :], in_=r1[32:64, :]), reads=[tk_dtc[7]], writes=[tk_dst])
                S.op("dve", lambda e: e.tensor_copy(out=hi[64:96, 0:W // 2].bitcast(BF16), in_=r1[64:96, :]), reads=[tk_dtc[7]], writes=[tk_dtc[6]])
                S.op("dve", lambda e: e.tensor_tensor(out=r1[64:96, :], in0=r1[64:96, :], in1=hi[64:96, 0:W // 2].bitcast(BF16), op=ALU.subtract),
                     reads=[tk_dtc[6], tk_dtc[7]], writes=[tk_dtc[7]])
                S.op("dve", lambda e: e.tensor_copy(out=dst[64:96, :], in_=r1[64:96, :]), reads=[tk_dtc[7]], writes=[tk_dst])
            split3(dtc[3], tk_dtc[3], S3, tk_S3)
            split3(dtc[5], tk_dtc[5], nS3, tk_nS3)
            for j in range(NCH):
                cs = slice(128 * j, 128 * j + 128)
                S.op("act", lambda e, cs=cs, j=j: e.activation(out=dtc[1][0:32, cs], in_=dtc[5][0:32, cs], func=AF.Exp,
                                                               bias=dtc[3][0:32, 128 * j + 127:128 * j + 128], scale=1.0),
                     reads=[tk_dtc[5], tk_dtc[3]], writes=[tk_dtc[1]])
                S.op("pe", lambda e, cs=cs, j=j: e.transpose(out=psb[PX_][:, 128 + 32 * j:160 + 32 * j], in_=dtc[1][0:32, cs], identity=identf[0:32, 0:32]),
                     reads=[tk_dtc[1], tk_const], writes=[tk_ps[PX_]])
            S.op("act", lambda e: e.activation(out=wtok.rearrange("p j h -> p (j h)"), in_=psb[PX_][:, 128:224], func=AF.Copy),
                 reads=[tk_ps[PX_]], writes=tk_wtok)

        def xbc_phase(first, with_c):
            names = ["BC0"] + (["BC1"] if with_c else []) + ["X0", "X1", "X2", "X3"]
            pend = [None]
            for name in names:
                wb = w_get(name)
                for c in range(4):
                    if name[0] == "B":
                        idx = 16 + 4 * int(name[2]) + c
                        out_ap, otk = BCT[:, idx - 16, :], [tk_BCT[idx - 16]]
                    else:
                        idx = 4 * int(name[1]) + c
                        out_ap, otk = xs[:, idx, :], [tk_xs[idx]]
                    bank = next_bank()
                    proj_fm(wb, 128 * c, lambda kc: cur.v[:, kc, :], 8, cur.tk, bank)
                    pb_ = conv_chunk(bank, halo_x[:, idx, :], tk_hx[idx], 3, [0 + idx, 24 + idx, 48 + idx, 72 + idx], 96 + idx,
                                     first, out_ap, otk, "dve")
                    if pend[0] is not None:
                        pend[0]()
                    pend[0] = pb_
                w_rel(wb)
            if pend[0] is not None:
                pend[0]()

        def state_prep_chunk(j):
            cs = slice(128 * j, 128 * j + 128)
            pTb = psb[PG_][:, 0:256].bitcast(BF16)
            for g in range(4):
                S.op("pe", lambda e, g=g: e.transpose(out=pTb[:, 128 * g:128 * g + 128], in_=BCT[:, g, cs], identity=identb),
                     reads=[tk_BCT[g], tk_const], writes=[tk_ps[PG_]])
            S.op("act", lambda e: e.activation(out=Btok, in_=pTb, func=AF.Copy), reads=[tk_ps[PG_]], writes=[tk_Btok])
            for q4 in range(4):
                tb = (PT_, PY_)[q4 % 2]
                for ii in range(4):
                    i = 4 * q4 + ii
                    S.op("pe", lambda e, i=i, ii=ii, tb=tb: e.transpose(out=psb[tb][:, 128 * ii:128 * ii + 128], in_=xs[:, i, cs], identity=identf),
                         reads=[tk_xs[i], tk_const], writes=[tk_ps[tb]])
                S.op("act", lambda e, q4=q4, tb=tb: e.activation(out=xtok[:, j, 512 * q4:512 * q4 + 512], in_=psb[tb][:, 0:512], func=AF.Copy),
                     reads=[tk_ps[tb]], writes=[tk_xtok[j]])
            S.op("dve", lambda e: e.tensor_tensor(out=xw.rearrange("p (h c) -> p h c", h=NH), in0=xtok[:, j, :].rearrange("p (h c) -> p h c", h=NH),
                                                  in1=wtok[:, j, :].unsqueeze(2).to_broadcast([128, NH, 64]), op=ALU.mult),
                 reads=[tk_xtok[j], tk_wtok[j]], writes=[tk_xw])

        def state_update_chunk(j):
            S.op("dve", lambda e: e.tensor_scalar(out=rdA[0:96, :], in0=SEL[0:96, :], scalar1=S3[0:96, 128 * j + 127:128 * j + 128], scalar2=None, op0=ALU.mult),
                 reads=[tk_S3, tk_const], writes=[tk_rdA])
            S.op("pe", lambda e: e.matmul(psb[PX_][:, 256:288], lhsT=onesb[0:96, :], rhs=rdA[0:96, :], start=True, stop=True),
                 reads=[tk_rdA, tk_const], writes=[tk_ps[PX_]])
            S.op("act", lambda e: e.activation(out=dAt, in_=psb[PX_][:, 256:288], func=AF.Exp), reads=[tk_ps[PX_]], writes=[tk_dA])
            for g in range(4):
                gs = slice(512 * g, 512 * g + 512)
                S.op("dve", lambda e, g=g, gs=gs: e.tensor_tensor(out=hS[:, gs].rearrange("p (h c) -> p h c", h=8), in0=hS[:, gs].rearrange("p (h c) -> p h c", h=8),
                                                                 in1=dAt[:, 8 * g:8 * g + 8].unsqueeze(2).to_broadcast([128, 8, 64]), op=ALU.mult),
                     reads=[tk_dA, tk_hS[g]], writes=[tk_hS[g]])
            for g in range(4):
                gs = slice(512 * g, 512 * g + 512)
                bk = (PO_, 0)[g % 2]
                S.op("pe", lambda e, g=g, gs=gs, bk=bk: e.matmul(psb[bk][:, 0:512], lhsT=Btok[:, 128 * g:128 * g + 128], rhs=xw[:, gs], start=True, stop=True),
                     reads=[tk_Btok, tk_xw], writes=[tk_ps[bk]])
                S.op("dve", lambda e, gs=gs, bk=bk: e.tensor_tensor(out=hS[:, gs], in0=hS[:, gs], in1=psb[bk][:, 0:512], op=ALU.add),
                     reads=[tk_ps[bk], tk_hS[g]], writes=[tk_hS[g]])
                S.op("act", lambda e, gs=gs: e.activation(out=hSb[:, gs], in_=hS[:, gs], func=AF.Copy), reads=[tk_hS[g]], writes=[tk_hSb[g]])

        esg2 = f32v(o_u, 512)
        tk_esg2 = Tk()
        Et2 = f32v(o_tmp, 512)
        tk_E2 = Tk()
        esgs, tk_esgs = [esg, esg2], [tk_esg, tk_esg2]
        Ets, tk_Es = [Et, Et2], [tk_E, tk_E2]

        def ssd_y_chunk(j):
            cs = slice(128 * j, 128 * j + 128)
            for g in range(4):
                S.op("pe", lambda e, g=g: e.matmul(psb[PG_][:, 128 * g:128 * g + 128], lhsT=BCT[:, g, cs], rhs=BCT[:, 4 + g, cs], start=True, stop=True),
                     reads=[tk_BCT[g], tk_BCT[4 + g]], writes=[tk_ps[PG_]])
            for hq in range(8):
                psk = (PS_, 0)[hq % 2]
                eb, tke = esgs[hq % 2], tk_esgs[hq % 2]
                for hh in range(4):
                    h = 4 * hq + hh
                    o = psb[psk][:, 128 * hh:128 * hh + 128]
                    S.op("pe", lambda e, o=o: e.matmul(o, lhsT=identb, rhs=maskneg, start=True, stop=False),
                         reads=[tk_const], writes=[tk_ps[psk]])
                    S.op("pe", lambda e, o=o, h=h: e.matmul(o, lhsT=SEL[0:96, h:h + 1].to_broadcast([96, 128]), rhs=S3[0:96, cs], start=False, stop=False),
                         reads=[tk_S3, tk_const], writes=[tk_ps[psk]])
                    S.op("pe", lambda e, o=o, h=h: e.matmul(o, lhsT=nS3[0:96, cs], rhs=SEL[0:96, h:h + 1].to_broadcast([96, 128]), start=False, stop=True),
                         reads=[tk_nS3, tk_const], writes=[tk_ps[psk]])
                S.op("act", lambda e, psk=psk, eb=eb: e.activation(out=eb, in_=psb[psk][:, 0:512], func=AF.Exp), reads=[tk_ps[psk]], writes=[tke])
                g = hq // 2
                S.op("dve", lambda e, hq=hq, g=g, eb=eb: e.tensor_tensor(out=Mt[:, 4 * hq:4 * hq + 4, :], in0=eb.rearrange("p (h t) -> p h t", h=4),
                                                                        in1=psb[PG_][:, 128 * g:128 * g + 128].unsqueeze(1).to_broadcast([128, 4, 128]), op=ALU.mult),
                     reads=[tke, tk_ps[PG_]], writes=[tk_Mt[hq]])
            for pq in range(4):
                g = pq
                pxk = (PX_, 1)[pq % 2]
                Eb, tkE = Ets[pq % 2], tk_Es[pq % 2]
                for ii in range(4):
                    i = 4 * pq + ii
                    for hh in range(2):
                        h = 2 * i + hh
                        S.op("pe", lambda e, ii=ii, hh=hh, h=h, pxk=pxk: e.matmul(psb[pxk][64 * hh:64 * hh + 64, 128 * ii:128 * ii + 128], lhsT=SEL[0:96, h:h + 1].to_broadcast([96, 64]),
                                                                                  rhs=S3[0:96, cs], start=True, stop=True, tile_position=(0, 64 * hh)),
                             reads=[tk_S3, tk_const], writes=[tk_ps[pxk]])
                S.op("act", lambda e, pxk=pxk, Eb=Eb: e.activation(out=Eb, in_=psb[pxk][:, 0:512], func=AF.Exp), reads=[tk_ps[pxk]], writes=[tkE])
                for ii in range(4):
                    i = 4 * pq + ii
                    S.op("pe", lambda e, ii=ii, i=i, g=g: e.matmul(psb[PO_][:, 128 * ii:128 * ii + 128], lhsT=hSb[:, 128 * i:128 * i + 128], rhs=BCT[:, 4 + g, cs], start=True, stop=True),
                         reads=[tk_hSb[g], tk_BCT[4 + g]], writes=[tk_ps[PO_]])
                for ii in range(4):
                    i = 4 * pq + ii
                    for hh in range(2):
                        h = 2 * i + hh
                        S.op("pe", lambda e, ii=ii, hh=hh, h=h: e.matmul(psb[PY_][64 * hh:64 * hh + 64, 128 * ii:128 * ii + 128], lhsT=xtok[:, j, 64 * h:64 * h + 64],
                                                                         rhs=Mt[:, h, :], start=True, stop=True, tile_position=(0, 64 * hh)),
                             reads=[tk_xtok[j], tk_Mt[h // 4]], writes=[tk_ps[PY_]])
                S.op("dve", lambda e, Eb=Eb: e.tensor_tensor(out=t1b, in0=Eb, in1=psb[PO_][:, 0:512], op=ALU.mult), reads=[tkE, tk_ps[PO_]], writes=[tk_t1])
                for ii in range(4):
                    i = 4 * pq + ii
                    S.op("dve", lambda e, ii=ii, i=i: e.scalar_tensor_tensor(out=xs[:, i, cs], in0=xs[:, i, cs], scalar=PPc(120 + i), in1=t1b[:, 128 * ii:128 * ii + 128],
                                                                             op0=ALU.mult, op1=ALU.add),
                         reads=[tk_t1, tk_xs[i], tk_const], writes=[tk_xs[i]])
                S.op("dve", lambda e, pq=pq: e.tensor_tensor(out=xs[:, 4 * pq:4 * pq + 4, cs], in0=xs[:, 4 * pq:4 * pq + 4, cs],
                                                             in1=psb[PY_][:, 0:512].rearrange("p (i t) -> p i t", i=4), op=ALU.add),
                     reads=[tk_ps[PY_]] + tk_xs[4 * pq:4 * pq + 4], writes=tk_xs[4 * pq:4 * pq + 4])

        tk_dbg = []

        def dump(dst, src_ap, toks):
            t = Tk()
            tk_dbg.append(t)
            S.dma("pool", lambda e: e.dma_start(out=dst[:, :], in_=src_ap), "dbg", reads=toks, writes=[t])

        def tile_pre(ti):
            gt = ti
            cur.v, cur.tk = hTs[gt % 2], tk_hTs[gt % 2]
            dt_phase(mpre_d, ti * W, True)
            xbc_phase(tokmp if ti == 0 else None, False)
            if gt + 1 < NT_ALL:
                pf_load(gt + 1)
            for j in range(NCH):
                state_prep_chunk(j)
                state_update_chunk(j)
                if gt + 1 < NT_ALL:
                    pf_ln(j)
            if gt + 1 < NT_ALL:
                pf_transposes(gt + 1)

        def tile_main(ti):
            first = (ti == 0)
            gt = NT_PRE + ti
            cur.v, cur.tk = hTs[gt % 2], tk_hTs[gt % 2]
            for j in range(NCH):
                S.op("act", lambda e, j=j: e.activation(out=hres[:, j, :], in_=hstage[:, j, :], func=AF.Copy), reads=[tk_hst[j]], writes=[tk_hres[j]])
            dbgt = debug and ti == NT_MAIN - 1
            if dbgt:
                dump(dbg_h0, hres.rearrange('p j d -> p (j d)'), tk_hres)
            alias_tokens(tk_pT + tk_ypT, tk_ynT + tk_hst)
            for ub_i in range(2):
                wb = w_get(f"U{ub_i}")
                for c in range(4):
                    uc = 4 * ub_i + c
                    wwin = POOL_WINDOWS[uc // 2]
                    bank = next_bank()
                    proj_fm(wb, 128 * c, lambda kc: cur.v[:, kc, :], 8, cur.tk, bank)
                    u0i = 0 if uc % 2 == 0 else 3
                    u0 = ub[u0i]
                    S.op("pool", lambda e, u0=u0, uc=uc: e.tensor_copy(out=u0[:, 1:16], in_=halo_u[:, uc, :]), reads=[tk_hu[uc]], writes=[tk_ub[u0i]])
                    S.op("act", lambda e, u0=u0, bank=bank: e.activation(out=u0[:, 16:16 + W], in_=psb[bank][:, 0:W], func=AF.Copy), reads=[tk_ps[bank]], writes=[tk_ub[u0i]])
                    if first:
                        S.op("pool", lambda e, u0=u0: e.tensor_tensor(out=u0[:, 1:16 + W], in0=u0[:, 1:16 + W], in1=tokm0[:, 1:16 + W], op=ALU.mult),
                             reads=[tk_ub[u0i], tk_const], writes=[tk_ub[u0i]])
                    S.op("pool", lambda e, u0=u0, uc=uc: e.tensor_copy(out=halo_u[:, uc, :], in_=u0[:, 16 + W - 15:16 + W]), reads=[tk_ub[u0i]], writes=[tk_hu[uc]])
                    src, si = u0, u0i
                    k = 1
                    lo = 1
                    while k < wwin:
                        lo += k
                        di = 1 if si != 1 else 2
                        dst = ub[di]
                        S.op("pool", lambda e, src=src, dst=dst, lo=lo, k=k: e.tensor_tensor(out=dst[:, lo:16 + W], in0=src[:, lo:16 + W], in1=src[:, lo - k:16 + W - k], op=ALU.add),
                             reads=[tk_ub[si]], writes=[tk_ub[di]])
                        src, si = dst, di
                        k *= 2
                    if first:
                        S.op("dve", lambda e, src=src, uc=uc: e.tensor_tensor(out=src[:, 16:16 + W], in0=src[:, 16:16 + W], in1=invc0[:, uc // 2, :], op=ALU.mult),
                             reads=[tk_ub[si], tk_const, tk_rstd[uc // 2]], writes=[tk_ub[si]])
                        S.op("dve", lambda e, u0=u0, src=src, uc=uc: e.tensor_tensor(out=pT[:, uc, :], in0=src[:, 16:16 + W], in1=u0[:, 16:16 + W], op=ALU.subtract),
                             reads=[tk_ub[si], tk_ub[u0i]], writes=[tk_pT[uc]])
                    else:
                        S.op("dve", lambda e, u0=u0, src=src, uc=uc, wwin=wwin: e.scalar_tensor_tensor(out=pT[:, uc, :], in0=src[:, 16:16 + W], scalar=1.0 / wwin, in1=u0[:, 16:16 + W],
                                                                                                 op0=ALU.mult, op1=ALU.subtract),
                             reads=[tk_ub[si], tk_ub[u0i]], writes=[tk_pT[uc]])
                w_rel(wb)
            wb = w_get("PW")
            for oc in range(8):
                g, oh = oc // 2, oc % 2
                bank = next_bank()
                for kc in range(2):
                    mm(psb[bank][:, 0:W], wb.v[:, 2 * g + kc, 128 * oh:128 * oh + 128], pT[:, 2 * g + kc, :], kc == 0, kc == 1,
                       [wb.tk, tk_pT[2 * g + kc]], bank)
                S.op("act", lambda e, bank=bank, oc=oc: e.activation(out=ypT[:, oc, :], in_=psb[bank][:, 0:W], func=AF.Identity, scale=PPc(152 + oc)),
                     reads=[tk_ps[bank], tk_const], writes=[tk_ypT[oc]])
            w_rel(wb)
            for half in range(2):
                wpp = w_get(f"PP{half}")
                wgp = w_get(f"GP{half}")
                for c in range(4):
                    dc = 4 * half + c
                    b1 = next_bank()
                    proj_fm(wgp, 128 * c, lambda kc: cur.v[:, kc, :], 8, cur.tk, b1)
                    b0 = next_bank()
                    proj_fm(wpp, 128 * c, lambda kc: ypT[:, kc, :], 8, tk_ypT, b0)
                    ta = dc % 3
                    S.op("act", lambda e, b1=b1, ta=ta: e.activation(out=tmpA[ta], in_=psb[b1][:, 0:W], func=AF.Sigmoid), reads=[tk_ps[b1]], writes=[tk_tmpA[ta]])
                    S.op("dve", lambda e, b0=b0, ta=ta, dc=dc: e.tensor_tensor(out=poolpart[:, dc, :], in0=tmpA[ta], in1=psb[b0][:, 0:W], op=ALU.mult),
                         reads=[tk_ps[b0], tk_tmpA[ta]], writes=[tk_pp[dc]])
                w_rel(wpp)
                w_rel(wgp)
            if dbgt:
                dump(dbg_yp, ypT.rearrange('p i t -> p (i t)'), tk_ypT)
                dump(dbg_pp, poolpart.rearrange('p i t -> p (i t)'), tk_pp)
            alias_tokens(tk_xs, tk_act)
            alias_tokens(tk_xtok, tk_mrg)
            alias_tokens([tk_esg2], tk_ub[0:2])
            alias_tokens([tk_E2], tk_tmpA[0:2])
            dt_phase(mmain_d, ti * W, first)
            xbc_phase(tokm0 if first else None, True)
            if dbgt:
                dump(dbg_xs, xs.rearrange('p i t -> p (i t)'), tk_xs)
                dump(dbg_bc, BCT.rearrange('p i t -> p (i t)'), tk_BCT)
            for j in range(NCH):
                state_prep_chunk(j)
                ssd_y_chunk(j)
                state_update_chunk(j)
            alias_tokens(tk_ub[0:2], [tk_esg2])
            alias_tokens(tk_tmpA[0:2], [tk_E2])
            alias_tokens(tk_ynT, tk_pT + tk_ypT + tk_hst)
            def ones_mm(i):
                sb_, c, zb = i % 2, i % 4, i // 4
                S.op("pe", lambda e: e.matmul(psb[PX_][:, 0:W], lhsT=onesb, rhs=sqb[sb_], start=(c == 0), stop=(c == 3)),
                     reads=[tk_sq[sb_], tk_const], writes=[tk_ps[PX_]])
                if c == 3:
                    S.op("act", lambda e: e.activation(out=rstd[:, zb, :], in_=psb[PX_][:, 0:W], func=AF.Sqrt, bias=rmsepsc, scale=1.0 / 512.0),
                         reads=[tk_ps[PX_], tk_const], writes=[tk_rstd[zb]])
                    S.op("dve", lambda e: e.reciprocal(out=rstd[:, zb, :], in_=rstd[:, zb, :]), reads=[tk_rstd[zb]], writes=[tk_rstd[zb]])
            prev = None
            for zb in range(4):
                wb = w_get(f"Z{zb}")
                for c in range(4):
                    i = 4 * zb + c
                    bank = next_bank()
                    proj_fm(wb, 128 * c, lambda kc: cur.v[:, kc, :], 8, cur.tk, bank)
                    ta = i % 3
                    S.op("act", lambda e, bank=bank, ta=ta: e.activation(out=tmpA[ta], in_=psb[bank][:, 0:W], func=AF.Silu), reads=[tk_ps[bank]], writes=[tk_tmpA[ta]])
                    S.op("dve", lambda e, i=i, ta=ta: e.tensor_tensor(out=xs[:, i, :], in0=xs[:, i, :], in1=tmpA[ta], op=ALU.mult),
                         reads=[tk_tmpA[ta], tk_xs[i]], writes=[tk_xs[i]])
                    sb_ = i % 2
                    S.op("pool", lambda e, i=i, sb_=sb_: e.tensor_tensor(out=sqb[sb_], in0=xs[:, i, :], in1=xs[:, i, :], op=ALU.mult), reads=[tk_xs[i]], writes=[tk_sq[sb_]])
                    if prev is not None:
                        ones_mm(prev)
                    prev = i
                w_rel(wb)
            ones_mm(prev)
            for i in range(16):
                eng = "dve"
                S.op(eng, lambda e, i=i: e.scalar_tensor_tensor(out=ynT[:, i, :], in0=xs[:, i, :], scalar=PPc(136 + i), in1=rstd[:, i // 4, :], op0=ALU.mult, op1=ALU.mult),
                     reads=[tk_xs[i], tk_rstd[i // 4], tk_const], writes=[tk_ynT[i]])
            if dbgt:
                dump(dbg_yn, ynT.rearrange('p i t -> p (i t)'), tk_ynT)
            alias_tokens(tk_mrg, tk_xtok)
            wgs = None
            for q in range(4):
                wps = w_get(f"PS{q}")
                if q % 2 == 0:
                    wgs = w_get(f"GS{q // 2}")
                for c in range(2):
                    dc = 2 * q + c
                    b1 = next_bank()
                    proj_fm(wgs, 128 * (dc % 4), lambda kc: cur.v[:, kc, :], 8, cur.tk, b1)
                    b0 = next_bank()
                    proj_fm(wps, 128 * c, lambda kc: ynT[:, kc, :], 16, tk_ynT, b0)
                    ta = dc % 3
                    S.op("act", lambda e, b1=b1, ta=ta: e.activation(out=tmpA[ta], in_=psb[b1][:, 0:W], func=AF.Sigmoid), reads=[tk_ps[b1]], writes=[tk_tmpA[ta]])
                    S.op("dve", lambda e, b0=b0, ta=ta: e.tensor_tensor(out=tmpA[ta], in0=tmpA[ta], in1=psb[b0][:, 0:W], op=ALU.mult),
                         reads=[tk_ps[b0], tk_tmpA[ta]], writes=[tk_tmpA[ta]])
                    S.op("pool", lambda e, ta=ta, dc=dc: e.tensor_tensor(out=mergedT[:, dc, :], in0=tmpA[ta], in1=poolpart[:, dc, :], op=ALU.add),
                         reads=[tk_tmpA[ta], tk_pp[dc]], writes=[tk_mrg[dc]])
                w_rel(wps)
                if q % 2 == 1:
                    w_rel(wgs)
            if dbgt:
                dump(dbg_mg, mergedT.rearrange('p i t -> p (i t)'), tk_mrg)
            wo = [w_get("WO0"), w_get("WO1")]
            load_gb(1)
            for j in range(NCH):
                cs = slice(128 * j, 128 * j + 128)
                for half in range(2):
                    bank = next_bank()
                    for kc in range(8):
                        mm(psb[bank][:, 0:512], mergedT[:, kc, cs], wo[half].v[:, kc, :], kc == 0, kc == 7, [wo[half].tk, tk_mrg[kc]], bank)
                    S.op("dve", lambda e, bank=bank, half=half, j=j: e.scalar_tensor_tensor(out=hres[:, j, 512 * half:512 * half + 512], in0=hres[:, j, 512 * half:512 * half + 512],
                                                                                         scalar=ALPHA, in1=psb[bank][:, 0:512], op0=ALU.mult, op1=ALU.add),
                         reads=[tk_ps[bank], tk_hres[j]], writes=[tk_hres[j]])
                layernorm_chunk(hres, tk_hres, j)
                transpose_to(hres, tk_hres, j, cur.v, cur.tk)
            w_rel(wo[0])
            w_rel(wo[1])
            if dbgt:
                dump(dbg_h1, hres.rearrange('p j d -> p (j d)'), tk_hres)
            if gt + 1 < NT_ALL:
                alias_tokens(tk_hst, tk_ynT + tk_pT + tk_ypT)
                pf_load(gt + 1)
            alias_tokens(tk_act, tk_xs)
            pend = [None]
            for fb in range(11):
                wb = w_get(f"F{fb}")
                for c in range(2):
                    fi = 2 * fb + c
                    b0 = (0, 2, 4)[fi % 3]
                    b1 = (1, 3, 5)[fi % 3]
                    proj_fm(wb, 128 * c, lambda kc: cur.v[:, kc, :], 8, cur.tk, b0)
                    proj_fm(wb, 256 + 128 * c, lambda kc: cur.v[:, kc, :], 8, cur.tk, b1)
                    ta = fi % 3
                    pb_ = conv_chunk(b0, halo_f[:, fi, :], tk_hf[fi], 2, [160 + fi, 182 + fi, 204 + fi], 226 + fi, tokm0 if first else None, tmpA[ta], [tk_tmpA[ta]],
                                     "dve")

                    def partB2(pb_=pb_, b1=b1, ta=ta, fi=fi):
                        pb_()
                        S.op("dve", lambda e: e.tensor_tensor(out=actT[:, fi, :], in0=tmpA[ta], in1=psb[b1][:, 0:W], op=ALU.mult),
                             reads=[tk_ps[b1], tk_tmpA[ta]], writes=[tk_act[fi]])
                    if pend[0] is not None:
                        pend[0]()
                    pend[0] = partB2
                w_rel(wb)
            pend[0]()
            if dbgt:
                dump(dbg_act, actT.rearrange('p i t -> p (i t)'), tk_act)
            for q in range(4):
                wb = w_get(f"FD{q}")
                for j in range(NCH):
                    cs = slice(128 * j, 128 * j + 128)
                    bank = next_bank()
                    for kc in range(22):
                        mm(psb[bank][:, 0:256], actT[:, kc, cs], wb.v[:, kc, :], kc == 0, kc == 21, [wb.tk, tk_act[kc]], bank)
                    S.op("dve", lambda e, bank=bank, q=q, j=j: e.scalar_tensor_tensor(out=hres[:, j, 256 * q:256 * q + 256], in0=hres[:, j, 256 * q:256 * q + 256],
                                                                                   scalar=ALPHA, in1=psb[bank][:, 0:256], op0=ALU.mult, op1=ALU.add),
                         reads=[tk_ps[bank], tk_hres[j]], writes=[tk_hres[j]])
                w_rel(wb)
                if gt + 1 < NT_ALL and q < NCH:
                    pf_ln(q)
            load_gb(2)
            if gt + 1 < NT_ALL:
                pf_transposes(gt + 1)
            for j in range(NCH):
                layernorm_chunk(hres, tk_hres, j)
                ch = ti * NCH + j
                if ch >= 1:
                    r0 = (ch - 1) * 128
                    ob = j
                    S.dma("pool", lambda e, j=j, r0=r0: e.dma_start(out=y_d[r0:r0 + 128, :], in_=hres[:, j, :]), f"out{ob}", reads=[tk_hres[j]])

        print('arena words used', off[0], flush=True)
        pf_compute(0)
        pf_transposes(0)
        for ti in range(NT_PRE):
            tile_pre(ti)
        for ti in range(NT_MAIN):
            tile_main(ti)
        S.final_wait("pool", tk_hres + tk_dbg)
        S.emit(block)
    return nc


def make_pp(inp):
    pp = np.zeros((128, 256), np.float32)
    cw = np.asarray(inp["ssm_conv_w"])[0]
    cb = np.asarray(inp["ssm_conv_b"])[0]
    for k in range(4):
        pp[:, 24 * k:24 * k + 24] = cw[k].reshape(24, 128).T
    pp[:, 96:120] = cb.reshape(24, 128).T
    dsk = np.repeat(np.asarray(inp["ssm_d"])[0], 64)
    pp[:, 120:136] = dsk.reshape(16, 128).T
    pp[:, 136:152] = np.asarray(inp["ssm_norm_w"])[0].reshape(16, 128).T
    pp[:, 152:160] = np.asarray(inp["pool_scale"])[0].reshape(8, 128).T
    fw_ = np.asarray(inp["ffn_conv_w"])[0]
    for k in range(3):
        pp[:, 160 + 22 * k:160 + 22 * k + 22] = fw_[k].reshape(22, 128).T
    pp[:, 226:248] = np.asarray(inp["ffn_conv_b"])[0].reshape(22, 128).T
    pp[0:96, 248] = np.tile(np.asarray(inp["ssm_dt_bias"])[0], 3)
    pp[0:96, 249] = np.tile(np.asarray(inp["ssm_a_log"])[0], 3)
    return pp


def core_inputs(inp, b, hf, NT_PRE, NT_MAIN, shared):
    TP, TM = NT_PRE * W, NT_MAIN * W
    Q1 = TP - 128
    x = np.asarray(inp["x"])[b]
    meta = np.asarray(inp["meta_tokens"])
    seq_len = x.shape[0]

    def rows(q0, n):
        out = np.zeros((n, D), np.float32)
        msk = np.zeros((n,), np.float32)
        for lo, hi, src, s0 in ((112, 128, meta, 0), (128, 128 + seq_len, x, 0)):
            a = max(q0, lo)
            bnd = min(q0 + n, hi)
            if a < bnd:
                out[a - q0:bnd - q0] = src[a - lo:bnd - lo]
                msk[a - q0:bnd - q0] = 1.0
        return out, msk
    if hf == 0:
        xpre = np.zeros((TP, D), np.float32)
        mpre = np.zeros((TP,), np.float32)
        xmain, mmain = rows(0, TM)
    else:
        xpre, mpre = rows(Q1 - TP, TP)
        xmain, mmain = rows(Q1, TM)
    tokmask0 = np.ones((128, 16 + W), np.float32)
    tokmask0[:, 0:16] = float(hf)
    tokmask0[:, 16:] = mmain[None, 0:W]
    tokmaskp = np.zeros((128, 16 + W), np.float32)
    tokmaskp[:, 16:] = mpre[None, 0:W]
    invcnt0 = np.zeros((128, 4, W), np.float32)
    q0 = 0 if hf == 0 else Q1
    lpos = np.arange(q0, q0 + W) - 112
    for g, wdw in enumerate(POOL_WINDOWS):
        cnt = np.minimum(np.maximum(lpos + 1, 1), wdw).astype(np.float32)
        invcnt0[:, g, :] = (1.0 / cnt)[None, :]
    d = dict(shared)
    d.update({
        "xpre": xpre, "xmain": xmain,
        "mpre": np.ascontiguousarray(np.broadcast_to(mpre[None, :], (96, TP))),
        "mmain": np.ascontiguousarray(np.broadcast_to(mmain[None, :], (96, TM))),
        "tokmask0": tokmask0, "invcnt0": invcnt0.reshape(128, 4 * W), "tokmaskp": tokmaskp,
    })
    return d


def shared_inputs(inp):
    lnrows = np.stack([
        np.concatenate([np.asarray(inp["ln_in_g"]), np.asarray(inp["ln_in_b"])]),
        np.concatenate([np.asarray(inp["ln1_g"])[0], np.asarray(inp["ln1_b"])[0]]),
        np.concatenate([np.asarray(inp["ln2_g"])[0], np.asarray(inp["ln2_b"])[0]]),
    ]).astype(np.float32)
    return {
        "pp": make_pp(inp), "lnrows": lnrows,
        "w_in": np.ascontiguousarray(np.asarray(inp["w_in"])[0]),
        "w_proj_ssm": np.ascontiguousarray(np.asarray(inp["w_proj_ssm"])[0]),
        "w_proj_pool": np.ascontiguousarray(np.asarray(inp["w_proj_pool"])[0]),
        "pool_w": np.ascontiguousarray(np.asarray(inp["pool_w"])[0]),
        "w_out": np.ascontiguousarray(np.asarray(inp["w_out"])[0]),
        "ffn_w_up": np.ascontiguousarray(np.asarray(inp["ffn_w_up"])[0]),
        "ffn_w_down": np.ascontiguousarray(np.asarray(inp["ffn_w_down"])[0]),
    }


_NC_CACHE = {}


def kernel(**inputs):
    NT_PRE, NT_MAIN = 11, 11
    key = (NT_PRE, NT_MAIN)
    if key not in _NC_CACHE:
        _NC_CACHE[key] = build_program(NT_PRE, NT_MAIN)
    nc = _NC_CACHE[key]
    shared = shared_inputs(inputs)
    B = np.asarray(inputs["x"]).shape[0]
    in_maps = []
    for core in range(8):
        b, hf = core // 2, core % 2
        in_maps.append(core_inputs(inputs, b, hf, NT_PRE, NT_MAIN, shared))
    res = run_bass_kernel_spmd(nc, in_maps, core_ids=list(range(8)))
    out = np.zeros((B, 8192, D), np.float32)
    for core in range(8):
        b, hf = core // 2, core % 2
        out[b, 4096 * hf:4096 * hf + 4096] = res.results[core]["y"]
    return out
```

```python
import numpy as np
from contextlib import ExitStack
import concourse.bass as bass
import concourse.mybir as mybir
from concourse.bass_utils import run_bass_kernel_spmd

F32 = mybir.dt.float32
BF16 = mybir.dt.bfloat16
ALU = mybir.AluOpType
AF = mybir.ActivationFunctionType

D = 1024
DI = 2048
NH = 32
W = 384
NCH = 3
ALPHA = 2.0 ** 0.25
LN_EPS = 1e-5
RMS_EPS = 1e-5
POOL_WINDOWS = (2, 4, 8, 16)
DFF = 2816
SLOTW = 5632
NSLOT = 3


class Tk:
    __slots__ = ("w", "rs")

    def __init__(self):
        self.w = None
        self.rs = []


def alias_tokens(new, old):
    deps = []
    for t in old:
        if t.w is not None:
            deps.append(t.w)
        deps.extend(t.rs)
    for n in new:
        n.rs = list(n.rs) + deps


class Sync:
    ENG = ("pe", "act", "dve", "pool", "sp")

    def __init__(self):
        self.prog = {e: [] for e in self.ENG}
        self.cnt = {}
        self.seen = {e: {} for e in self.ENG}
        self.sems = {}

    def add_sem(self, key, handle):
        self.sems[key] = handle
        self.cnt[key] = 0

    def _deps(self, eng, reads, writes, pe_chain=False):
        deps = {}

        def add(d):
            if d is None:
                return
            k, v = d
            if deps.get(k, 0) < v:
                deps[k] = v
        for t in reads:
            add(t.w)
        for t in writes:
            add(t.w)
            for r in t.rs:
                add(r)
        out = []
        for k, v in deps.items():
            if k == "pe" and eng == "pe":
                continue
            if self.seen[eng].get(k, 0) < v:
                self.seen[eng][k] = v
                out.append((k, v))
        return out

    def op(self, eng, fn, reads=(), writes=()):
        waits = self._deps(eng, reads, writes)
        self.cnt[eng] += 1
        me = (eng, self.cnt[eng])
        self.prog[eng].append((waits, fn, (eng, 1)))
        for t in reads:
            t.rs.append(me)
            if len(t.rs) > 64:
                t.rs = _compact(t.rs)
        for t in writes:
            t.w = me
            t.rs = []

    def dma(self, eng, fn, semkey, reads=(), writes=()):
        waits = self._deps(eng, reads, writes)
        self.cnt[semkey] += 16
        me = (semkey, self.cnt[semkey])
        self.prog[eng].append((waits, fn, (semkey, 16)))
        for t in reads:
            t.rs.append(me)
        for t in writes:
            t.w = me
            t.rs = []

    def final_wait(self, eng, toks):
        waits = self._deps(eng, (), toks)
        self.prog[eng].append((waits, None, None))

    def emit(self, block):
        sems = self.sems

        def run(engname):
            def body(e):
                for waits, fn, inc in self.prog[engname]:
                    for k, v in waits:
                        e.wait_ge(sems[k], v)
                    if fn is not None:
                        fn(e).then_inc(sems[inc[0]], inc[1])
            return body
        block.tensor(run("pe"))
        block.scalar(run("act"))
        block.vector(run("dve"))
        block.gpsimd(run("pool"))
        block.sync(run("sp"))


def _compact(rs):
    best = {}
    for k, v in rs:
        if best.get(k, 0) < v:
            best[k] = v
    return list(best.items())


C_Z, C_X, C_B, C_C, C_DT, C_U, C_GS, C_GP = 0, 2048, 4096, 4608, 5120, 5152, 6176, 7200

BLK_NAMES = (["U0", "U1", "PW", "PP0", "PP1", "GP0", "GP1", "DT", "BC0", "BC1"]
             + [f"X{i}" for i in range(4)] + [f"Z{i}" for i in range(4)]
             + [f"PS{i}" for i in range(4)] + ["GS0", "GS1", "WO0", "WO1"]
             + [f"F{i}" for i in range(11)] + [f"FD{i}" for i in range(4)])
BLK_ID = {n: i for i, n in enumerate(BLK_NAMES)}
NBLK = len(BLK_NAMES)

MAIN_SEQ = (["U0", "U1", "PW", "PP0", "GP0", "PP1", "GP1", "DT", "BC0", "BC1", "X0", "X1", "X2", "X3",
             "Z0", "Z1", "Z2", "Z3", "PS0", "GS0", "PS1", "PS2", "GS1", "PS3", "WO0", "WO1"]
            + [f"F{i}" for i in range(11)] + [f"FD{i}" for i in range(4)])
PRE_SEQ = ["DT", "BC0", "X0", "X1", "X2", "X3"]


def build_program(NT_PRE, NT_MAIN, debug=False):
    TP = NT_PRE * W
    TM = NT_MAIN * W
    NOUT = (NT_MAIN * NCH - 1) * 128
    nc = bass.Bass("TRN2", target_bir_lowering=False)

    def din(name, shape, dt=F32):
        return nc.dram_tensor(name, list(shape), dt, kind="ExternalInput").ap()
    xpre_d = din("xpre", [TP, D])
    xmain_d = din("xmain", [TM, D])
    mpre_d = din("mpre", [96, TP])
    mmain_d = din("mmain", [96, TM])
    tm0_d = din("tokmask0", [128, 16 + W])
    ic0_d = din("invcnt0", [128, 4 * W])
    tmp_d = din("tokmaskp", [128, 16 + W])
    pp_d = din("pp", [128, 256])
    lnrow_d = din("lnrows", [3, 2 * D])
    w_in_d = din("w_in", [D, 8224])
    w_ps_d = din("w_proj_ssm", [DI, D])
    w_pp_d = din("w_proj_pool", [D, D])
    w_pw_d = din("pool_w", [4, 256, 256])
    w_o_d = din("w_out", [D, D])
    w_up_d = din("ffn_w_up", [D, 2 * DFF])
    w_dn_d = din("ffn_w_down", [DFF, D])
    y_d = nc.dram_tensor("y", [NOUT, D], F32, kind="ExternalOutput").ap()
    wsc = nc.dram_tensor("wsc", [NBLK, 128, SLOTW], BF16, kind="Internal").ap()
    if debug:
        dbg_yn = nc.dram_tensor("dbg_yn", [128, 16 * W], BF16, kind="ExternalOutput").ap()
        dbg_yp = nc.dram_tensor("dbg_yp", [128, 8 * W], BF16, kind="ExternalOutput").ap()
        dbg_pp = nc.dram_tensor("dbg_pp", [128, 8 * W], F32, kind="ExternalOutput").ap()
        dbg_mg = nc.dram_tensor("dbg_mg", [128, 8 * W], BF16, kind="ExternalOutput").ap()
        dbg_h1 = nc.dram_tensor("dbg_h1", [128, 3 * D], F32, kind="ExternalOutput").ap()
        dbg_h0 = nc.dram_tensor("dbg_h0", [128, 3 * D], F32, kind="ExternalOutput").ap()
        dbg_act = nc.dram_tensor("dbg_act", [128, 22 * W], BF16, kind="ExternalOutput").ap()
        dbg_xs = nc.dram_tensor("dbg_xs", [128, 16 * W], F32, kind="ExternalOutput").ap()
        dbg_bc = nc.dram_tensor("dbg_bc", [128, 8 * W], BF16, kind="ExternalOutput").ap()

    es = ExitStack()
    with es:
        ARENA_WORDS = 53100
        arena = es.enter_context(nc.sbuf_tensor("arena", [128, ARENA_WORDS], F32))
        psb = [es.enter_context(nc.psum_tensor(f"psb{i}", [128, 512], F32)) for i in range(8)]
        tk_ps = [Tk() for _ in range(8)]
        S = Sync()
        for e in Sync.ENG:
            S.add_sem(e, es.enter_context(nc.semaphore("s_" + e)))
        DMA_KEYS = (["wl0", "wl1", "wl2", "cvi0", "cvi1", "cvo0", "cvo1", "cvo2", "xin0", "xin1", "xin2",
                     "gb", "msk", "out0", "out1", "out2", "setup", "dbg"])
        for k in DMA_KEYS:
            S.add_sem(k, es.enter_context(nc.semaphore("s_" + k)))
        block = es.enter_context(nc.Block())

        off = [0]

        def alloc(nwords):
            o = off[0]
            off[0] += nwords
            assert off[0] <= ARENA_WORDS, off[0]
            return o

        def f32v(o, n):
            return arena[:, o:o + n]

        def bfv(o, nbf):
            return arena[:, o:o + nbf // 2].bitcast(BF16)

        o_hres = alloc(3 * D)
        hres = f32v(o_hres, 3 * D).rearrange("p (j d) -> p j d", j=3)
        tk_hres = [Tk() for _ in range(3)]
        o_gb = alloc(2 * D)
        gbt = f32v(o_gb, 2 * D)
        tk_gb = Tk()
        hTs = [bfv(alloc(8 * W // 2), 8 * W).rearrange("p (k t) -> p k t", k=8) for _ in range(2)]
        tk_hTs = [[Tk() for _ in range(3)] for _ in range(2)]

        class Cur:
            pass
        cur = Cur()
        cur.v, cur.tk = hTs[0], tk_hTs[0]
        o_RA = alloc(16 * W)
        xs = f32v(o_RA, 16 * W).rearrange("p (i t) -> p i t", i=16)
        actT = bfv(o_RA, 22 * W).rearrange("p (i t) -> p i t", i=22)
        tk_xs = [Tk() for _ in range(16)]
        tk_act = [Tk() for _ in range(22)]
        o_RB = alloc(8 * W)
        pT = bfv(o_RB, 8 * W).rearrange("p (i t) -> p i t", i=8)
        ypT = bfv(o_RB + 4 * W, 8 * W).rearrange("p (i t) -> p i t", i=8)
        ynT = bfv(o_RB, 16 * W).rearrange("p (i t) -> p i t", i=16)
        tk_pT = [Tk() for _ in range(8)]
        hstage = f32v(o_RB, 3 * D).rearrange("p (j d) -> p j d", j=3)
        tk_hst = [Tk() for _ in range(3)]
        tk_ypT = [Tk() for _ in range(8)]
        tk_ynT = [Tk() for _ in range(16)]
        o_PP = alloc(8 * W)
        poolpart = f32v(o_PP, 8 * W).rearrange("p (i t) -> p i t", i=8)
        tk_pp = [Tk() for _ in range(8)]
        stg = [f32v(o_RA, SLOTW), f32v(o_RB, SLOTW)]
        tk_stg = [Tk(), Tk()]
        o_RC = alloc(3 * DI // 2)
        xtok = bfv(o_RC, 3 * DI).rearrange("p (j c) -> p j c", j=3)
        mergedT = bfv(o_RC, 8 * W).rearrange("p (i t) -> p i t", i=8)
        tk_xtok = [Tk() for _ in range(3)]
        tk_mrg = [Tk() for _ in range(8)]
        BCT = bfv(alloc(8 * W // 2), 8 * W).rearrange("p (i t) -> p i t", i=8)
        tk_BCT = [Tk() for _ in range(8)]
        RAWW = 16 + W
        o_raw = alloc(2 * RAWW)
        raw = [f32v(o_raw, RAWW), f32v(o_raw + RAWW, RAWW)]
        tk_raw = [Tk(), Tk()]
        o_acc = alloc(2 * W)
        cacc = [f32v(o_acc, W), f32v(o_acc + W, W)]
        tk_acc = [Tk(), Tk()]
        halo_x = f32v(alloc(24 * 3), 72).rearrange("p (i k) -> p i k", i=24)
        halo_u = f32v(alloc(8 * 15), 120).rearrange("p (i k) -> p i k", i=8)
        halo_f = f32v(alloc(22 * 2), 44).rearrange("p (i k) -> p i k", i=22)
        tk_hx = [Tk() for _ in range(24)]
        tk_hu = [Tk() for _ in range(8)]
        tk_hf = [Tk() for _ in range(22)]
        xw = bfv(alloc(DI // 2), DI)
        tk_xw = Tk()
        Btok = bfv(alloc(256), 512)
        tk_Btok = Tk()
        Mt = bfv(alloc(NH * 128 // 2), NH * 128).rearrange("p (h t) -> p h t", h=NH)
        tk_Mt = [Tk() for _ in range(8)]
        esg = f32v(alloc(512), 512)
        tk_esg = Tk()
        Et = f32v(alloc(512), 512)
        tk_E = Tk()
        t1b = esg
        tk_t1 = tk_esg
        hS = f32v(alloc(DI), DI)
        tk_hS = [Tk() for _ in range(4)]
        hSb = bfv(alloc(DI // 2), DI)
        tk_hSb = [Tk() for _ in range(4)]
        o_tmp = alloc(3 * W)
        tmpA = [f32v(o_tmp + i * W, W) for i in range(3)]
        tk_tmpA = [Tk() for _ in range(3)]
        o_sq = alloc(W)
        sqb = [bfv(o_sq, W), bfv(o_sq + W // 2, W)]
        tk_sq = [Tk(), Tk()]
        rstd = f32v(alloc(4 * W), 4 * W).rearrange("p (g t) -> p g t", g=4)
        tk_rstd = [Tk() for _ in range(4)]
        UW = 16 + W
        o_u = alloc(4 * UW)
        ub = [f32v(o_u + i * UW, UW) for i in range(4)]
        tk_ub = [Tk() for _ in range(4)]
        NDT = 8
        o_dt = alloc(NDT * W)
        dtc = [f32v(o_dt + i * W, W) for i in range(NDT)]
        tk_dtc = [Tk() for _ in range(NDT)]
        S3 = bfv(alloc(W // 2), W)
        nS3 = bfv(alloc(W // 2), W)
        tk_S3, tk_nS3 = Tk(), Tk()
        wtok = f32v(alloc(3 * 32), 96).rearrange("p (j h) -> p j h", j=3)
        tk_wtok = [Tk() for _ in range(3)]
        dAt = f32v(alloc(32), 32)
        tk_dA = Tk()
        rdA = bfv(alloc(16), 32)
        tk_rdA = Tk()
        st12s = [f32v(alloc(16), 16) for _ in range(3)]
        tk_sts = [Tk() for _ in range(3)]
        mvs = [f32v(alloc(8), 8) for _ in range(3)]
        tk_mvs = [Tk() for _ in range(3)]
        dmask = f32v(alloc(W), W)
        tk_dmask = Tk()
        tokm0 = f32v(alloc(16 + W), 16 + W)
        invc0 = rstd
        ppt = f32v(alloc(256), 256)
        identf = f32v(alloc(128), 128)
        identb = bfv(alloc(64), 128)
        onesb = bfv(alloc(64), 128)
        SEL = bfv(alloc(16), 32)
        maskneg = bfv(alloc(64), 128)
        aneg = f32v(alloc(2), 2)
        tk_const = Tk()
        o_slot = alloc(NSLOT * SLOTW // 2)
        slots = [bfv(o_slot + i * SLOTW // 2, SLOTW) for i in range(NSLOT)]
        tk_slot = [Tk() for _ in range(NSLOT)]
        tk_blk = [Tk() for _ in range(NBLK)]

        def PPc(c):
            return ppt[:, c:c + 1]

        tokmp = f32v(alloc(16 + W), 16 + W)
        epst = f32v(alloc(2), 2)
        epsc = epst[:, 0:1]
        rmsepsc = epst[:, 1:2]
        tk_cdma = Tk()
        tk_k = Tk()
        S.dma("sp", lambda e: e.dma_start(out=ppt, in_=pp_d[:, :]), "setup", writes=[tk_cdma])
        S.dma("sp", lambda e: e.dma_start(out=tokm0, in_=tm0_d[:, :]), "setup", writes=[tk_cdma])
        S.dma("sp", lambda e: e.dma_start(out=tokmp, in_=tmp_d[:, :]), "setup", writes=[tk_cdma])
        S.dma("sp", lambda e: e.dma_start(out=invc0.rearrange("p g t -> p (g t)"), in_=ic0_d[:, :]), "setup", writes=[tk_cdma])
        tk_cdma.w = ("setup", S.cnt["setup"])
        for t_ in tk_rstd:
            t_.w = tk_cdma.w
        S.op("pool", lambda e: e.memset(identf, 1.0), writes=[tk_k])
        S.op("pool", lambda e: e.affine_select(out=identf, in_=identf, pattern=[[1, 128]], compare_op=ALU.is_equal, fill=0.0, base=0, channel_multiplier=-1), reads=[tk_k], writes=[tk_k])
        S.op("pool", lambda e: e.tensor_copy(out=identb, in_=identf), reads=[tk_k], writes=[tk_k])
        S.op("pool", lambda e: e.memset(onesb, 1.0), writes=[tk_k])
        S.op("pool", lambda e: e.memset(epst[:, 0:1], LN_EPS), writes=[tk_k])
        S.op("pool", lambda e: e.memset(epst[:, 1:2], RMS_EPS), writes=[tk_k])
        S.op("pool", lambda e: e.memset(maskneg, 0.0), writes=[tk_k])
        S.op("pool", lambda e: e.affine_select(out=maskneg, in_=maskneg, pattern=[[1, 128]], compare_op=ALU.is_ge, fill=-30000.0, base=0, channel_multiplier=-1), reads=[tk_k], writes=[tk_k])
        S.op("pool", lambda e: e.memset(SEL, 0.0), writes=[tk_k])
        for r in range(3):
            S.op("pool", lambda e, r=r: e.tensor_copy(out=SEL[32 * r:32 * r + 32, :], in_=identf[32 * r:32 * r + 32, 32 * r:32 * r + 32]), reads=[tk_k], writes=[tk_k])
        S.op("pool", lambda e: e.memset(hS, 0.0), writes=tk_hS)
        S.op("pool", lambda e: e.memset(hSb, 0.0), writes=tk_hSb)
        S.op("pool", lambda e: e.memset(halo_x.rearrange("p i k -> p (i k)"), 0.0), writes=tk_hx)
        S.op("pool", lambda e: e.memset(halo_u.rearrange("p i k -> p (i k)"), 0.0), writes=tk_hu)
        S.op("pool", lambda e: e.memset(halo_f.rearrange("p i k -> p (i k)"), 0.0), writes=tk_hf)
        S.op("act", lambda e: e.activation(out=aneg[:, 0:1], in_=PPc(249), func=AF.Exp), reads=[tk_cdma], writes=[tk_k])
        S.op("dve", lambda e: e.tensor_scalar(out=aneg[:, 1:2], in0=aneg[:, 0:1], scalar1=-1.0, scalar2=None, op0=ALU.mult), reads=[tk_k], writes=[tk_k])
        S.op("pool", lambda e: e.memset(aneg[:, 0:1], 0.0), reads=[tk_cdma, tk_k], writes=[tk_const])

        def kc_view(ap2d, kc):
            return ap2d.rearrange("(k p) c -> p k c", p=128)

        def conv_sources(name):
            if name in ("U0", "U1"):
                c0 = C_U + 512 * int(name[1])
                return 8, 512, [((0, 512), kc_view(w_in_d[:, c0:c0 + 512], 8))]
            if name in ("GP0", "GP1"):
                c0 = C_GP + 512 * int(name[2])
                return 8, 512, [((0, 512), kc_view(w_in_d[:, c0:c0 + 512], 8))]
            if name in ("GS0", "GS1"):
                c0 = C_GS + 512 * int(name[2])
                return 8, 512, [((0, 512), kc_view(w_in_d[:, c0:c0 + 512], 8))]
            if name in ("BC0", "BC1"):
                c0 = C_B + 512 * int(name[2])
                return 8, 512, [((0, 512), kc_view(w_in_d[:, c0:c0 + 512], 8))]
            if name[0] == "X":
                c0 = C_X + 512 * int(name[1])
                return 8, 512, [((0, 512), kc_view(w_in_d[:, c0:c0 + 512], 8))]
            if name[0] == "Z":
                c0 = C_Z + 512 * int(name[1])
                return 8, 512, [((0, 512), kc_view(w_in_d[:, c0:c0 + 512], 8))]
            if name == "DT":
                return 8, 96, [((32 * r, 32 * r + 32), kc_view(w_in_d[:, C_DT:C_DT + 32], 8)) for r in range(3)]
            if name[:2] == "PS":
                q = int(name[2])
                return 16, 256, [((0, 256), kc_view(w_ps_d[:, 256 * q:256 * q + 256], 16))]
            if name[:2] == "PP":
                q = int(name[2])
                return 8, 512, [((0, 512), kc_view(w_pp_d[:, 512 * q:512 * q + 512], 8))]
            if name == "PW":
                return 8, 256, [((0, 256), w_pw_d.rearrange("g (k p) c -> p (g k) c", p=128))]
            if name[:2] == "WO":
                q = int(name[2])
                return 8, 512, [((0, 512), kc_view(w_o_d[:, 512 * q:512 * q + 512], 8))]
            if name[:2] == "FD":
                q = int(name[2])
                return 22, 256, [((0, 256), kc_view(w_dn_d[:, 256 * q:256 * q + 256], 22))]
            if name[0] == "F":
                fb = int(name[1:])
                return 8, 512, [((0, 256), kc_view(w_up_d[:, 256 * fb:256 * fb + 256], 8)),
                                ((256, 512), kc_view(w_up_d[:, DFF + 256 * fb:DFF + 256 * fb + 256], 8))]
            raise KeyError(name)

        BLK_SHAPE = {}
        cast_engs = ["dve", "act", "dve"]
        for bi, name in enumerate(BLK_NAMES):
            KC, CB, srcs = conv_sources(name)
            BLK_SHAPE[name] = (KC, CB)
            n = KC * CB
            si = bi % 2
            sl = bi % NSLOT
            sview = stg[si][:, 0:n].rearrange("p (k c) -> p k c", k=KC)
            for (c0, c1), src in srcs:
                S.dma("sp", lambda e, sview=sview, c0=c0, c1=c1, src=src: e.dma_start(out=sview[:, :, c0:c1], in_=src),
                      f"cvi{si}", writes=[tk_stg[si]])
            ce = cast_engs[bi % 3]
            if ce == "act":
                S.op("act", lambda e, sl=sl, si=si, n=n: e.activation(out=slots[sl][:, 0:n], in_=stg[si][:, 0:n], func=AF.Copy),
                     reads=[tk_stg[si]], writes=[tk_slot[sl]])
            else:
                S.op(ce, lambda e, sl=sl, si=si, n=n: e.tensor_copy(out=slots[sl][:, 0:n], in_=stg[si][:, 0:n]),
                     reads=[tk_stg[si]], writes=[tk_slot[sl]])
            S.dma("act", lambda e, sl=sl, bi=bi, n=n: e.dma_start(out=wsc[bi, :, 0:n], in_=slots[sl][:, 0:n]),
                  f"cvo{sl}", reads=[tk_slot[sl]], writes=[tk_blk[bi]])
        alias_tokens(tk_xs + tk_act, [tk_stg[0]])
        alias_tokens(tk_pT + tk_ypT + tk_ynT + tk_pp + tk_hst, [tk_stg[1]])

        seq = []
        for _ in range(NT_PRE):
            seq += PRE_SEQ
        for _ in range(NT_MAIN):
            seq += MAIN_SEQ
        wst = {"issued": 0, "next": 0}

        def w_issue():
            k = wst["issued"]
            if k >= len(seq):
                return
            name = seq[k]
            bi = BLK_ID[name]
            KC, CB = BLK_SHAPE[name]
            n = KC * CB
            sl = k % NSLOT
            S.dma("sp", lambda e, sl=sl, bi=bi, n=n: e.dma_start(out=slots[sl][:, 0:n], in_=wsc[bi, :, 0:n]),
                  f"wl{sl}", reads=[tk_blk[bi]], writes=[tk_slot[sl]])
            wst["issued"] += 1

        class WB:
            pass

        def w_get(name):
            k = wst["next"]
            assert seq[k] == name, (k, seq[k], name)
            assert k < wst["issued"], (k, wst["issued"])
            wst["next"] += 1
            KC, CB = BLK_SHAPE[name]
            b = WB()
            b.k = k
            b.tk = tk_slot[k % NSLOT]
            b.v = slots[k % NSLOT][:, 0:KC * CB].rearrange("p (k c) -> p k c", k=KC)
            return b

        def w_rel(b):
            w_issue()

        for _ in range(NSLOT):
            w_issue()

        bank_rr = [0]

        def next_bank():
            b = bank_rr[0]
            bank_rr[0] ^= 1
            return b
        PT_, PS_, PG_, PY_, PO_, PX_ = 2, 3, 4, 5, 6, 7

        def mm(out_ap, lhsT, rhs, start, stop, reads, bank, **kw):
            S.op("pe", lambda e: e.matmul(out_ap, lhsT=lhsT, rhs=rhs, start=start, stop=stop, **kw),
                 reads=reads, writes=[tk_ps[bank]])

        def proj_fm(wb, col0, rhs_fn, KC, rhs_tks, bank, ncols=128, wcols=W):
            for kc in range(KC):
                mm(psb[bank][0:ncols, 0:wcols], wb.v[:, kc, col0:col0 + ncols], rhs_fn(kc), kc == 0, kc == KC - 1,
                   [wb.tk] + rhs_tks, bank)

        def layernorm_chunk(buf, tks, j):
            hj = buf[:, j, :]
            st12, tk_st, mv, tk_mv = st12s[j], tk_sts[j], mvs[j], tk_mvs[j]
            for h2 in range(2):
                S.op("dve", lambda e, h2=h2: e.bn_stats(out=st12[:, 6 * h2:6 * h2 + 6], in_=buf[:, j, 512 * h2:512 * h2 + 512]),
                     reads=[tks[j]], writes=[tk_st])
            S.op("dve", lambda e: e.bn_aggr(out=mv[:, 0:2], in_=st12[:, 0:12]),
                 reads=[tk_st], writes=[tk_mv])
            S.op("act", lambda e: e.activation(out=mv[:, 2:3], in_=mv[:, 1:2], func=AF.Sqrt, bias=epsc, scale=1.0),
                 reads=[tk_mv, tk_const], writes=[tk_mv])
            S.op("dve", lambda e: e.reciprocal(out=mv[:, 3:4], in_=mv[:, 2:3]), reads=[tk_mv], writes=[tk_mv])
            S.op("dve", lambda e: e.tensor_scalar(out=hj, in0=hj, scalar1=mv[:, 0:1], scalar2=mv[:, 3:4], op0=ALU.subtract, op1=ALU.mult),
                 reads=[tk_mv, tks[j]], writes=[tks[j]])
            S.op("dve", lambda e: e.tensor_tensor(out=hj, in0=hj, in1=gbt[:, 0:D], op=ALU.mult),
                 reads=[tk_gb, tks[j]], writes=[tks[j]])
            S.op("pool", lambda e: e.tensor_tensor(out=hj, in0=hj, in1=gbt[:, D:2 * D], op=ALU.add),
                 reads=[tk_gb, tks[j]], writes=[tks[j]])

        gb_cur = [None]

        def load_gb(row):
            if gb_cur[0] == row:
                return
            gb_cur[0] = row
            S.dma("pool", lambda e: e.dma_start(out=gbt, in_=lnrow_d[row:row + 1, :].partition_broadcast(128)),
                  "gb", writes=[tk_gb])

        def transpose_to(buf, tks, j, dst, dst_tks):
            for half in range(2):
                for k4 in range(4):
                    kc = half * 4 + k4
                    S.op("pe", lambda e, kc=kc, k4=k4: e.transpose(out=psb[PT_][:, 128 * k4:128 * k4 + 128], in_=buf[:, j, 128 * kc:128 * kc + 128], identity=identf),
                         reads=[tks[j], tk_const], writes=[tk_ps[PT_]])
                S.op("act", lambda e, half=half: e.activation(out=dst[:, 4 * half:4 * half + 4, 128 * j:128 * j + 128],
                                                              in_=psb[PT_][:, 0:512].rearrange("p (k t) -> p k t", k=4), func=AF.Copy),
                     reads=[tk_ps[PT_]], writes=[dst_tks[j]])

        NT_ALL = NT_PRE + NT_MAIN

        def pf_load(gt):
            if gt < NT_PRE:
                xd, row0 = xpre_d, gt * W
            else:
                xd, row0 = xmain_d, (gt - NT_PRE) * W
            for j in range(NCH):
                r0 = row0 + 128 * j
                S.dma("pool", lambda e, j=j, r0=r0: e.dma_start(out=hstage[:, j, :], in_=xd[r0:r0 + 128, :]), f"xin{j}", writes=[tk_hst[j]])

        def pf_ln(j):
            load_gb(0)
            layernorm_chunk(hstage, tk_hst, j)

        def pf_compute(gt):
            pf_load(gt)
            for j in range(NCH):
                pf_ln(j)

        def pf_transposes(gt):
            for j in range(NCH):
                transpose_to(hstage, tk_hst, j, hTs[gt % 2], tk_hTs[gt % 2])

        def conv_chunk(bank, halo, tk_h, hk, wcols, bcol, first, silu_out, silu_tks, conv_eng):
            ri = conv_chunk.rr
            conv_chunk.rr ^= 1
            rw = raw[ri]
            ac = cacc[ri]
            S.op("pool", lambda e: e.tensor_copy(out=rw[:, 16 - hk:16], in_=halo), reads=[tk_h], writes=[tk_raw[ri]])
            S.op("act", lambda e: e.activation(out=rw[:, 16:16 + W], in_=psb[bank][:, 0:W], func=AF.Copy), reads=[tk_ps[bank]], writes=[tk_raw[ri]])
            if first is not None and first is not False:
                mk = first
                S.op("pool", lambda e: e.tensor_tensor(out=rw[:, 16 - hk:16 + W], in0=rw[:, 16 - hk:16 + W], in1=mk[:, 16 - hk:16 + W], op=ALU.mult),
                     reads=[tk_raw[ri], tk_const], writes=[tk_raw[ri]])
            S.op("pool", lambda e: e.tensor_copy(out=halo, in_=rw[:, 16 + W - hk:16 + W]), reads=[tk_raw[ri]], writes=[tk_h])
            if first is not None and first is not False:
                S.op(conv_eng, lambda e: e.tensor_scalar(out=ac, in0=rw[:, 16:16 + W], scalar1=PPc(wcols[hk]), scalar2=PPc(bcol), op0=ALU.mult, op1=ALU.add),
                     reads=[tk_raw[ri], tk_const], writes=[tk_acc[ri]])
            else:
                S.op("act", lambda e: e.activation(out=ac, in_=psb[bank][:, 0:W], func=AF.Identity, scale=PPc(wcols[hk]), bias=PPc(bcol)),
                     reads=[tk_ps[bank], tk_const], writes=[tk_acc[ri]])
            for k in range(hk):
                sh = hk - k
                S.op(conv_eng, lambda e, k=k, sh=sh: e.scalar_tensor_tensor(out=ac, in0=rw[:, 16 - sh:16 - sh + W], scalar=PPc(wcols[k]), in1=ac, op0=ALU.mult, op1=ALU.add),
                     reads=[tk_raw[ri], tk_acc[ri], tk_const], writes=[tk_acc[ri]])
            def partB():
                S.op("act", lambda e: e.activation(out=silu_out, in_=ac, func=AF.Silu), reads=[tk_acc[ri]], writes=silu_tks)
            return partB
        conv_chunk.rr = 0

        def dt_phase(md, col0, masked):
            wb = w_get("DT")
            proj_fm(wb, 0, lambda kc: cur.v[:, kc, :], 8, cur.tk, PX_, ncols=96)
            w_rel(wb)
            P = slice(0, 96)
            e_, dt_, dta_, acum_, lnd_, q_, t0_, t1_ = [d[P, :] for d in dtc]
            S.op("act", lambda e: e.activation(out=e_, in_=psb[PX_][0:96, 0:W], func=AF.Exp, bias=ppt[0:96, 248:249], scale=1.0),
                 reads=[tk_ps[PX_], tk_const], writes=[tk_dtc[0]])
            S.op("act", lambda e: e.activation(out=dt_, in_=e_, func=AF.Ln, bias=1.0), reads=[tk_dtc[0]], writes=[tk_dtc[1]])
            if masked:
                S.dma("pool", lambda e: e.dma_start(out=dmask[0:96, :], in_=md[:, col0:col0 + W]), "msk", writes=[tk_dmask])
                S.op("dve", lambda e: e.tensor_scalar(out=t0_, in0=dmask[0:96, :], scalar1=-1.0, scalar2=1.0, op0=ALU.mult, op1=ALU.add),
                     reads=[tk_dmask], writes=[tk_dtc[6]])
                S.op("dve", lambda e: e.tensor_tensor(out=dt_, in0=dt_, in1=dmask[0:96, :], op=ALU.mult), reads=[tk_dmask, tk_dtc[1]], writes=[tk_dtc[1]])
                S.op("dve", lambda e: e.tensor_tensor(out=t1_, in0=dt_, in1=t0_, op=ALU.add), reads=[tk_dtc[1], tk_dtc[6]], writes=[tk_dtc[7]])
                S.op("act", lambda e: e.activation(out=lnd_, in_=t1_, func=AF.Ln), reads=[tk_dtc[7]], writes=[tk_dtc[4]])
                S.op("dve", lambda e: e.scalar_tensor_tensor(out=lnd_, in0=t0_, scalar=-200.0, in1=lnd_, op0=ALU.mult, op1=ALU.add),
                     reads=[tk_dtc[6], tk_dtc[4]], writes=[tk_dtc[4]])
            else:
                S.op("act", lambda e: e.activation(out=lnd_, in_=dt_, func=AF.Ln), reads=[tk_dtc[1]], writes=[tk_dtc[4]])
            S.op("dve", lambda e: e.tensor_scalar(out=dta_, in0=dt_, scalar1=aneg[0:96, 1:2], scalar2=None, op0=ALU.mult),
                 reads=[tk_dtc[1], tk_const], writes=[tk_dtc[2]])
            S.op("pool", lambda e: e.memset(t1_, 1.0), reads=[], writes=[tk_dtc[7]])
            for j in range(NCH):
                cs = slice(128 * j, 128 * j + 128)
                S.op("dve", lambda e, cs=cs: e.tensor_tensor_scan(out=acum_[:, cs], data0=t1_[:, cs], data1=dta_[:, cs], initial=0.0, op0=ALU.mult, op1=ALU.add),
                     reads=[tk_dtc[2], tk_dtc[7]], writes=[tk_dtc[3]])
            S.op("dve", lambda e: e.tensor_tensor(out=q_, in0=lnd_, in1=acum_, op=ALU.subtract), reads=[tk_dtc[4], tk_dtc[3]], writes=[tk_dtc[5]])

            def split3(src, tk_src, dst, tk_dst):
                hi = dtc[6]
                r1 = dtc[7]
                mid = dtc[0]
                S.op("dve", lambda e: e.tensor_copy(out=dst[0:32, :], in_=src[0:32, :]), reads=[tk_src], writes=[tk_dst])
                S.op("dve", lambda e: e.tensor_copy(out=mid[0:96, 0:W // 2].bitcast(BF16), in_=src[0:96, :]), reads=[tk_src], writes=[tk_dtc[0]])
                S.op("dve", lambda e: e.tensor_tensor(out=r1[0:96, :], in0=src[0:96, :], in1=mid[0:96, 0:W // 2].bitcast(BF16), op=ALU.subtract),
                     reads=[tk_src, tk_dtc[0]], writes=[tk_dtc[7]])
                S.op("dve", lambda e: e.tensor_copy(out=dst[32:64, :], in_=r1[32:64, :]), reads=[tk_dtc[7]], writes=[tk_dst])
                S.op("dve", lambda e: e.tensor_copy(out=hi[64:96, 0:W // 2].bitcast(BF16), in_=r1[64:96, :]), reads=[tk_dtc[7]], writes=[tk_dtc[6]])
                S.op("dve", lambda e: e.tensor_tensor(out=r1[64:96, :], in0=r1[64:96, :], in1=hi[64:96, 0:W // 2].bitcast(BF16), op=ALU.subtract),
                     reads=[tk_dtc[6], tk_dtc[7]], writes=[tk_dtc[7]])
                S.op("dve", lambda e: e.tensor_copy(out=dst[64:96, :], in_=r1[64:96, :]), reads=[tk_dtc[7]], writes=[tk_dst])
            split3(dtc[3], tk_dtc[3], S3, tk_S3)
            split3(dtc[5], tk_dtc[5], nS3, tk_nS3)
            for j in range(NCH):
                cs = slice(128 * j, 128 * j + 128)
                S.op("act", lambda e, cs=cs, j=j: e.activation(out=dtc[1][0:32, cs], in_=dtc[5][0:32, cs], func=AF.Exp,
                                                               bias=dtc[3][0:32, 128 * j + 127:128 * j + 128], scale=1.0),
                     reads=[tk_dtc[5], tk_dtc[3]], writes=[tk_dtc[1]])
                S.op("pe", lambda e, cs=cs, j=j: e.transpose(out=psb[PX_][:, 128 + 32 * j:160 + 32 * j], in_=dtc[1][0:32, cs], identity=identf[0:32, 0:32]),
                     reads=[tk_dtc[1], tk_const], writes=[tk_ps[PX_]])
            S.op("act", lambda e: e.activation(out=wtok.rearrange("p j h -> p (j h)"), in_=psb[PX_][:, 128:224], func=AF.Copy),
                 reads=[tk_ps[PX_]], writes=tk_wtok)

        def xbc_phase(first, with_c):
            names = ["BC0"] + (["BC1"] if with_c else []) + ["X0", "X1", "X2", "X3"]
            pend = [None]
            for name in names:
                wb = w_get(name)
                for c in range(4):
                    if name[0] == "B":
                        idx = 16 + 4 * int(name[2]) + c
                        out_ap, otk = BCT[:, idx - 16, :], [tk_BCT[idx - 16]]
                    else:
                        idx = 4 * int(name[1]) + c
                        out_ap, otk = xs[:, idx, :], [tk_xs[idx]]
                    bank = next_bank()
                    proj_fm(wb, 128 * c, lambda kc: cur.v[:, kc, :], 8, cur.tk, bank)
                    pb_ = conv_chunk(bank, halo_x[:, idx, :], tk_hx[idx], 3, [0 + idx, 24 + idx, 48 + idx, 72 + idx], 96 + idx,
                                     first, out_ap, otk, "dve")
                    if pend[0] is not None:
                        pend[0]()
                    pend[0] = pb_
                w_rel(wb)
            if pend[0] is not None:
                pend[0]()

        def state_prep_chunk(j):
            cs = slice(128 * j, 128 * j + 128)
            pTb = psb[PG_][:, 0:256].bitcast(BF16)
            for g in range(4):
                S.op("pe", lambda e, g=g: e.transpose(out=pTb[:, 128 * g:128 * g + 128], in_=BCT[:, g, cs], identity=identb),
                     reads=[tk_BCT[g], tk_const], writes=[tk_ps[PG_]])
            S.op("act", lambda e: e.activation(out=Btok, in_=pTb, func=AF.Copy), reads=[tk_ps[PG_]], writes=[tk_Btok])
            for q4 in range(4):
                tb = (PT_, PY_)[q4 % 2]
                for ii in range(4):
                    i = 4 * q4 + ii
                    S.op("pe", lambda e, i=i, ii=ii, tb=tb: e.transpose(out=psb[tb][:, 128 * ii:128 * ii + 128], in_=xs[:, i, cs], identity=identf),
                         reads=[tk_xs[i], tk_const], writes=[tk_ps[tb]])
                S.op("act", lambda e, q4=q4, tb=tb: e.activation(out=xtok[:, j, 512 * q4:512 * q4 + 512], in_=psb[tb][:, 0:512], func=AF.Copy),
                     reads=[tk_ps[tb]], writes=[tk_xtok[j]])
            S.op("dve", lambda e: e.tensor_tensor(out=xw.rearrange("p (h c) -> p h c", h=NH), in0=xtok[:, j, :].rearrange("p (h c) -> p h c", h=NH),
                                                  in1=wtok[:, j, :].unsqueeze(2).to_broadcast([128, NH, 64]), op=ALU.mult),
                 reads=[tk_xtok[j], tk_wtok[j]], writes=[tk_xw])

        def state_update_chunk(j):
            S.op("dve", lambda e: e.tensor_scalar(out=rdA[0:96, :], in0=SEL[0:96, :], scalar1=S3[0:96, 128 * j + 127:128 * j + 128], scalar2=None, op0=ALU.mult),
                 reads=[tk_S3, tk_const], writes=[tk_rdA])
            S.op("pe", lambda e: e.matmul(psb[PX_][:, 256:288], lhsT=onesb[0:96, :], rhs=rdA[0:96, :], start=True, stop=True),
                 reads=[tk_rdA, tk_const], writes=[tk_ps[PX_]])
            S.op("act", lambda e: e.activation(out=dAt, in_=psb[PX_][:, 256:288], func=AF.Exp), reads=[tk_ps[PX_]], writes=[tk_dA])
            for g in range(4):
                gs = slice(512 * g, 512 * g + 512)
                S.op("dve", lambda e, g=g, gs=gs: e.tensor_tensor(out=hS[:, gs].rearrange("p (h c) -> p h c", h=8), in0=hS[:, gs].rearrange("p (h c) -> p h c", h=8),
                                                                 in1=dAt[:, 8 * g:8 * g + 8].unsqueeze(2).to_broadcast([128, 8, 64]), op=ALU.mult),
                     reads=[tk_dA, tk_hS[g]], writes=[tk_hS[g]])
            for g in range(4):
                gs = slice(512 * g, 512 * g + 512)
                bk = (PO_, 0)[g % 2]
                S.op("pe", lambda e, g=g, gs=gs, bk=bk: e.matmul(psb[bk][:, 0:512], lhsT=Btok[:, 128 * g:128 * g + 128], rhs=xw[:, gs], start=True, stop=True),
                     reads=[tk_Btok, tk_xw], writes=[tk_ps[bk]])
                S.op("dve", lambda e, gs=gs, bk=bk: e.tensor_tensor(out=hS[:, gs], in0=hS[:, gs], in1=psb[bk][:, 0:512], op=ALU.add),
                     reads=[tk_ps[bk], tk_hS[g]], writes=[tk_hS[g]])
                S.op("act", lambda e, gs=gs: e.activation(out=hSb[:, gs], in_=hS[:, gs], func=AF.Copy), reads=[tk_hS[g]], writes=[tk_hSb[g]])

        esg2 = f32v(o_u, 512)
        tk_esg2 = Tk()
        Et2 = f32v(o_tmp, 512)
        tk_E2 = Tk()
        esgs, tk_esgs = [esg, esg2], [tk_esg, tk_esg2]
        Ets, tk_Es = [Et, Et2], [tk_E, tk_E2]

        def ssd_y_chunk(j):
            cs = slice(128 * j, 128 * j + 128)
            for g in range(4):
                S.op("pe", lambda e, g=g: e.matmul(psb[PG_][:, 128 * g:128 * g + 128], lhsT=BCT[:, g, cs], rhs=BCT[:, 4 + g, cs], start=True, stop=True),
                     reads=[tk_BCT[g], tk_BCT[4 + g]], writes=[tk_ps[PG_]])
            for hq in range(8):
                psk = (PS_, 0)[hq % 2]
                eb, tke = esgs[hq % 2], tk_esgs[hq % 2]
                for hh in range(4):
                    h = 4 * hq + hh
                    o = psb[psk][:, 128 * hh:128 * hh + 128]
                    S.op("pe", lambda e, o=o: e.matmul(o, lhsT=identb, rhs=maskneg, start=True, stop=False),
                         reads=[tk_const], writes=[tk_ps[psk]])
                    S.op("pe", lambda e, o=o, h=h: e.matmul(o, lhsT=SEL[0:96, h:h + 1].to_broadcast([96, 128]), rhs=S3[0:96, cs], start=False, stop=False),
                         reads=[tk_S3, tk_const], writes=[tk_ps[psk]])
                    S.op("pe", lambda e, o=o, h=h: e.matmul(o, lhsT=nS3[0:96, cs], rhs=SEL[0:96, h:h + 1].to_broadcast([96, 128]), start=False, stop=True),
                         reads=[tk_nS3, tk_const], writes=[tk_ps[psk]])
                S.op("act", lambda e, psk=psk, eb=eb: e.activation(out=eb, in_=psb[psk][:, 0:512], func=AF.Exp), reads=[tk_ps[psk]], writes=[tke])
                g = hq // 2
                S.op("dve", lambda e, hq=hq, g=g, eb=eb: e.tensor_tensor(out=Mt[:, 4 * hq:4 * hq + 4, :], in0=eb.rearrange("p (h t) -> p h t", h=4),
                                                                        in1=psb[PG_][:, 128 * g:128 * g + 128].unsqueeze(1).to_broadcast([128, 4, 128]), op=ALU.mult),
                     reads=[tke, tk_ps[PG_]], writes=[tk_Mt[hq]])
            for pq in range(4):
                g = pq
                pxk = (PX_, 1)[pq % 2]
                Eb, tkE = Ets[pq % 2], tk_Es[pq % 2]
                for ii in range(4):
                    i = 4 * pq + ii
                    for hh in range(2):
                        h = 2 * i + hh
                        S.op("pe", lambda e, ii=ii, hh=hh, h=h, pxk=pxk: e.matmul(psb[pxk][64 * hh:64 * hh + 64, 128 * ii:128 * ii + 128], lhsT=SEL[0:96, h:h + 1].to_broadcast([96, 64]),
                                                                                  rhs=S3[0:96, cs], start=True, stop=True, tile_position=(0, 64 * hh)),
                             reads=[tk_S3, tk_const], writes=[tk_ps[pxk]])
                S.op("act", lambda e, pxk=pxk, Eb=Eb: e.activation(out=Eb, in_=psb[pxk][:, 0:512], func=AF.Exp), reads=[tk_ps[pxk]], writes=[tkE])
                for ii in range(4):
                    i = 4 * pq + ii
                    S.op("pe", lambda e, ii=ii, i=i, g=g: e.matmul(psb[PO_][:, 128 * ii:128 * ii + 128], lhsT=hSb[:, 128 * i:128 * i + 128], rhs=BCT[:, 4 + g, cs], start=True, stop=True),
                         reads=[tk_hSb[g], tk_BCT[4 + g]], writes=[tk_ps[PO_]])
                for ii in range(4):
                    i = 4 * pq + ii
                    for hh in range(2):
                        h = 2 * i + hh
                        S.op("pe", lambda e, ii=ii, hh=hh, h=h: e.matmul(psb[PY_][64 * hh:64 * hh + 64, 128 * ii:128 * ii + 128], lhsT=xtok[:, j, 64 * h:64 * h + 64],
                                                                         rhs=Mt[:, h, :], start=True, stop=True, tile_position=(0, 64 * hh)),
                             reads=[tk_xtok[j], tk_Mt[h // 4]], writes=[tk_ps[PY_]])
                S.op("dve", lambda e, Eb=Eb: e.tensor_tensor(out=t1b, in0=Eb, in1=psb[PO_][:, 0:512], op=ALU.mult), reads=[tkE, tk_ps[PO_]], writes=[tk_t1])
                for ii in range(4):
                    i = 4 * pq + ii
                    S.op("dve", lambda e, ii=ii, i=i: e.scalar_tensor_tensor(out=xs[:, i, cs], in0=xs[:, i, cs], scalar=PPc(120 + i), in1=t1b[:, 128 * ii:128 * ii + 128],
                                                                             op0=ALU.mult, op1=ALU.add),
                         reads=[tk_t1, tk_xs[i], tk_const], writes=[tk_xs[i]])
                S.op("dve", lambda e, pq=pq: e.tensor_tensor(out=xs[:, 4 * pq:4 * pq + 4, cs], in0=xs[:, 4 * pq:4 * pq + 4, cs],
                                                             in1=psb[PY_][:, 0:512].rearrange("p (i t) -> p i t", i=4), op=ALU.add),
                     reads=[tk_ps[PY_]] + tk_xs[4 * pq:4 * pq + 4], writes=tk_xs[4 * pq:4 * pq + 4])

        tk_dbg = []

        def dump(dst, src_ap, toks):
            t = Tk()
            tk_dbg.append(t)
            S.dma("pool", lambda e: e.dma_start(out=dst[:, :], in_=src_ap), "dbg", reads=toks, writes=[t])

        def tile_pre(ti):
            gt = ti
            cur.v, cur.tk = hTs[gt % 2], tk_hTs[gt % 2]
            dt_phase(mpre_d, ti * W, True)
            xbc_phase(tokmp if ti == 0 else None, False)
            if gt + 1 < NT_ALL:
                pf_load(gt + 1)
            for j in range(NCH):
                state_prep_chunk(j)
                state_update_chunk(j)
                if gt + 1 < NT_ALL:
                    pf_ln(j)
            if gt + 1 < NT_ALL:
                pf_transposes(gt + 1)

        def tile_main(ti):
            first = (ti == 0)
            gt = NT_PRE + ti
            cur.v, cur.tk = hTs[gt % 2], tk_hTs[gt % 2]
            for j in range(NCH):
                S.op("act", lambda e, j=j: e.activation(out=hres[:, j, :], in_=hstage[:, j, :], func=AF.Copy), reads=[tk_hst[j]], writes=[tk_hres[j]])
            dbgt = debug and ti == NT_MAIN - 1
            if dbgt:
                dump(dbg_h0, hres.rearrange('p j d -> p (j d)'), tk_hres)
            alias_tokens(tk_pT + tk_ypT, tk_ynT + tk_hst)
            for ub_i in range(2):
                wb = w_get(f"U{ub_i}")
                for c in range(4):
                    uc = 4 * ub_i + c
                    wwin = POOL_WINDOWS[uc // 2]
                    bank = next_bank()
                    proj_fm(wb, 128 * c, lambda kc: cur.v[:, kc, :], 8, cur.tk, bank)
                    u0i = 0 if uc % 2 == 0 else 3
                    u0 = ub[u0i]
                    S.op("pool", lambda e, u0=u0, uc=uc: e.tensor_copy(out=u0[:, 1:16], in_=halo_u[:, uc, :]), reads=[tk_hu[uc]], writes=[tk_ub[u0i]])
                    S.op("act", lambda e, u0=u0, bank=bank: e.activation(out=u0[:, 16:16 + W], in_=psb[bank][:, 0:W], func=AF.Copy), reads=[tk_ps[bank]], writes=[tk_ub[u0i]])
                    if first:
                        S.op("pool", lambda e, u0=u0: e.tensor_tensor(out=u0[:, 1:16 + W], in0=u0[:, 1:16 + W], in1=tokm0[:, 1:16 + W], op=ALU.mult),
                             reads=[tk_ub[u0i], tk_const], writes=[tk_ub[u0i]])
                    S.op("pool", lambda e, u0=u0, uc=uc: e.tensor_copy(out=halo_u[:, uc, :], in_=u0[:, 16 + W - 15:16 + W]), reads=[tk_ub[u0i]], writes=[tk_hu[uc]])
                    src, si = u0, u0i
                    k = 1
                    lo = 1
                    while k < wwin:
                        lo += k
                        di = 1 if si != 1 else 2
                        dst = ub[di]
                        S.op("pool", lambda e, src=src, dst=dst, lo=lo, k=k: e.tensor_tensor(out=dst[:, lo:16 + W], in0=src[:, lo:16 + W], in1=src[:, lo - k:16 + W - k], op=ALU.add),
                             reads=[tk_ub[si]], writes=[tk_ub[di]])
                        src, si = dst, di
                        k *= 2
                    if first:
                        S.op("dve", lambda e, src=src, uc=uc: e.tensor_tensor(out=src[:, 16:16 + W], in0=src[:, 16:16 + W], in1=invc0[:, uc // 2, :], op=ALU.mult),
                             reads=[tk_ub[si], tk_const, tk_rstd[uc // 2]], writes=[tk_ub[si]])
                        S.op("dve", lambda e, u0=u0, src=src, uc=uc: e.tensor_tensor(out=pT[:, uc, :], in0=src[:, 16:16 + W], in1=u0[:, 16:16 + W], op=ALU.subtract),
                             reads=[tk_ub[si], tk_ub[u0i]], writes=[tk_pT[uc]])
                    else:
                        S.op("dve", lambda e, u0=u0, src=src, uc=uc, wwin=wwin: e.scalar_tensor_tensor(out=pT[:, uc, :], in0=src[:, 16:16 + W], scalar=1.0 / wwin, in1=u0[:, 16:16 + W],
                                                                                                 op0=ALU.mult, op1=ALU.subtract),
                             reads=[tk_ub[si], tk_ub[u0i]], writes=[tk_pT[uc]])
                w_rel(wb)
            wb = w_get("PW")
            for oc in range(8):
                g, oh = oc // 2, oc % 2
                bank = next_bank()
                for kc in range(2):
                    mm(psb[bank][:, 0:W], wb.v[:, 2 * g + kc, 128 * oh:128 * oh + 128], pT[:, 2 * g + kc, :], kc == 0, kc == 1,
                       [wb.tk, tk_pT[2 * g + kc]], bank)
                S.op("act", lambda e, bank=bank, oc=oc: e.activation(out=ypT[:, oc, :], in_=psb[bank][:, 0:W], func=AF.Identity, scale=PPc(152 + oc)),
                     reads=[tk_ps[bank], tk_const], writes=[tk_ypT[oc]])
            w_rel(wb)
            for half in range(2):
                wpp = w_get(f"PP{half}")
                wgp = w_get(f"GP{half}")
                for c in range(4):
                    dc = 4 * half + c
                    b1 = next_bank()
                    proj_fm(wgp, 128 * c, lambda kc: cur.v[:, kc, :], 8, cur.tk, b1)
                    b0 = next_bank()
                    proj_fm(wpp, 128 * c, lambda kc: ypT[:, kc, :], 8, tk_ypT, b0)
                    ta = dc % 3
                    S.op("act", lambda e, b1=b1, ta=ta: e.activation(out=tmpA[ta], in_=psb[b1][:, 0:W], func=AF.Sigmoid), reads=[tk_ps[b1]], writes=[tk_tmpA[ta]])
                    S.op("dve", lambda e, b0=b0, ta=ta, dc=dc: e.tensor_tensor(out=poolpart[:, dc, :], in0=tmpA[ta], in1=psb[b0][:, 0:W], op=ALU.mult),
                         reads=[tk_ps[b0], tk_tmpA[ta]], writes=[tk_pp[dc]])
                w_rel(wpp)
                w_rel(wgp)
            if dbgt:
                dump(dbg_yp, ypT.rearrange('p i t -> p (i t)'), tk_ypT)
                dump(dbg_pp, poolpart.rearrange('p i t -> p (i t)'), tk_pp)
            alias_tokens(tk_xs, tk_act)
            alias_tokens(tk_xtok, tk_mrg)
            alias_tokens([tk_esg2], tk_ub[0:2])
            alias_tokens([tk_E2], tk_tmpA[0:2])
            dt_phase(mmain_d, ti * W, first)
            xbc_phase(tokm0 if first else None, True)
            if dbgt:
                dump(dbg_xs, xs.rearrange('p i t -> p (i t)'), tk_xs)
                dump(dbg_bc, BCT.rearrange('p i t -> p (i t)'), tk_BCT)
            for j in range(NCH):
                state_prep_chunk(j)
                ssd_y_chunk(j)
                state_update_chunk(j)
            alias_tokens(tk_ub[0:2], [tk_esg2])
            alias_tokens(tk_tmpA[0:2], [tk_E2])
            alias_tokens(tk_ynT, tk_pT + tk_ypT + tk_hst)
            def ones_mm(i):
                sb_, c, zb = i % 2, i % 4, i // 4
                S.op("pe", lambda e: e.matmul(psb[PX_][:, 0:W], lhsT=onesb, rhs=sqb[sb_], start=(c == 0), stop=(c == 3)),
                     reads=[tk_sq[sb_], tk_const], writes=[tk_ps[PX_]])
                if c == 3:
                    S.op("act", lambda e: e.activation(out=rstd[:, zb, :], in_=psb[PX_][:, 0:W], func=AF.Sqrt, bias=rmsepsc, scale=1.0 / 512.0),
                         reads=[tk_ps[PX_], tk_const], writes=[tk_rstd[zb]])
                    S.op("dve", lambda e: e.reciprocal(out=rstd[:, zb, :], in_=rstd[:, zb, :]), reads=[tk_rstd[zb]], writes=[tk_rstd[zb]])
            prev = None
            for zb in range(4):
                wb = w_get(f"Z{zb}")
                for c in range(4):
                    i = 4 * zb + c
                    bank = next_bank()
                    proj_fm(wb, 128 * c, lambda kc: cur.v[:, kc, :], 8, cur.tk, bank)
                    ta = i % 3
                    S.op("act", lambda e, bank=bank, ta=ta: e.activation(out=tmpA[ta], in_=psb[bank][:, 0:W], func=AF.Silu), reads=[tk_ps[bank]], writes=[tk_tmpA[ta]])
                    S.op("dve", lambda e, i=i, ta=ta: e.tensor_tensor(out=xs[:, i, :], in0=xs[:, i, :], in1=tmpA[ta], op=ALU.mult),
                         reads=[tk_tmpA[ta], tk_xs[i]], writes=[tk_xs[i]])
                    sb_ = i % 2
                    S.op("pool", lambda e, i=i, sb_=sb_: e.tensor_tensor(out=sqb[sb_], in0=xs[:, i, :], in1=xs[:, i, :], op=ALU.mult), reads=[tk_xs[i]], writes=[tk_sq[sb_]])
                    if prev is not None:
                        ones_mm(prev)
                    prev = i
                w_rel(wb)
            ones_mm(prev)
            for i in range(16):
                eng = "dve"
                S.op(eng, lambda e, i=i: e.scalar_tensor_tensor(out=ynT[:, i, :], in0=xs[:, i, :], scalar=PPc(136 + i), in1=rstd[:, i // 4, :], op0=ALU.mult, op1=ALU.mult),
                     reads=[tk_xs[i], tk_rstd[i // 4], tk_const], writes=[tk_ynT[i]])
            if dbgt:
                dump(dbg_yn, ynT.rearrange('p i t -> p (i t)'), tk_ynT)
            alias_tokens(tk_mrg, tk_xtok)
            wgs = None
            for q in range(4):
                wps = w_get(f"PS{q}")
                if q % 2 == 0:
                    wgs = w_get(f"GS{q // 2}")
                for c in range(2):
                    dc = 2 * q + c
                    b1 = next_bank()
                    proj_fm(wgs, 128 * (dc % 4), lambda kc: cur.v[:, kc, :], 8, cur.tk, b1)
                    b0 = next_bank()
                    proj_fm(wps, 128 * c, lambda kc: ynT[:, kc, :], 16, tk_ynT, b0)
                    ta = dc % 3
                    S.op("act", lambda e, b1=b1, ta=ta: e.activation(out=tmpA[ta], in_=psb[b1][:, 0:W], func=AF.Sigmoid), reads=[tk_ps[b1]], writes=[tk_tmpA[ta]])
                    S.op("dve", lambda e, b0=b0, ta=ta: e.tensor_tensor(out=tmpA[ta], in0=tmpA[ta], in1=psb[b0][:, 0:W], op=ALU.mult),
                         reads=[tk_ps[b0], tk_tmpA[ta]], writes=[tk_tmpA[ta]])
                    S.op("pool", lambda e, ta=ta, dc=dc: e.tensor_tensor(out=mergedT[:, dc, :], in0=tmpA[ta], in1=poolpart[:, dc, :], op=ALU.add),
                         reads=[tk_tmpA[ta], tk_pp[dc]], writes=[tk_mrg[dc]])
                w_rel(wps)
                if q % 2 == 1:
                    w_rel(wgs)
            if dbgt:
                dump(dbg_mg, mergedT.rearrange('p i t -> p (i t)'), tk_mrg)
            wo = [w_get("WO0"), w_get("WO1")]
            load_gb(1)
            for j in range(NCH):
                cs = slice(128 * j, 128 * j + 128)
                for half in range(2):
                    bank = next_bank()
                    for kc in range(8):
                        mm(psb[bank][:, 0:512], mergedT[:, kc, cs], wo[half].v[:, kc, :], kc == 0, kc == 7, [wo[half].tk, tk_mrg[kc]], bank)
                    S.op("dve", lambda e, bank=bank, half=half, j=j: e.scalar_tensor_tensor(out=hres[:, j, 512 * half:512 * half + 512], in0=hres[:, j, 512 * half:512 * half + 512],
                                                                                         scalar=ALPHA, in1=psb[bank][:, 0:512], op0=ALU.mult, op1=ALU.add),
                         reads=[tk_ps[bank], tk_hres[j]], writes=[tk_hres[j]])
                layernorm_chunk(hres, tk_hres, j)
                transpose_to(hres, tk_hres, j, cur.v, cur.tk)
            w_rel(wo[0])
            w_rel(wo[1])
            if dbgt:
                dump(dbg_h1, hres.rearrange('p j d -> p (j d)'), tk_hres)
            if gt + 1 < NT_ALL:
                alias_tokens(tk_hst, tk_ynT + tk_pT + tk_ypT)
                pf_load(gt + 1)
            alias_tokens(tk_act, tk_xs)
            pend = [None]
            for fb in range(11):
                wb = w_get(f"F{fb}")
                for c in range(2):
                    fi = 2 * fb + c
                    b0 = (0, 2, 4)[fi % 3]
                    b1 = (1, 3, 5)[fi % 3]
                    proj_fm(wb, 128 * c, lambda kc: cur.v[:, kc, :], 8, cur.tk, b0)
                    proj_fm(wb, 256 + 128 * c, lambda kc: cur.v[:, kc, :], 8, cur.tk, b1)
                    ta = fi % 3
                    pb_ = conv_chunk(b0, halo_f[:, fi, :], tk_hf[fi], 2, [160 + fi, 182 + fi, 204 + fi], 226 + fi, tokm0 if first else None, tmpA[ta], [tk_tmpA[ta]],
                                     "dve")

                    def partB2(pb_=pb_, b1=b1, ta=ta, fi=fi):
                        pb_()
                        S.op("dve", lambda e: e.tensor_tensor(out=actT[:, fi, :], in0=tmpA[ta], in1=psb[b1][:, 0:W], op=ALU.mult),
                             reads=[tk_ps[b1], tk_tmpA[ta]], writes=[tk_act[fi]])
                    if pend[0] is not None:
                        pend[0]()
                    pend[0] = partB2
                w_rel(wb)
            pend[0]()
            if dbgt:
                dump(dbg_act, actT.rearrange('p i t -> p (i t)'), tk_act)
            for q in range(4):
                wb = w_get(f"FD{q}")
                for j in range(NCH):
                    cs = slice(128 * j, 128 * j + 128)
                    bank = next_bank()
                    for kc in range(22):
                        mm(psb[bank][:, 0:256], actT[:, kc, cs], wb.v[:, kc, :], kc == 0, kc == 21, [wb.tk, tk_act[kc]], bank)
                    S.op("dve", lambda e, bank=bank, q=q, j=j: e.scalar_tensor_tensor(out=hres[:, j, 256 * q:256 * q + 256], in0=hres[:, j, 256 * q:256 * q + 256],
                                                                                   scalar=ALPHA, in1=psb[bank][:, 0:256], op0=ALU.mult, op1=ALU.add),
                         reads=[tk_ps[bank], tk_hres[j]], writes=[tk_hres[j]])
                w_rel(wb)
                if gt + 1 < NT_ALL and q < NCH:
                    pf_ln(q)
            load_gb(2)
            if gt + 1 < NT_ALL:
                pf_transposes(gt + 1)
            for j in range(NCH):
                layernorm_chunk(hres, tk_hres, j)
                ch = ti * NCH + j
                if ch >= 1:
                    r0 = (ch - 1) * 128
                    ob = j
                    S.dma("pool", lambda e, j=j, r0=r0: e.dma_start(out=y_d[r0:r0 + 128, :], in_=hres[:, j, :]), f"out{ob}", reads=[tk_hres[j]])

        print('arena words used', off[0], flush=True)
        pf_compute(0)
        pf_transposes(0)
        for ti in range(NT_PRE):
            tile_pre(ti)
        for ti in range(NT_MAIN):
            tile_main(ti)
        S.final_wait("pool", tk_hres + tk_dbg)
        S.emit(block)
    return nc


def make_pp(inp):
    pp = np.zeros((128, 256), np.float32)
    cw = np.asarray(inp["ssm_conv_w"])[0]
    cb = np.asarray(inp["ssm_conv_b"])[0]
    for k in range(4):
        pp[:, 24 * k:24 * k + 24] = cw[k].reshape(24, 128).T
    pp[:, 96:120] = cb.reshape(24, 128).T
    dsk = np.repeat(np.asarray(inp["ssm_d"])[0], 64)
    pp[:, 120:136] = dsk.reshape(16, 128).T
    pp[:, 136:152] = np.asarray(inp["ssm_norm_w"])[0].reshape(16, 128).T
    pp[:, 152:160] = np.asarray(inp["pool_scale"])[0].reshape(8, 128).T
    fw_ = np.asarray(inp["ffn_conv_w"])[0]
    for k in range(3):
        pp[:, 160 + 22 * k:160 + 22 * k + 22] = fw_[k].reshape(22, 128).T
    pp[:, 226:248] = np.asarray(inp["ffn_conv_b"])[0].reshape(22, 128).T
    pp[0:96, 248] = np.tile(np.asarray(inp["ssm_dt_bias"])[0], 3)
    pp[0:96, 249] = np.tile(np.asarray(inp["ssm_a_log"])[0], 3)
    return pp


def core_inputs(inp, b, hf, NT_PRE, NT_MAIN, shared):
    TP, TM = NT_PRE * W, NT_MAIN * W
    Q1 = TP - 128
    x = np.asarray(inp["x"])[b]
    meta = np.asarray(inp["meta_tokens"])
    seq_len = x.shape[0]

    def rows(q0, n):
        out = np.zeros((n, D), np.float32)
        msk = np.zeros((n,), np.float32)
        for lo, hi, src, s0 in ((112, 128, meta, 0), (128, 128 + seq_len, x, 0)):
            a = max(q0, lo)
            bnd = min(q0 + n, hi)
            if a < bnd:
                out[a - q0:bnd - q0] = src[a - lo:bnd - lo]
                msk[a - q0:bnd - q0] = 1.0
        return out, msk
    if hf == 0:
        xpre = np.zeros((TP, D), np.float32)
        mpre = np.zeros((TP,), np.float32)
        xmain, mmain = rows(0, TM)
    else:
        xpre, mpre = rows(Q1 - TP, TP)
        xmain, mmain = rows(Q1, TM)
    tokmask0 = np.ones((128, 16 + W), np.float32)
    tokmask0[:, 0:16] = float(hf)
    tokmask0[:, 16:] = mmain[None, 0:W]
    tokmaskp = np.zeros((128, 16 + W), np.float32)
    tokmaskp[:, 16:] = mpre[None, 0:W]
    invcnt0 = np.zeros((128, 4, W), np.float32)
    q0 = 0 if hf == 0 else Q1
    lpos = np.arange(q0, q0 + W) - 112
    for g, wdw in enumerate(POOL_WINDOWS):
        cnt = np.minimum(np.maximum(lpos + 1, 1), wdw).astype(np.float32)
        invcnt0[:, g, :] = (1.0 / cnt)[None, :]
    d = dict(shared)
    d.update({
        "xpre": xpre, "xmain": xmain,
        "mpre": np.ascontiguousarray(np.broadcast_to(mpre[None, :], (96, TP))),
        "mmain": np.ascontiguousarray(np.broadcast_to(mmain[None, :], (96, TM))),
        "tokmask0": tokmask0, "invcnt0": invcnt0.reshape(128, 4 * W), "tokmaskp": tokmaskp,
    })
    return d


def shared_inputs(inp):
    lnrows = np.stack([
        np.concatenate([np.asarray(inp["ln_in_g"]), np.asarray(inp["ln_in_b"])]),
        np.concatenate([np.asarray(inp["ln1_g"])[0], np.asarray(inp["ln1_b"])[0]]),
        np.concatenate([np.asarray(inp["ln2_g"])[0], np.asarray(inp["ln2_b"])[0]]),
    ]).astype(np.float32)
    return {
        "pp": make_pp(inp), "lnrows": lnrows,
        "w_in": np.ascontiguousarray(np.asarray(inp["w_in"])[0]),
        "w_proj_ssm": np.ascontiguousarray(np.asarray(inp["w_proj_ssm"])[0]),
        "w_proj_pool": np.ascontiguousarray(np.asarray(inp["w_proj_pool"])[0]),
        "pool_w": np.ascontiguousarray(np.asarray(inp["pool_w"])[0]),
        "w_out": np.ascontiguousarray(np.asarray(inp["w_out"])[0]),
        "ffn_w_up": np.ascontiguousarray(np.asarray(inp["ffn_w_up"])[0]),
        "ffn_w_down": np.ascontiguousarray(np.asarray(inp["ffn_w_down"])[0]),
    }


_NC_CACHE = {}


def kernel(**inputs):
    NT_PRE, NT_MAIN = 11, 11
    key = (NT_PRE, NT_MAIN)
    if key not in _NC_CACHE:
        _NC_CACHE[key] = build_program(NT_PRE, NT_MAIN)
    nc = _NC_CACHE[key]
    shared = shared_inputs(inputs)
    B = np.asarray(inputs["x"]).shape[0]
    in_maps = []
    for core in range(8):
        b, hf = core // 2, core % 2
        in_maps.append(core_inputs(inputs, b, hf, NT_PRE, NT_MAIN, shared))
    res = run_bass_kernel_spmd(nc, in_maps, core_ids=list(range(8)))
    out = np.zeros((B, 8192, D), np.float32)
    for core in range(8):
        b, hf = core // 2, core % 2
        out[b, 4096 * hf:4096 * hf + 4096] = res.results[core]["y"]
    return out
```

```python
import numpy as np
from contextlib import ExitStack
import concourse.bass as bass
import concourse.mybir as mybir
from concourse.bass_utils import run_bass_kernel_spmd

F32 = mybir.dt.float32
BF16 = mybir.dt.bfloat16
ALU = mybir.AluOpType
AF = mybir.ActivationFunctionType

D = 1024
DI = 2048
NH = 32
W = 384
NCH = 3
ALPHA = 2.0 ** 0.25
LN_EPS = 1e-5
RMS_EPS = 1e-5
POOL_WINDOWS = (2, 4, 8, 16)
DFF = 2816
SLOTW = 5632
NSLOT = 3


class Tk:
    __slots__ = ("w", "rs")

    def __init__(self):
        self.w = None
        self.rs = []


def alias_tokens(new, old):
    deps = []
    for t in old:
        if t.w is not None:
            deps.append(t.w)
        deps.extend(t.rs)
    for n in new:
        n.rs = list(n.rs) + deps


class Sync:
    ENG = ("pe", "act", "dve", "pool", "sp")

    def __init__(self):
        self.prog = {e: [] for e in self.ENG}
        self.cnt = {}
        self.seen = {e: {} for e in self.ENG}
        self.sems = {}

    def add_sem(self, key, handle):
        self.sems[key] = handle
        self.cnt[key] = 0

    def _deps(self, eng, reads, writes, pe_chain=False):
        deps = {}

        def add(d):
            if d is None:
                return
            k, v = d
            if deps.get(k, 0) < v:
                deps[k] = v
        for t in reads:
            add(t.w)
        for t in writes:
            add(t.w)
            for r in t.rs:
                add(r)
        out = []
        for k, v in deps.items():
            if k == "pe" and eng == "pe":
                continue
            if self.seen[eng].get(k, 0) < v:
                self.seen[eng][k] = v
                out.append((k, v))
        return out

    def op(self, eng, fn, reads=(), writes=()):
        waits = self._deps(eng, reads, writes)
        self.cnt[eng] += 1
        me = (eng, self.cnt[eng])
        self.prog[eng].append((waits, fn, (eng, 1)))
        for t in reads:
            t.rs.append(me)
            if len(t.rs) > 64:
                t.rs = _compact(t.rs)
        for t in writes:
            t.w = me
            t.rs = []

    def dma(self, eng, fn, semkey, reads=(), writes=()):
        waits = self._deps(eng, reads, writes)
        self.cnt[semkey] += 16
        me = (semkey, self.cnt[semkey])
        self.prog[eng].append((waits, fn, (semkey, 16)))
        for t in reads:
            t.rs.append(me)
        for t in writes:
            t.w = me
            t.rs = []

    def final_wait(self, eng, toks):
        waits = self._deps(eng, (), toks)
        self.prog[eng].append((waits, None, None))

    def emit(self, block):
        sems = self.sems

        def run(engname):
            def body(e):
                for waits, fn, inc in self.prog[engname]:
                    for k, v in waits:
                        e.wait_ge(sems[k], v)
                    if fn is not None:
                        fn(e).then_inc(sems[inc[0]], inc[1])
            return body
        block.tensor(run("pe"))
        block.scalar(run("act"))
        block.vector(run("dve"))
        block.gpsimd(run("pool"))
        block.sync(run("sp"))


def _compact(rs):
    best = {}
    for k, v in rs:
        if best.get(k, 0) < v:
            best[k] = v
    return list(best.items())


C_Z, C_X, C_B, C_C, C_DT, C_U, C_GS, C_GP = 0, 2048, 4096, 4608, 5120, 5152, 6176, 7200

BLK_NAMES = (["U0", "U1", "PW", "PP0", "PP1", "GP0", "GP1", "DT", "BC0", "BC1"]
             + [f"X{i}" for i in range(4)] + [f"Z{i}" for i in range(4)]
             + [f"PS{i}" for i in range(4)] + ["GS0", "GS1", "WO0", "WO1"]
             + [f"F{i}" for i in range(11)] + [f"FD{i}" for i in range(4)])
BLK_ID = {n: i for i, n in enumerate(BLK_NAMES)}
NBLK = len(BLK_NAMES)

MAIN_SEQ = (["U0", "U1", "PW", "PP0", "GP0", "PP1", "GP1", "DT", "BC0", "BC1", "X0", "X1", "X2", "X3",
             "Z0", "Z1", "Z2", "Z3", "PS0", "GS0", "PS1", "PS2", "GS1", "PS3", "WO0", "WO1"]
            + [f"F{i}" for i in range(11)] + [f"FD{i}" for i in range(4)])
PRE_SEQ = ["DT", "BC0", "X0", "X1", "X2", "X3"]


def build_program(NT_PRE, NT_MAIN, debug=False):
    TP = NT_PRE * W
    TM = NT_MAIN * W
    NOUT = (NT_MAIN * NCH - 1) * 128
    nc = bass.Bass("TRN2", target_bir_lowering=False)

    def din(name, shape, dt=F32):
        return nc.dram_tensor(name, list(shape), dt, kind="ExternalInput").ap()
    xpre_d = din("xpre", [TP, D])
    xmain_d = din("xmain", [TM, D])
    mpre_d = din("mpre", [96, TP])
    mmain_d = din("mmain", [96, TM])
    tm0_d = din("tokmask0", [128, 16 + W])
    ic0_d = din("invcnt0", [128, 4 * W])
    tmp_d = din("tokmaskp", [128, 16 + W])
    pp_d = din("pp", [128, 256])
    lnrow_d = din("lnrows", [3, 2 * D])
    w_in_d = din("w_in", [D, 8224])
    w_ps_d = din("w_proj_ssm", [DI, D])
    w_pp_d = din("w_proj_pool", [D, D])
    w_pw_d = din("pool_w", [4, 256, 256])
    w_o_d = din("w_out", [D, D])
    w_up_d = din("ffn_w_up", [D, 2 * DFF])
    w_dn_d = din("ffn_w_down", [DFF, D])
    y_d = nc.dram_tensor("y", [NOUT, D], F32, kind="ExternalOutput").ap()
    wsc = nc.dram_tensor("wsc", [NBLK, 128, SLOTW], BF16, kind="Internal").ap()
    if debug:
        dbg_yn = nc.dram_tensor("dbg_yn", [128, 16 * W], BF16, kind="ExternalOutput").ap()
        dbg_yp = nc.dram_tensor("dbg_yp", [128, 8 * W], BF16, kind="ExternalOutput").ap()
        dbg_pp = nc.dram_tensor("dbg_pp", [128, 8 * W], F32, kind="ExternalOutput").ap()
        dbg_mg = nc.dram_tensor("dbg_mg", [128, 8 * W], BF16, kind="ExternalOutput").ap()
        dbg_h1 = nc.dram_tensor("dbg_h1", [128, 3 * D], F32, kind="ExternalOutput").ap()
        dbg_h0 = nc.dram_tensor("dbg_h0", [128, 3 * D], F32, kind="ExternalOutput").ap()
        dbg_act = nc.dram_tensor("dbg_act", [128, 22 * W], BF16, kind="ExternalOutput").ap()
        dbg_xs = nc.dram_tensor("dbg_xs", [128, 16 * W], F32, kind="ExternalOutput").ap()
        dbg_bc = nc.dram_tensor("dbg_bc", [128, 8 * W], BF16, kind="ExternalOutput").ap()

    es = ExitStack()
    with es:
        ARENA_WORDS = 53100
        arena = es.enter_context(nc.sbuf_tensor("arena", [128, ARENA_WORDS], F32))
        psb = [es.enter_context(nc.psum_tensor(f"psb{i}", [128, 512], F32)) for i in range(8)]
        tk_ps = [Tk() for _ in range(8)]
        S = Sync()
        for e in Sync.ENG:
            S.add_sem(e, es.enter_context(nc.semaphore("s_" + e)))
        DMA_KEYS = (["wl0", "wl1", "wl2", "cvi0", "cvi1", "cvo0", "cvo1", "cvo2", "xin0", "xin1", "xin2",
                     "gb", "msk", "out0", "out1", "out2", "setup", "dbg"])
        for k in DMA_KEYS:
            S.add_sem(k, es.enter_context(nc.semaphore("s_" + k)))
        block = es.enter_context(nc.Block())

        off = [0]

        def alloc(nwords):
            o = off[0]
            off[0] += nwords
            assert off[0] <= ARENA_WORDS, off[0]
            return o

        def f32v(o, n):
            return arena[:, o:o + n]

        def bfv(o, nbf):
            return arena[:, o:o + nbf // 2].bitcast(BF16)

        o_hres = alloc(3 * D)
        hres = f32v(o_hres, 3 * D).rearrange("p (j d) -> p j d", j=3)
        tk_hres = [Tk() for _ in range(3)]
        o_gb = alloc(2 * D)
        gbt = f32v(o_gb, 2 * D)
        tk_gb = Tk()
        hTs = [bfv(alloc(8 * W // 2), 8 * W).rearrange("p (k t) -> p k t", k=8) for _ in range(2)]
        tk_hTs = [[Tk() for _ in range(3)] for _ in range(2)]

        class Cur:
            pass
        cur = Cur()
        cur.v, cur.tk = hTs[0], tk_hTs[0]
        o_RA = alloc(16 * W)
        xs = f32v(o_RA, 16 * W).rearrange("p (i t) -> p i t", i=16)
        actT = bfv(o_RA, 22 * W).rearrange("p (i t) -> p i t", i=22)
        tk_xs = [Tk() for _ in range(16)]
        tk_act = [Tk() for _ in range(22)]
        o_RB = alloc(8 * W)
        pT = bfv(o_RB, 8 * W).rearrange("p (i t) -> p i t", i=8)
        ypT = bfv(o_RB + 4 * W, 8 * W).rearrange("p (i t) -> p i t", i=8)
        ynT = bfv(o_RB, 16 * W).rearrange("p (i t) -> p i t", i=16)
        tk_pT = [Tk() for _ in range(8)]
        hstage = f32v(o_RB, 3 * D).rearrange("p (j d) -> p j d", j=3)
        tk_hst = [Tk() for _ in range(3)]
        tk_ypT = [Tk() for _ in range(8)]
        tk_ynT = [Tk() for _ in range(16)]
        o_PP = alloc(8 * W)
        poolpart = f32v(o_PP, 8 * W).rearrange("p (i t) -> p i t", i=8)
        tk_pp = [Tk() for _ in range(8)]
        stg = [f32v(o_RA, SLOTW), f32v(o_RB, SLOTW)]
        tk_stg = [Tk(), Tk()]
        o_RC = alloc(3 * DI // 2)
        xtok = bfv(o_RC, 3 * DI).rearrange("p (j c) -> p j c", j=3)
        mergedT = bfv(o_RC, 8 * W).rearrange("p (i t) -> p i t", i=8)
        tk_xtok = [Tk() for _ in range(3)]
        tk_mrg = [Tk() for _ in range(8)]
        BCT = bfv(alloc(8 * W // 2), 8 * W).rearrange("p (i t) -> p i t", i=8)
        tk_BCT = [Tk() for _ in range(8)]
        RAWW = 16 + W
        o_raw = alloc(2 * RAWW)
        raw = [f32v(o_raw, RAWW), f32v(o_raw + RAWW, RAWW)]
        tk_raw = [Tk(), Tk()]
        o_acc = alloc(2 * W)
        cacc = [f32v(o_acc, W), f32v(o_acc + W, W)]
        tk_acc = [Tk(), Tk()]
        halo_x = f32v(alloc(24 * 3), 72).rearrange("p (i k) -> p i k", i=24)
        halo_u = f32v(alloc(8 * 15), 120).rearrange("p (i k) -> p i k", i=8)
        halo_f = f32v(alloc(22 * 2), 44).rearrange("p (i k) -> p i k", i=22)
        tk_hx = [Tk() for _ in range(24)]
        tk_hu = [Tk() for _ in range(8)]
        tk_hf = [Tk() for _ in range(22)]
        xw = bfv(alloc(DI // 2), DI)
        tk_xw = Tk()
        Btok = bfv(alloc(256), 512)
        tk_Btok = Tk()
        Mt = bfv(alloc(NH * 128 // 2), NH * 128).rearrange("p (h t) -> p h t", h=NH)
        tk_Mt = [Tk() for _ in range(8)]
        esg = f32v(alloc(512), 512)
        tk_esg = Tk()
        Et = f32v(alloc(512), 512)
        tk_E = Tk()
        t1b = esg
        tk_t1 = tk_esg
        hS = f32v(alloc(DI), DI)
        tk_hS = [Tk() for _ in range(4)]
        hSb = bfv(alloc(DI // 2), DI)
        tk_hSb = [Tk() for _ in range(4)]
        o_tmp = alloc(3 * W)
        tmpA = [f32v(o_tmp + i * W, W) for i in range(3)]
        tk_tmpA = [Tk() for _ in range(3)]
        o_sq = alloc(W)
        sqb = [bfv(o_sq, W), bfv(o_sq + W // 2, W)]
        tk_sq = [Tk(), Tk()]
        rstd = f32v(alloc(4 * W), 4 * W).rearrange("p (g t) -> p g t", g=4)
        tk_rstd = [Tk() for _ in range(4)]
        UW = 16 + W
        o_u = alloc(4 * UW)
        ub = [f32v(o_u + i * UW, UW) for i in range(4)]
        tk_ub = [Tk() for _ in range(4)]
        NDT = 8
        o_dt = alloc(NDT * W)
        dtc = [f32v(o_dt + i * W, W) for i in range(NDT)]
        tk_dtc = [Tk() for _ in range(NDT)]
        S3 = bfv(alloc(W // 2), W)
        nS3 = bfv(alloc(W // 2), W)
        tk_S3, tk_nS3 = Tk(), Tk()
        wtok = f32v(alloc(3 * 32), 96).rearrange("p (j h) -> p j h", j=3)
        tk_wtok = [Tk() for _ in range(3)]
        dAt = f32v(alloc(32), 32)
        tk_dA = Tk()
        rdA = bfv(alloc(16), 32)
        tk_rdA = Tk()
        st12s = [f32v(alloc(16), 16) for _ in range(3)]
        tk_sts = [Tk() for _ in range(3)]
        mvs = [f32v(alloc(8), 8) for _ in range(3)]
        tk_mvs = [Tk() for _ in range(3)]
        dmask = f32v(alloc(W), W)
        tk_dmask = Tk()
        tokm0 = f32v(alloc(16 + W), 16 + W)
        invc0 = rstd
        ppt = f32v(alloc(256), 256)
        identf = f32v(alloc(128), 128)
        identb = bfv(alloc(64), 128)
        onesb = bfv(alloc(64), 128)
        SEL = bfv(alloc(16), 32)
        maskneg = bfv(alloc(64), 128)
        aneg = f32v(alloc(2), 2)
        tk_const = Tk()
        o_slot = alloc(NSLOT * SLOTW // 2)
        slots = [bfv(o_slot + i * SLOTW // 2, SLOTW) for i in range(NSLOT)]
        tk_slot = [Tk() for _ in range(NSLOT)]
        tk_blk = [Tk() for _ in range(NBLK)]

        def PPc(c):
            return ppt[:, c:c + 1]

        tokmp = f32v(alloc(16 + W), 16 + W)
        epst = f32v(alloc(2), 2)
        epsc = epst[:, 0:1]
        rmsepsc = epst[:, 1:2]
        tk_cdma = Tk()
        tk_k = Tk()
        S.dma("sp", lambda e: e.dma_start(out=ppt, in_=pp_d[:, :]), "setup", writes=[tk_cdma])
        S.dma("sp", lambda e: e.dma_start(out=tokm0, in_=tm0_d[:, :]), "setup", writes=[tk_cdma])
        S.dma("sp", lambda e: e.dma_start(out=tokmp, in_=tmp_d[:, :]), "setup", writes=[tk_cdma])
        S.dma("sp", lambda e: e.dma_start(out=invc0.rearrange("p g t -> p (g t)"), in_=ic0_d[:, :]), "setup", writes=[tk_cdma])
        tk_cdma.w = ("setup", S.cnt["setup"])
        for t_ in tk_rstd:
            t_.w = tk_cdma.w
        S.op("pool", lambda e: e.memset(identf, 1.0), writes=[tk_k])
        S.op("pool", lambda e: e.affine_select(out=identf, in_=identf, pattern=[[1, 128]], compare_op=ALU.is_equal, fill=0.0, base=0, channel_multiplier=-1), reads=[tk_k], writes=[tk_k])
        S.op("pool", lambda e: e.tensor_copy(out=identb, in_=identf), reads=[tk_k], writes=[tk_k])
        S.op("pool", lambda e: e.memset(onesb, 1.0), writes=[tk_k])
        S.op("pool", lambda e: e.memset(epst[:, 0:1], LN_EPS), writes=[tk_k])
        S.op("pool", lambda e: e.memset(epst[:, 1:2], RMS_EPS), writes=[tk_k])
        S.op("pool", lambda e: e.memset(maskneg, 0.0), writes=[tk_k])
        S.op("pool", lambda e: e.affine_select(out=maskneg, in_=maskneg, pattern=[[1, 128]], compare_op=ALU.is_ge, fill=-30000.0, base=0, channel_multiplier=-1), reads=[tk_k], writes=[tk_k])
        S.op("pool", lambda e: e.memset(SEL, 0.0), writes=[tk_k])
        for r in range(3):
            S.op("pool", lambda e, r=r: e.tensor_copy(out=SEL[32 * r:32 * r + 32, :], in_=identf[32 * r:32 * r + 32, 32 * r:32 * r + 32]), reads=[tk_k], writes=[tk_k])
        S.op("pool", lambda e: e.memset(hS, 0.0), writes=tk_hS)
        S.op("pool", lambda e: e.memset(hSb, 0.0), writes=tk_hSb)
        S.op("pool", lambda e: e.memset(halo_x.rearrange("p i k -> p (i k)"), 0.0), writes=tk_hx)
        S.op("pool", lambda e: e.memset(halo_u.rearrange("p i k -> p (i k)"), 0.0), writes=tk_hu)
        S.op("pool", lambda e: e.memset(halo_f.rearrange("p i k -> p (i k)"), 0.0), writes=tk_hf)
        S.op("act", lambda e: e.activation(out=aneg[:, 0:1], in_=PPc(249), func=AF.Exp), reads=[tk_cdma], writes=[tk_k])
        S.op("dve", lambda e: e.tensor_scalar(out=aneg[:, 1:2], in0=aneg[:, 0:1], scalar1=-1.0, scalar2=None, op0=ALU.mult), reads=[tk_k], writes=[tk_k])
        S.op("pool", lambda e: e.memset(aneg[:, 0:1], 0.0), reads=[tk_cdma, tk_k], writes=[tk_const])

        def kc_view(ap2d, kc):
            return ap2d.rearrange("(k p) c -> p k c", p=128)

        def conv_sources(name):
            if name in ("U0", "U1"):
                c0 = C_U + 512 * int(name[1])
                return 8, 512, [((0, 512), kc_view(w_in_d[:, c0:c0 + 512], 8))]
            if name in ("GP0", "GP1"):
                c0 = C_GP + 512 * int(name[2])
                return 8, 512, [((0, 512), kc_view(w_in_d[:, c0:c0 + 512], 8))]
            if name in ("GS0", "GS1"):
                c0 = C_GS + 512 * int(name[2])
                return 8, 512, [((0, 512), kc_view(w_in_d[:, c0:c0 + 512], 8))]
            if name in ("BC0", "BC1"):
                c0 = C_B + 512 * int(name[2])
                return 8, 512, [((0, 512), kc_view(w_in_d[:, c0:c0 + 512], 8))]
            if name[0] == "X":
                c0 = C_X + 512 * int(name[1])
                return 8, 512, [((0, 512), kc_view(w_in_d[:, c0:c0 + 512], 8))]
            if name[0] == "Z":
                c0 = C_Z + 512 * int(name[1])
                return 8, 512, [((0, 512), kc_view(w_in_d[:, c0:c0 + 512], 8))]
            if name == "DT":
                return 8, 96, [((32 * r, 32 * r + 32), kc_view(w_in_d[:, C_DT:C_DT + 32], 8)) for r in range(3)]
            if name[:2] == "PS":
                q = int(name[2])
                return 16, 256, [((0, 256), kc_view(w_ps_d[:, 256 * q:256 * q + 256], 16))]
            if name[:2] == "PP":
                q = int(name[2])
                return 8, 512, [((0, 512), kc_view(w_pp_d[:, 512 * q:512 * q + 512], 8))]
            if name == "PW":
                return 8, 256, [((0, 256), w_pw_d.rearrange("g (k p) c -> p (g k) c", p=128))]
            if name[:2] == "WO":
                q = int(name[2])
                return 8, 512, [((0, 512), kc_view(w_o_d[:, 512 * q:512 * q + 512], 8))]
            if name[:2] == "FD":
                q = int(name[2])
                return 22, 256, [((0, 256), kc_view(w_dn_d[:, 256 * q:256 * q + 256], 22))]
            if name[0] == "F":
                fb = int(name[1:])
                return 8, 512, [((0, 256), kc_view(w_up_d[:, 256 * fb:256 * fb + 256], 8)),
                                ((256, 512), kc_view(w_up_d[:, DFF + 256 * fb:DFF + 256 * fb + 256], 8))]
            raise KeyError(name)

        BLK_SHAPE = {}
        cast_engs = ["dve", "act", "dve"]
        for bi, name in enumerate(BLK_NAMES):
            KC, CB, srcs = conv_sources(name)
            BLK_SHAPE[name] = (KC, CB)
            n = KC * CB
            si = bi % 2
            sl = bi % NSLOT
            sview = stg[si][:, 0:n].rearrange("p (k c) -> p k c", k=KC)
            for (c0, c1), src in srcs:
                S.dma("sp", lambda e, sview=sview, c0=c0, c1=c1, src=src: e.dma_start(out=sview[:, :, c0:c1], in_=src),
                      f"cvi{si}", writes=[tk_stg[si]])
            ce = cast_engs[bi % 3]
            if ce == "act":
                S.op("act", lambda e, sl=sl, si=si, n=n: e.activation(out=slots[sl][:, 0:n], in_=stg[si][:, 0:n], func=AF.Copy),
                     reads=[tk_stg[si]], writes=[tk_slot[sl]])
            else:
                S.op(ce, lambda e, sl=sl, si=si, n=n: e.tensor_copy(out=slots[sl][:, 0:n], in_=stg[si][:, 0:n]),
                     reads=[tk_stg[si]], writes=[tk_slot[sl]])
            S.dma("act", lambda e, sl=sl, bi=bi, n=n: e.dma_start(out=wsc[bi, :, 0:n], in_=slots[sl][:, 0:n]),
                  f"cvo{sl}", reads=[tk_slot[sl]], writes=[tk_blk[bi]])
        alias_tokens(tk_xs + tk_act, [tk_stg[0]])
        alias_tokens(tk_pT + tk_ypT + tk_ynT + tk_pp + tk_hst, [tk_stg[1]])

        seq = []
        for _ in range(NT_PRE):
            seq += PRE_SEQ
        for _ in range(NT_MAIN):
            seq += MAIN_SEQ
        wst = {"issued": 0, "next": 0}

        def w_issue():
            k = wst["issued"]
            if k >= len(seq):
                return
            name = seq[k]
            bi = BLK_ID[name]
            KC, CB = BLK_SHAPE[name]
            n = KC * CB
            sl = k % NSLOT
            S.dma("sp", lambda e, sl=sl, bi=bi, n=n: e.dma_start(out=slots[sl][:, 0:n], in_=wsc[bi, :, 0:n]),
                  f"wl{sl}", reads=[tk_blk[bi]], writes=[tk_slot[sl]])
            wst["issued"] += 1

        class WB:
            pass

        def w_get(name):
            k = wst["next"]
            assert seq[k] == name, (k, seq[k], name)
            assert k < wst["issued"], (k, wst["issued"])
            wst["next"] += 1
            KC, CB = BLK_SHAPE[name]
            b = WB()
            b.k = k
            b.tk = tk_slot[k % NSLOT]
            b.v = slots[k % NSLOT][:, 0:KC * CB].rearrange("p (k c) -> p k c", k=KC)
            return b

        def w_rel(b):
            w_issue()

        for _ in range(NSLOT):
            w_issue()

        bank_rr = [0]

        def next_bank():
            b = bank_rr[0]
            bank_rr[0] ^= 1
            return b
        PT_, PS_, PG_, PY_, PO_, PX_ = 2, 3, 4, 5, 6, 7

        def mm(out_ap, lhsT, rhs, start, stop, reads, bank, **kw):
            S.op("pe", lambda e: e.matmul(out_ap, lhsT=lhsT, rhs=rhs, start=start, stop=stop, **kw),
                 reads=reads, writes=[tk_ps[bank]])

        def proj_fm(wb, col0, rhs_fn, KC, rhs_tks, bank, ncols=128, wcols=W):
            for kc in range(KC):
                mm(psb[bank][0:ncols, 0:wcols], wb.v[:, kc, col0:col0 + ncols], rhs_fn(kc), kc == 0, kc == KC - 1,
                   [wb.tk] + rhs_tks, bank)

        def layernorm_chunk(buf, tks, j):
            hj = buf[:, j, :]
            st12, tk_st, mv, tk_mv = st12s[j], tk_sts[j], mvs[j], tk_mvs[j]
            for h2 in range(2):
                S.op("dve", lambda e, h2=h2: e.bn_stats(out=st12[:, 6 * h2:6 * h2 + 6], in_=buf[:, j, 512 * h2:512 * h2 + 512]),
                     reads=[tks[j]], writes=[tk_st])
            S.op("dve", lambda e: e.bn_aggr(out=mv[:, 0:2], in_=st12[:, 0:12]),
                 reads=[tk_st], writes=[tk_mv])
            S.op("act", lambda e: e.activation(out=mv[:, 2:3], in_=mv[:, 1:2], func=AF.Sqrt, bias=epsc, scale=1.0),
                 reads=[tk_mv, tk_const], writes=[tk_mv])
            S.op("dve", lambda e: e.reciprocal(out=mv[:, 3:4], in_=mv[:, 2:3]), reads=[tk_mv], writes=[tk_mv])
            S.op("dve", lambda e: e.tensor_scalar(out=hj, in0=hj, scalar1=mv[:, 0:1], scalar2=mv[:, 3:4], op0=ALU.subtract, op1=ALU.mult),
                 reads=[tk_mv, tks[j]], writes=[tks[j]])
            S.op("dve", lambda e: e.tensor_tensor(out=hj, in0=hj, in1=gbt[:, 0:D], op=ALU.mult),
                 reads=[tk_gb, tks[j]], writes=[tks[j]])
            S.op("pool", lambda e: e.tensor_tensor(out=hj, in0=hj, in1=gbt[:, D:2 * D], op=ALU.add),
                 reads=[tk_gb, tks[j]], writes=[tks[j]])

        gb_cur = [None]

        def load_gb(row):
            if gb_cur[0] == row:
                return
            gb_cur[0] = row
            S.dma("pool", lambda e: e.dma_start(out=gbt, in_=lnrow_d[row:row + 1, :].partition_broadcast(128)),
                  "gb", writes=[tk_gb])

        def transpose_to(buf, tks, j, dst, dst_tks):
            for half in range(2):
                for k4 in range(4):
                    kc = half * 4 + k4
                    S.op("pe", lambda e, kc=kc, k4=k4: e.transpose(out=psb[PT_][:, 128 * k4:128 * k4 + 128], in_=buf[:, j, 128 * kc:128 * kc + 128], identity=identf),
                         reads=[tks[j], tk_const], writes=[tk_ps[PT_]])
                S.op("act", lambda e, half=half: e.activation(out=dst[:, 4 * half:4 * half + 4, 128 * j:128 * j + 128],
                                                              in_=psb[PT_][:, 0:512].rearrange("p (k t) -> p k t", k=4), func=AF.Copy),
                     reads=[tk_ps[PT_]], writes=[dst_tks[j]])

        NT_ALL = NT_PRE + NT_MAIN

        def pf_load(gt):
            if gt < NT_PRE:
                xd, row0 = xpre_d, gt * W
            else:
                xd, row0 = xmain_d, (gt - NT_PRE) * W
            for j in range(NCH):
                r0 = row0 + 128 * j
                S.dma("pool", lambda e, j=j, r0=r0: e.dma_start(out=hstage[:, j, :], in_=xd[r0:r0 + 128, :]), f"xin{j}", writes=[tk_hst[j]])

        def pf_ln(j):
            load_gb(0)
            layernorm_chunk(hstage, tk_hst, j)

        def pf_compute(gt):
            pf_load(gt)
            for j in range(NCH):
                pf_ln(j)

        def pf_transposes(gt):
            for j in range(NCH):
                transpose_to(hstage, tk_hst, j, hTs[gt % 2], tk_hTs[gt % 2])

        def conv_chunk(bank, halo, tk_h, hk, wcols, bcol, first, silu_out, silu_tks, conv_eng):
            ri = conv_chunk.rr
            conv_chunk.rr ^= 1
            rw = raw[ri]
            ac = cacc[ri]
            S.op("pool", lambda e: e.tensor_copy(out=rw[:, 16 - hk:16], in_=halo), reads=[tk_h], writes=[tk_raw[ri]])
            S.op("act", lambda e: e.activation(out=rw[:, 16:16 + W], in_=psb[bank][:, 0:W], func=AF.Copy), reads=[tk_ps[bank]], writes=[tk_raw[ri]])
            if first is not None and first is not False:
                mk = first
                S.op("pool", lambda e: e.tensor_tensor(out=rw[:, 16 - hk:16 + W], in0=rw[:, 16 - hk:16 + W], in1=mk[:, 16 - hk:16 + W], op=ALU.mult),
                     reads=[tk_raw[ri], tk_const], writes=[tk_raw[ri]])
            S.op("pool", lambda e: e.tensor_copy(out=halo, in_=rw[:, 16 + W - hk:16 + W]), reads=[tk_raw[ri]], writes=[tk_h])
            if first is not None and first is not False:
                S.op(conv_eng, lambda e: e.tensor_scalar(out=ac, in0=rw[:, 16:16 + W], scalar1=PPc(wcols[hk]), scalar2=PPc(bcol), op0=ALU.mult, op1=ALU.add),
                     reads=[tk_raw[ri], tk_const], writes=[tk_acc[ri]])
            else:
                S.op("act", lambda e: e.activation(out=ac, in_=psb[bank][:, 0:W], func=AF.Identity, scale=PPc(wcols[hk]), bias=PPc(bcol)),
                     reads=[tk_ps[bank], tk_const], writes=[tk_acc[ri]])
            for k in range(hk):
                sh = hk - k
                S.op(conv_eng, lambda e, k=k, sh=sh: e.scalar_tensor_tensor(out=ac, in0=rw[:, 16 - sh:16 - sh + W], scalar=PPc(wcols[k]), in1=ac, op0=ALU.mult, op1=ALU.add),
                     reads=[tk_raw[ri], tk_acc[ri], tk_const], writes=[tk_acc[ri]])
            def partB():
                S.op("act", lambda e: e.activation(out=silu_out, in_=ac, func=AF.Silu), reads=[tk_acc[ri]], writes=silu_tks)
            return partB
        conv_chunk.rr = 0

        def dt_phase(md, col0, masked):
            wb = w_get("DT")
            proj_fm(wb, 0, lambda kc: cur.v[:, kc, :], 8, cur.tk, PX_, ncols=96)
            w_rel(wb)
            P = slice(0, 96)
            e_, dt_, dta_, acum_, lnd_, q_, t0_, t1_ = [d[P, :] for d in dtc]
            S.op("act", lambda e: e.activation(out=e_, in_=psb[PX_][0:96, 0:W], func=AF.Exp, bias=ppt[0:96, 248:249], scale=1.0),
                 reads=[tk_ps[PX_], tk_const], writes=[tk_dtc[0]])
            S.op("act", lambda e: e.activation(out=dt_, in_=e_, func=AF.Ln, bias=1.0), reads=[tk_dtc[0]], writes=[tk_dtc[1]])
            if masked:
                S.dma("pool", lambda e: e.dma_start(out=dmask[0:96, :], in_=md[:, col0:col0 + W]), "msk", writes=[tk_dmask])
                S.op("dve", lambda e: e.tensor_scalar(out=t0_, in0=dmask[0:96, :], scalar1=-1.0, scalar2=1.0, op0=ALU.mult, op1=ALU.add),
                     reads=[tk_dmask], writes=[tk_dtc[6]])
                S.op("dve", lambda e: e.tensor_tensor(out=dt_, in0=dt_, in1=dmask[0:96, :], op=ALU.mult), reads=[tk_dmask, tk_dtc[1]], writes=[tk_dtc[1]])
                S.op("dve", lambda e: e.tensor_tensor(out=t1_, in0=dt_, in1=t0_, op=ALU.add), reads=[tk_dtc[1], tk_dtc[6]], writes=[tk_dtc[7]])
                S.op("act", lambda e: e.activation(out=lnd_, in_=t1_, func=AF.Ln), reads=[tk_dtc[7]], writes=[tk_dtc[4]])
                S.op("dve", lambda e: e.scalar_tensor_tensor(out=lnd_, in0=t0_, scalar=-200.0, in1=lnd_, op0=ALU.mult, op1=ALU.add),
                     reads=[tk_dtc[6], tk_dtc[4]], writes=[tk_dtc[4]])
            else:
                S.op("act", lambda e: e.activation(out=lnd_, in_=dt_, func=AF.Ln), reads=[tk_dtc[1]], writes=[tk_dtc[4]])
            S.op("dve", lambda e: e.tensor_scalar(out=dta_, in0=dt_, scalar1=aneg[0:96, 1:2], scalar2=None, op0=ALU.mult),
                 reads=[tk_dtc[1], tk_const], writes=[tk_dtc[2]])
            S.op("pool", lambda e: e.memset(t1_, 1.0), reads=[], writes=[tk_dtc[7]])
            for j in range(NCH):
                cs = slice(128 * j, 128 * j + 128)
                S.op("dve", lambda e, cs=cs: e.tensor_tensor_scan(out=acum_[:, cs], data0=t1_[:, cs], data1=dta_[:, cs], initial=0.0, op0=ALU.mult, op1=ALU.add),
                     reads=[tk_dtc[2], tk_dtc[7]], writes=[tk_dtc[3]])
            S.op("dve", lambda e: e.tensor_tensor(out=q_, in0=lnd_, in1=acum_, op=ALU.subtract), reads=[tk_dtc[4], tk_dtc[3]], writes=[tk_dtc[5]])

            def split3(src, tk_src, dst, tk_dst):
                hi = dtc[6]
                r1 = dtc[7]
                mid = dtc[0]
                S.op("dve", lambda e: e.tensor_copy(out=dst[0:32, :], in_=src[0:32, :]), reads=[tk_src], writes=[tk_dst])
                S.op("dve", lambda e: e.tensor_copy(out=mid[0:96, 0:W // 2].bitcast(BF16), in_=src[0:96, :]), reads=[tk_src], writes=[tk_dtc[0]])
                S.op("dve", lambda e: e.tensor_tensor(out=r1[0:96, :], in0=src[0:96, :], in1=mid[0:96, 0:W // 2].bitcast(BF16), op=ALU.subtract),
                     reads=[tk_src, tk_dtc[0]], writes=[tk_dtc[7]])
                S.op("dve", lambda e: e.tensor_copy(out=dst[32:64, :], in_=r1[32:64, :]), reads=[tk_dtc[7]], writes=[tk_dst])
                S.op("dve", lambda e: e.tensor_copy(out=hi[64:96, 0:W // 2].bitcast(BF16), in_=r1[64:96, :]), reads=[tk_dtc[7]], writes=[tk_dtc[6]])
                S.op("dve", lambda e: e.tensor_tensor(out=r1[64:96, :], in0=r1[64:96, :], in1=hi[64:96, 0:W // 2].bitcast(BF16), op=ALU.subtract),
                     reads=[tk_dtc[6], tk_dtc[7]], writes=[tk_dtc[7]])
                S.op("dve", lambda e: e.tensor_copy(out=dst[64:96, :], in_=r1[64:96, :]), reads=[tk_dtc[7]], writes=[tk_dst])
            split3(dtc[3], tk_dtc[3], S3, tk_S3)
            split3(dtc[5], tk_dtc[5], nS3, tk_nS3)
            for j in range(NCH):
                cs = slice(128 * j, 128 * j + 128)
                S.op("act", lambda e, cs=cs, j=j: e.activation(out=dtc[1][0:32, cs], in_=dtc[5][0:32, cs], func=AF.Exp,
                                                               bias=dtc[3][0:32, 128 * j + 127:128 * j + 128], scale=1.0),
                     reads=[tk_dtc[5], tk_dtc[3]], writes=[tk_dtc[1]])
                S.op("pe", lambda e, cs=cs, j=j: e.transpose(out=psb[PX_][:, 128 + 32 * j:160 + 32 * j], in_=dtc[1][0:32, cs], identity=identf[0:32, 0:32]),
                     reads=[tk_dtc[1], tk_const], writes=[tk_ps[PX_]])
            S.op("act", lambda e: e.activation(out=wtok.rearrange("p j h -> p (j h)"), in_=psb[PX_][:, 128:224], func=AF.Copy),
                 reads=[tk_ps[PX_]], writes=tk_wtok)

        def xbc_phase(first, with_c):
            names = ["BC0"] + (["BC1"] if with_c else []) + ["X0", "X1", "X2", "X3"]
            pend = [None]
            for name in names:
                wb = w_get(name)
                for c in range(4):
                    if name[0] == "B":
                        idx = 16 + 4 * int(name[2]) + c
                        out_ap, otk = BCT[:, idx - 16, :], [tk_BCT[idx - 16]]
                    else:
                        idx = 4 * int(name[1]) + c
                        out_ap, otk = xs[:, idx, :], [tk_xs[idx]]
                    bank = next_bank()
                    proj_fm(wb, 128 * c, lambda kc: cur.v[:, kc, :], 8, cur.tk, bank)
                    pb_ = conv_chunk(bank, halo_x[:, idx, :], tk_hx[idx], 3, [0 + idx, 24 + idx, 48 + idx, 72 + idx], 96 + idx,
                                     first, out_ap, otk, "dve")
                    if pend[0] is not None:
                        pend[0]()
                    pend[0] = pb_
                w_rel(wb)
            if pend[0] is not None:
                pend[0]()

        def state_prep_chunk(j):
            cs = slice(128 * j, 128 * j + 128)
            pTb = psb[PG_][:, 0:256].bitcast(BF16)
            for g in range(4):
                S.op("pe", lambda e, g=g: e.transpose(out=pTb[:, 128 * g:128 * g + 128], in_=BCT[:, g, cs], identity=identb),
                     reads=[tk_BCT[g], tk_const], writes=[tk_ps[PG_]])
            S.op("act", lambda e: e.activation(out=Btok, in_=pTb, func=AF.Copy), reads=[tk_ps[PG_]], writes=[tk_Btok])
            for q4 in range(4):
                tb = (PT_, PY_)[q4 % 2]
                for ii in range(4):
                    i = 4 * q4 + ii
                    S.op("pe", lambda e, i=i, ii=ii, tb=tb: e.transpose(out=psb[tb][:, 128 * ii:128 * ii + 128], in_=xs[:, i, cs], identity=identf),
                         reads=[tk_xs[i], tk_const], writes=[tk_ps[tb]])
                S.op("act", lambda e, q4=q4, tb=tb: e.activation(out=xtok[:, j, 512 * q4:512 * q4 + 512], in_=psb[tb][:, 0:512], func=AF.Copy),
                     reads=[tk_ps[tb]], writes=[tk_xtok[j]])
            S.op("dve", lambda e: e.tensor_tensor(out=xw.rearrange("p (h c) -> p h c", h=NH), in0=xtok[:, j, :].rearrange("p (h c) -> p h c", h=NH),
                                                  in1=wtok[:, j, :].unsqueeze(2).to_broadcast([128, NH, 64]), op=ALU.mult),
                 reads=[tk_xtok[j], tk_wtok[j]], writes=[tk_xw])

        def state_update_chunk(j):
            S.op("dve", lambda e: e.tensor_scalar(out=rdA[0:96, :], in0=SEL[0:96, :], scalar1=S3[0:96, 128 * j + 127:128 * j + 128], scalar2=None, op0=ALU.mult),
                 reads=[tk_S3, tk_const], writes=[tk_rdA])
            S.op("pe", lambda e: e.matmul(psb[PX_][:, 256:288], lhsT=onesb[0:96, :], rhs=rdA[0:96, :], start=True, stop=True),
                 reads=[tk_rdA, tk_const], writes=[tk_ps[PX_]])
            S.op("act", lambda e: e.activation(out=dAt, in_=psb[PX_][:, 256:288], func=AF.Exp), reads=[tk_ps[PX_]], writes=[tk_dA])
            for g in range(4):
                gs = slice(512 * g, 512 * g + 512)
                S.op("dve", lambda e, g=g, gs=gs: e.tensor_tensor(out=hS[:, gs].rearrange("p (h c) -> p h c", h=8), in0=hS[:, gs].rearrange("p (h c) -> p h c", h=8),
                                                                 in1=dAt[:, 8 * g:8 * g + 8].unsqueeze(2).to_broadcast([128, 8, 64]), op=ALU.mult),
                     reads=[tk_dA, tk_hS[g]], writes=[tk_hS[g]])
            for g in range(4):
                gs = slice(512 * g, 512 * g + 512)
                bk = (PO_, 0)[g % 2]
                S.op("pe", lambda e, g=g, gs=gs, bk=bk: e.matmul(psb[bk][:, 0:512], lhsT=Btok[:, 128 * g:128 * g + 128], rhs=xw[:, gs], start=True, stop=True),
                     reads=[tk_Btok, tk_xw], writes=[tk_ps[bk]])
                S.op("dve", lambda e, gs=gs, bk=bk: e.tensor_tensor(out=hS[:, gs], in0=hS[:, gs], in1=psb[bk][:, 0:512], op=ALU.add),
                     reads=[tk_ps[bk], tk_hS[g]], writes=[tk_hS[g]])
                S.op("act", lambda e, gs=gs: e.activation(out=hSb[:, gs], in_=hS[:, gs], func=AF.Copy), reads=[tk_hS[g]], writes=[tk_hSb[g]])

        esg2 = f32v(o_u, 512)
        tk_esg2 = Tk()
        Et2 = f32v(o_tmp, 512)
        tk_E2 = Tk()
        esgs, tk_esgs = [esg, esg2], [tk_esg, tk_esg2]
        Ets, tk_Es = [Et, Et2], [tk_E, tk_E2]

        def ssd_y_chunk(j):
            cs = slice(128 * j, 128 * j + 128)
            for g in range(4):
                S.op("pe", lambda e, g=g: e.matmul(psb[PG_][:, 128 * g:128 * g + 128], lhsT=BCT[:, g, cs], rhs=BCT[:, 4 + g, cs], start=True, stop=True),
                     reads=[tk_BCT[g], tk_BCT[4 + g]], writes=[tk_ps[PG_]])
            for hq in range(8):
                psk = (PS_, 0)[hq % 2]
                eb, tke = esgs[hq % 2], tk_esgs[hq % 2]
                for hh in range(4):
                    h = 4 * hq + hh
                    o = psb[psk][:, 128 * hh:128 * hh + 128]
                    S.op("pe", lambda e, o=o: e.matmul(o, lhsT=identb, rhs=maskneg, start=True, stop=False),
                         reads=[tk_const], writes=[tk_ps[psk]])
                    S.op("pe", lambda e, o=o, h=h: e.matmul(o, lhsT=SEL[0:96, h:h + 1].to_broadcast([96, 128]), rhs=S3[0:96, cs], start=False, stop=False),
                         reads=[tk_S3, tk_const], writes=[tk_ps[psk]])
                    S.op("pe", lambda e, o=o, h=h: e.matmul(o, lhsT=nS3[0:96, cs], rhs=SEL[0:96, h:h + 1].to_broadcast([96, 128]), start=False, stop=True),
                         reads=[tk_nS3, tk_const], writes=[tk_ps[psk]])
                S.op("act", lambda e, psk=psk, eb=eb: e.activation(out=eb, in_=psb[psk][:, 0:512], func=AF.Exp), reads=[tk_ps[psk]], writes=[tke])
                g = hq // 2
                S.op("dve", lambda e, hq=hq, g=g, eb=eb: e.tensor_tensor(out=Mt[:, 4 * hq:4 * hq + 4, :], in0=eb.rearrange("p (h t) -> p h t", h=4),
                                                                        in1=psb[PG_][:, 128 * g:128 * g + 128].unsqueeze(1).to_broadcast([128, 4, 128]), op=ALU.mult),
                     reads=[tke, tk_ps[PG_]], writes=[tk_Mt[hq]])
            for pq in range(4):
                g = pq
                pxk = (PX_, 1)[pq % 2]
                Eb, tkE = Ets[pq % 2], tk_Es[pq % 2]
                for ii in range(4):
                    i = 4 * pq + ii
                    for hh in range(2):
                        h = 2 * i + hh
                        S.op("pe", lambda e, ii=ii, hh=hh, h=h, pxk=pxk: e.matmul(psb[pxk][64 * hh:64 * hh + 64, 128 * ii:128 * ii + 128], lhsT=SEL[0:96, h:h + 1].to_broadcast([96, 64]),
                                                                                  rhs=S3[0:96, cs], start=True, stop=True, tile_position=(0, 64 * hh)),
                             reads=[tk_S3, tk_const], writes=[tk_ps[pxk]])
                S.op("act", lambda e, pxk=pxk, Eb=Eb: e.activation(out=Eb, in_=psb[pxk][:, 0:512], func=AF.Exp), reads=[tk_ps[pxk]], writes=[tkE])
                for ii in range(4):
                    i = 4 * pq + ii
                    S.op("pe", lambda e, ii=ii, i=i, g=g: e.matmul(psb[PO_][:, 128 * ii:128 * ii + 128], lhsT=hSb[:, 128 * i:128 * i + 128], rhs=BCT[:, 4 + g, cs], start=True, stop=True),
                         reads=[tk_hSb[g], tk_BCT[4 + g]], writes=[tk_ps[PO_]])
                for ii in range(4):
                    i = 4 * pq + ii
                    for hh in range(2):
                        h = 2 * i + hh
                        S.op("pe", lambda e, ii=ii, hh=hh, h=h: e.matmul(psb[PY_][64 * hh:64 * hh + 64, 128 * ii:128 * ii + 128], lhsT=xtok[:, j, 64 * h:64 * h + 64],
                                                                         rhs=Mt[:, h, :], start=True, stop=True, tile_position=(0, 64 * hh)),
                             reads=[tk_xtok[j], tk_Mt[h // 4]], writes=[tk_ps[PY_]])
                S.op("dve", lambda e, Eb=Eb: e.tensor_tensor(out=t1b, in0=Eb, in1=psb[PO_][:, 0:512], op=ALU.mult), reads=[tkE, tk_ps[PO_]], writes=[tk_t1])
                for ii in range(4):
                    i = 4 * pq + ii
                    S.op("dve", lambda e, ii=ii, i=i: e.scalar_tensor_tensor(out=xs[:, i, cs], in0=xs[:, i, cs], scalar=PPc(120 + i), in1=t1b[:, 128 * ii:128 * ii + 128],
                                                                             op0=ALU.mult, op1=ALU.add),
                         reads=[tk_t1, tk_xs[i], tk_const], writes=[tk_xs[i]])
                S.op("dve", lambda e, pq=pq: e.tensor_tensor(out=xs[:, 4 * pq:4 * pq + 4, cs], in0=xs[:, 4 * pq:4 * pq + 4, cs],
                                                             in1=psb[PY_][:, 0:512].rearrange("p (i t) -> p i t", i=4), op=ALU.add),
                     reads=[tk_ps[PY_]] + tk_xs[4 * pq:4 * pq + 4], writes=tk_xs[4 * pq:4 * pq + 4])

        tk_dbg = []

        def dump(dst, src_ap, toks):
            t = Tk()
            tk_dbg.append(t)
            S.dma("pool", lambda e: e.dma_start(out=dst[:, :], in_=src_ap), "dbg", reads=toks, writes=[t])

        def tile_pre(ti):
            gt = ti
            cur.v, cur.tk = hTs[gt % 2], tk_hTs[gt % 2]
            dt_phase(mpre_d, ti * W, True)
            xbc_phase(tokmp if ti == 0 else None, False)
            if gt + 1 < NT_ALL:
                pf_load(gt + 1)
            for j in range(NCH):
                state_prep_chunk(j)
                state_update_chunk(j)
                if gt + 1 < NT_ALL:
                    pf_ln(j)
            if gt + 1 < NT_ALL:
                pf_transposes(gt + 1)

        def tile_main(ti):
            first = (ti == 0)
            gt = NT_PRE + ti
            cur.v, cur.tk = hTs[gt % 2], tk_hTs[gt % 2]
            for j in range(NCH):
                S.op("act", lambda e, j=j: e.activation(out=hres[:, j, :], in_=hstage[:, j, :], func=AF.Copy), reads=[tk_hst[j]], writes=[tk_hres[j]])
            dbgt = debug and ti == NT_MAIN - 1
            if dbgt:
                dump(dbg_h0, hres.rearrange('p j d -> p (j d)'), tk_hres)
            alias_tokens(tk_pT + tk_ypT, tk_ynT + tk_hst)
            for ub_i in range(2):
                wb = w_get(f"U{ub_i}")
                for c in range(4):
                    uc = 4 * ub_i + c
                    wwin = POOL_WINDOWS[uc // 2]
                    bank = next_bank()
                    proj_fm(wb, 128 * c, lambda kc: cur.v[:, kc, :], 8, cur.tk, bank)
                    u0i = 0 if uc % 2 == 0 else 3
                    u0 = ub[u0i]
                    S.op("pool", lambda e, u0=u0, uc=uc: e.tensor_copy(out=u0[:, 1:16], in_=halo_u[:, uc, :]), reads=[tk_hu[uc]], writes=[tk_ub[u0i]])
                    S.op("act", lambda e, u0=u0, bank=bank: e.activation(out=u0[:, 16:16 + W], in_=psb[bank][:, 0:W], func=AF.Copy), reads=[tk_ps[bank]], writes=[tk_ub[u0i]])
                    if first:
                        S.op("pool", lambda e, u0=u0: e.tensor_tensor(out=u0[:, 1:16 + W], in0=u0[:, 1:16 + W], in1=tokm0[:, 1:16 + W], op=ALU.mult),
                             reads=[tk_ub[u0i], tk_const], writes=[tk_ub[u0i]])
                    S.op("pool", lambda e, u0=u0, uc=uc: e.tensor_copy(out=halo_u[:, uc, :], in_=u0[:, 16 + W - 15:16 + W]), reads=[tk_ub[u0i]], writes=[tk_hu[uc]])
                    src, si = u0, u0i
                    k = 1
                    lo = 1
                    while k < wwin:
                        lo += k
                        di = 1 if si != 1 else 2
                        dst = ub[di]
                        S.op("dve", lambda e, src=src, dst=dst, lo=lo, k=k: e.tensor_tensor(out=dst[:, lo:16 + W], in0=src[:, lo:16 + W], in1=src[:, lo - k:16 + W - k], op=ALU.add),
                             reads=[tk_ub[si]], writes=[tk_ub[di]])
                        src, si = dst, di
                        k *= 2
                    if first:
                        S.op("dve", lambda e, src=src, uc=uc: e.tensor_tensor(out=src[:, 16:16 + W], in0=src[:, 16:16 + W], in1=invc0[:, uc // 2, :], op=ALU.mult),
                             reads=[tk_ub[si], tk_const, tk_rstd[uc // 2]], writes=[tk_ub[si]])
                        S.op("dve", lambda e, u0=u0, src=src, uc=uc: e.tensor_tensor(out=pT[:, uc, :], in0=src[:, 16:16 + W], in1=u0[:, 16:16 + W], op=ALU.subtract),
                             reads=[tk_ub[si], tk_ub[u0i]], writes=[tk_pT[uc]])
                    else:
                        S.op("dve", lambda e, u0=u0, src=src, uc=uc, wwin=wwin: e.scalar_tensor_tensor(out=pT[:, uc, :], in0=src[:, 16:16 + W], scalar=1.0 / wwin, in1=u0[:, 16:16 + W],
                                                                                                 op0=ALU.mult, op1=ALU.subtract),
                             reads=[tk_ub[si], tk_ub[u0i]], writes=[tk_pT[uc]])
                w_rel(wb)
            wb = w_get("PW")
            for oc in range(8):
                g, oh = oc // 2, oc % 2
                bank = next_bank()
                for kc in range(2):
                    mm(psb[bank][:, 0:W], wb.v[:, 2 * g + kc, 128 * oh:128 * oh + 128], pT[:, 2 * g + kc, :], kc == 0, kc == 1,
                       [wb.tk, tk_pT[2 * g + kc]], bank)
                S.op("act", lambda e, bank=bank, oc=oc: e.activation(out=ypT[:, oc, :], in_=psb[bank][:, 0:W], func=AF.Identity, scale=PPc(152 + oc)),
                     reads=[tk_ps[bank], tk_const], writes=[tk_ypT[oc]])
            w_rel(wb)
            for half in range(2):
                wpp = w_get(f"PP{half}")
                wgp = w_get(f"GP{half}")
                for c in range(4):
                    dc = 4 * half + c
                    b1 = next_bank()
                    proj_fm(wgp, 128 * c, lambda kc: cur.v[:, kc, :], 8, cur.tk, b1)
                    b0 = next_bank()
                    proj_fm(wpp, 128 * c, lambda kc: ypT[:, kc, :], 8, tk_ypT, b0)
                    ta = dc % 3
                    S.op("act", lambda e, b1=b1, ta=ta: e.activation(out=tmpA[ta], in_=psb[b1][:, 0:W], func=AF.Sigmoid), reads=[tk_ps[b1]], writes=[tk_tmpA[ta]])
                    S.op("dve", lambda e, b0=b0, ta=ta, dc=dc: e.tensor_tensor(out=poolpart[:, dc, :], in0=tmpA[ta], in1=psb[b0][:, 0:W], op=ALU.mult),
                         reads=[tk_ps[b0], tk_tmpA[ta]], writes=[tk_pp[dc]])
                w_rel(wpp)
                w_rel(wgp)
            if dbgt:
                dump(dbg_yp, ypT.rearrange('p i t -> p (i t)'), tk_ypT)
                dump(dbg_pp, poolpart.rearrange('p i t -> p (i t)'), tk_pp)
            alias_tokens(tk_xs, tk_act)
            alias_tokens(tk_xtok, tk_mrg)
            alias_tokens([tk_esg2], tk_ub[0:2])
            alias_tokens([tk_E2], tk_tmpA[0:2])
            dt_phase(mmain_d, ti * W, first)
            xbc_phase(tokm0 if first else None, True)
            if dbgt:
                dump(dbg_xs, xs.rearrange('p i t -> p (i t)'), tk_xs)
                dump(dbg_bc, BCT.rearrange('p i t -> p (i t)'), tk_BCT)
            for j in range(NCH):
                state_prep_chunk(j)
                ssd_y_chunk(j)
                state_update_chunk(j)
            alias_tokens(tk_ub[0:2], [tk_esg2])
            alias_tokens(tk_tmpA[0:2], [tk_E2])
            alias_tokens(tk_ynT, tk_pT + tk_ypT + tk_hst)
            def ones_mm(i):
                sb_, c, zb = i % 2, i % 4, i // 4
                S.op("pe", lambda e: e.matmul(psb[PX_][:, 0:W], lhsT=onesb, rhs=sqb[sb_], start=(c == 0), stop=(c == 3)),
                     reads=[tk_sq[sb_], tk_const], writes=[tk_ps[PX_]])
                if c == 3:
                    S.op("act", lambda e: e.activation(out=rstd[:, zb, :], in_=psb[PX_][:, 0:W], func=AF.Sqrt, bias=rmsepsc, scale=1.0 / 512.0),
                         reads=[tk_ps[PX_], tk_const], writes=[tk_rstd[zb]])
                    S.op("dve", lambda e: e.reciprocal(out=rstd[:, zb, :], in_=rstd[:, zb, :]), reads=[tk_rstd[zb]], writes=[tk_rstd[zb]])
            prev = None
            for zb in range(4):
                wb = w_get(f"Z{zb}")
                for c in range(4):
                    i = 4 * zb + c
                    bank = next_bank()
                    proj_fm(wb, 128 * c, lambda kc: cur.v[:, kc, :], 8, cur.tk, bank)
                    ta = i % 3
                    S.op("act", lambda e, bank=bank, ta=ta: e.activation(out=tmpA[ta], in_=psb[bank][:, 0:W], func=AF.Silu), reads=[tk_ps[bank]], writes=[tk_tmpA[ta]])
                    S.op("dve", lambda e, i=i, ta=ta: e.tensor_tensor(out=xs[:, i, :], in0=xs[:, i, :], in1=tmpA[ta], op=ALU.mult),
                         reads=[tk_tmpA[ta], tk_xs[i]], writes=[tk_xs[i]])
                    sb_ = i % 2
                    S.op("pool", lambda e, i=i, sb_=sb_: e.tensor_tensor(out=sqb[sb_], in0=xs[:, i, :], in1=xs[:, i, :], op=ALU.mult), reads=[tk_xs[i]], writes=[tk_sq[sb_]])
                    if prev is not None:
                        ones_mm(prev)
                    prev = i
                w_rel(wb)
            ones_mm(prev)
            for i in range(16):
                eng = "dve"
                S.op(eng, lambda e, i=i: e.scalar_tensor_tensor(out=ynT[:, i, :], in0=xs[:, i, :], scalar=PPc(136 + i), in1=rstd[:, i // 4, :], op0=ALU.mult, op1=ALU.mult),
                     reads=[tk_xs[i], tk_rstd[i // 4], tk_const], writes=[tk_ynT[i]])
            if dbgt:
                dump(dbg_yn, ynT.rearrange('p i t -> p (i t)'), tk_ynT)
            alias_tokens(tk_mrg, tk_xtok)
            wgs = None
            for q in range(4):
                wps = w_get(f"PS{q}")
                if q % 2 == 0:
                    wgs = w_get(f"GS{q // 2}")
                for c in range(2):
                    dc = 2 * q + c
                    b1 = next_bank()
                    proj_fm(wgs, 128 * (dc % 4), lambda kc: cur.v[:, kc, :], 8, cur.tk, b1)
                    b0 = next_bank()
                    proj_fm(wps, 128 * c, lambda kc: ynT[:, kc, :], 16, tk_ynT, b0)
                    ta = dc % 3
                    S.op("act", lambda e, b1=b1, ta=ta: e.activation(out=tmpA[ta], in_=psb[b1][:, 0:W], func=AF.Sigmoid), reads=[tk_ps[b1]], writes=[tk_tmpA[ta]])
                    S.op("dve", lambda e, b0=b0, ta=ta: e.tensor_tensor(out=tmpA[ta], in0=tmpA[ta], in1=psb[b0][:, 0:W], op=ALU.mult),
                         reads=[tk_ps[b0], tk_tmpA[ta]], writes=[tk_tmpA[ta]])
                    S.op("pool", lambda e, ta=ta, dc=dc: e.tensor_tensor(out=mergedT[:, dc, :], in0=tmpA[ta], in1=poolpart[:, dc, :], op=ALU.add),
                         reads=[tk_tmpA[ta], tk_pp[dc]], writes=[tk_mrg[dc]])
                w_rel(wps)
                if q % 2 == 1:
                    w_rel(wgs)
            if dbgt:
                dump(dbg_mg, mergedT.rearrange('p i t -> p (i t)'), tk_mrg)
            wo = [w_get("WO0"), w_get("WO1")]
            load_gb(1)
            for j in range(NCH):
                cs = slice(128 * j, 128 * j + 128)
                for half in range(2):
                    bank = next_bank()
                    for kc in range(8):
                        mm(psb[bank][:, 0:512], mergedT[:, kc, cs], wo[half].v[:, kc, :], kc == 0, kc == 7, [wo[half].tk, tk_mrg[kc]], bank)
                    S.op("dve", lambda e, bank=bank, half=half, j=j: e.scalar_tensor_tensor(out=hres[:, j, 512 * half:512 * half + 512], in0=hres[:, j, 512 * half:512 * half + 512],
                                                                                         scalar=ALPHA, in1=psb[bank][:, 0:512], op0=ALU.mult, op1=ALU.add),
                         reads=[tk_ps[bank], tk_hres[j]], writes=[tk_hres[j]])
                layernorm_chunk(hres, tk_hres, j)
                transpose_to(hres, tk_hres, j, cur.v, cur.tk)
            w_rel(wo[0])
            w_rel(wo[1])
            if dbgt:
                dump(dbg_h1, hres.rearrange('p j d -> p (j d)'), tk_hres)
            if gt + 1 < NT_ALL:
                alias_tokens(tk_hst, tk_ynT + tk_pT + tk_ypT)
                pf_load(gt + 1)
            alias_tokens(tk_act, tk_xs)
            pend = [None]
            for fb in range(11):
                wb = w_get(f"F{fb}")
                for c in range(2):
                    fi = 2 * fb + c
                    b0 = (0, 2, 4)[fi % 3]
                    b1 = (1, 3, 5)[fi % 3]
                    proj_fm(wb, 128 * c, lambda kc: cur.v[:, kc, :], 8, cur.tk, b0)
                    proj_fm(wb, 256 + 128 * c, lambda kc: cur.v[:, kc, :], 8, cur.tk, b1)
                    ta = fi % 3
                    pb_ = conv_chunk(b0, halo_f[:, fi, :], tk_hf[fi], 2, [160 + fi, 182 + fi, 204 + fi], 226 + fi, tokm0 if first else None, tmpA[ta], [tk_tmpA[ta]],
                                     "dve")

                    def partB2(pb_=pb_, b1=b1, ta=ta, fi=fi):
                        pb_()
                        S.op("dve", lambda e: e.tensor_tensor(out=actT[:, fi, :], in0=tmpA[ta], in1=psb[b1][:, 0:W], op=ALU.mult),
                             reads=[tk_ps[b1], tk_tmpA[ta]], writes=[tk_act[fi]])
                    if pend[0] is not None:
                        pend[0]()
                    pend[0] = partB2
                w_rel(wb)
            pend[0]()
            if dbgt:
                dump(dbg_act, actT.rearrange('p i t -> p (i t)'), tk_act)
            for q in range(4):
                wb = w_get(f"FD{q}")
                for j in range(NCH):
                    cs = slice(128 * j, 128 * j + 128)
                    bank = next_bank()
                    for kc in range(22):
                        mm(psb[bank][:, 0:256], actT[:, kc, cs], wb.v[:, kc, :], kc == 0, kc == 21, [wb.tk, tk_act[kc]], bank)
                    S.op("dve", lambda e, bank=bank, q=q, j=j: e.scalar_tensor_tensor(out=hres[:, j, 256 * q:256 * q + 256], in0=hres[:, j, 256 * q:256 * q + 256],
                                                                                   scalar=ALPHA, in1=psb[bank][:, 0:256], op0=ALU.mult, op1=ALU.add),
                         reads=[tk_ps[bank], tk_hres[j]], writes=[tk_hres[j]])
                w_rel(wb)
                if gt + 1 < NT_ALL and q < NCH:
                    pf_ln(q)
            load_gb(2)
            if gt + 1 < NT_ALL:
                pf_transposes(gt + 1)
            for j in range(NCH):
                layernorm_chunk(hres, tk_hres, j)
                ch = ti * NCH + j
                if ch >= 1:
                    r0 = (ch - 1) * 128
                    ob = j
                    S.dma("pool", lambda e, j=j, r0=r0: e.dma_start(out=y_d[r0:r0 + 128, :], in_=hres[:, j, :]), f"out{ob}", reads=[tk_hres[j]])

        print('arena words used', off[0], flush=True)
        pf_compute(0)
        pf_transposes(0)
        for ti in range(NT_PRE):
            tile_pre(ti)
        for ti in range(NT_MAIN):
            tile_main(ti)
        S.final_wait("pool", tk_hres + tk_dbg)
        S.emit(block)
    return nc


def make_pp(inp):
    pp = np.zeros((128, 256), np.float32)
    cw = np.asarray(inp["ssm_conv_w"])[0]
    cb = np.asarray(inp["ssm_conv_b"])[0]
    for k in range(4):
        pp[:, 24 * k:24 * k + 24] = cw[k].reshape(24, 128).T
    pp[:, 96:120] = cb.reshape(24, 128).T
    dsk = np.repeat(np.asarray(inp["ssm_d"])[0], 64)
    pp[:, 120:136] = dsk.reshape(16, 128).T
    pp[:, 136:152] = np.asarray(inp["ssm_norm_w"])[0].reshape(16, 128).T
    pp[:, 152:160] = np.asarray(inp["pool_scale"])[0].reshape(8, 128).T
    fw_ = np.asarray(inp["ffn_conv_w"])[0]
    for k in range(3):
        pp[:, 160 + 22 * k:160 + 22 * k + 22] = fw_[k].reshape(22, 128).T
    pp[:, 226:248] = np.asarray(inp["ffn_conv_b"])[0].reshape(22, 128).T
    pp[0:96, 248] = np.tile(np.asarray(inp["ssm_dt_bias"])[0], 3)
    pp[0:96, 249] = np.tile(np.asarray(inp["ssm_a_log"])[0], 3)
    return pp


def core_inputs(inp, b, hf, NT_PRE, NT_MAIN, shared):
    TP, TM = NT_PRE * W, NT_MAIN * W
    Q1 = TP - 128
    x = np.asarray(inp["x"])[b]
    meta = np.asarray(inp["meta_tokens"])
    seq_len = x.shape[0]

    def rows(q0, n):
        out = np.zeros((n, D), np.float32)
        msk = np.zeros((n,), np.float32)
        for lo, hi, src, s0 in ((112, 128, meta, 0), (128, 128 + seq_len, x, 0)):
            a = max(q0, lo)
            bnd = min(q0 + n, hi)
            if a < bnd:
                out[a - q0:bnd - q0] = src[a - lo:bnd - lo]
                msk[a - q0:bnd - q0] = 1.0
        return out, msk
    if hf == 0:
        xpre = np.zeros((TP, D), np.float32)
        mpre = np.zeros((TP,), np.float32)
        xmain, mmain = rows(0, TM)
    else:
        xpre, mpre = rows(Q1 - TP, TP)
        xmain, mmain = rows(Q1, TM)
    tokmask0 = np.ones((128, 16 + W), np.float32)
    tokmask0[:, 0:16] = float(hf)
    tokmask0[:, 16:] = mmain[None, 0:W]
    tokmaskp = np.zeros((128, 16 + W), np.float32)
    tokmaskp[:, 16:] = mpre[None, 0:W]
    invcnt0 = np.zeros((128, 4, W), np.float32)
    q0 = 0 if hf == 0 else Q1
    lpos = np.arange(q0, q0 + W) - 112
    for g, wdw in enumerate(POOL_WINDOWS):
        cnt = np.minimum(np.maximum(lpos + 1, 1), wdw).astype(np.float32)
        invcnt0[:, g, :] = (1.0 / cnt)[None, :]
    d = dict(shared)
    d.update({
        "xpre": xpre, "xmain": xmain,
        "mpre": np.ascontiguousarray(np.broadcast_to(mpre[None, :], (96, TP))),
        "mmain": np.ascontiguousarray(np.broadcast_to(mmain[None, :], (96, TM))),
        "tokmask0": tokmask0, "invcnt0": invcnt0.reshape(128, 4 * W), "tokmaskp": tokmaskp,
    })
    return d


def shared_inputs(inp):
    lnrows = np.stack([
        np.concatenate([np.asarray(inp["ln_in_g"]), np.asarray(inp["ln_in_b"])]),
        np.concatenate([np.asarray(inp["ln1_g"])[0], np.asarray(inp["ln1_b"])[0]]),
        np.concatenate([np.asarray(inp["ln2_g"])[0], np.asarray(inp["ln2_b"])[0]]),
    ]).astype(np.float32)
    return {
        "pp": make_pp(inp), "lnrows": lnrows,
        "w_in": np.ascontiguousarray(np.asarray(inp["w_in"])[0]),
        "w_proj_ssm": np.ascontiguousarray(np.asarray(inp["w_proj_ssm"])[0]),
        "w_proj_pool": np.ascontiguousarray(np.asarray(inp["w_proj_pool"])[0]),
        "pool_w": np.ascontiguousarray(np.asarray(inp["pool_w"])[0]),
        "w_out": np.ascontiguousarray(np.asarray(inp["w_out"])[0]),
        "ffn_w_up": np.ascontiguousarray(np.asarray(inp["ffn_w_up"])[0]),
        "ffn_w_down": np.ascontiguousarray(np.asarray(inp["ffn_w_down"])[0]),
    }


_NC_CACHE = {}


def kernel(**inputs):
    NT_PRE, NT_MAIN = 11, 11
    key = (NT_PRE, NT_MAIN)
    if key not in _NC_CACHE:
        _NC_CACHE[key] = build_program(NT_PRE, NT_MAIN)
    nc = _NC_CACHE[key]
    shared = shared_inputs(inputs)
    B = np.asarray(inputs["x"]).shape[0]
    in_maps = []
    for core in range(8):
        b, hf = core // 2, core % 2
        in_maps.append(core_inputs(inputs, b, hf, NT_PRE, NT_MAIN, shared))
    res = run_bass_kernel_spmd(nc, in_maps, core_ids=list(range(8)))
    out = np.zeros((B, 8192, D), np.float32)
    for core in range(8):
        b, hf = core // 2, core % 2
        out[b, 4096 * hf:4096 * hf + 4096] = res.results[core]["y"]
    return out
```
